# Optimizing a Trainium2 kernel written in Bass

```python
import jax, jax.numpy as jnp
from jax import lax
import numpy as np

D_MODEL = 1024
BATCH = 1
SEQ = 16384
DEPTH = 2
DEC_BATCH = 16
DEC_SEQ = 2048
PAST_LEN = 128

HEAD_DIM = 64
ROPE_THETA = 500000.0
NORM_EPS = 1e-6
N_EVEN = (DEPTH + 1) // 2
N_ODD = DEPTH // 2
MLA_HEADS = 8
MLA_NOPE = 64
MLA_ROPE = 32
MLA_V = 64
MLA_Q_RANK = 256
MLA_KV_RANK = 128
Q_BLOCK = 128
DIL_PATTERNS = ((128, 1), (512, 4), (2048, 16))
DIL_GROUPS = len(DIL_PATTERNS)
DIL_HEADS = 8
DIL_ROT = HEAD_DIM // 4
MLA_IN = MLA_Q_RANK + MLA_KV_RANK + MLA_ROPE
DIL_QKV = 3 * DIL_GROUPS * DIL_HEADS * HEAD_DIM
MIX_IN = MLA_IN + DIL_QKV
MIX_OUT = MLA_HEADS * MLA_V + DIL_HEADS * HEAD_DIM
D_RNN = 1536
LRU_BLOCKS = 12
LRU_BW = D_RNN // LRU_BLOCKS
CONV_W = 4
CONV_LEFT = 2
LRU_C = 8.0
D_FF = ((8 * D_MODEL // 3 + 255) // 256) * 256
NEG_BIG = -1e30

kernel_name = "mla_dilated_rglru_encoder"


def rmsnorm(x, g):
    xf = x.astype(jnp.float32)
    y = xf * lax.rsqrt(jnp.mean(xf * xf, axis=-1, keepdims=True) + NORM_EPS)
    return (y * g.astype(jnp.float32)).astype(x.dtype)


def rope(x, rot_dim):
    S = x.shape[1]
    half = rot_dim // 2
    inv = jnp.power(ROPE_THETA, -jnp.arange(half, dtype=jnp.float32) / half)
    ang = jnp.arange(S, dtype=jnp.float32)[:, None] * inv[None, :]
    shape = (1, S) + (1,) * (x.ndim - 3) + (half,)
    cos = jnp.cos(ang).reshape(shape)
    sin = jnp.sin(ang).reshape(shape)
    xr = x[..., :rot_dim].astype(jnp.float32)
    x1, x2 = xr[..., :half], xr[..., half:]
    rot = jnp.concatenate([x1 * cos - x2 * sin, x2 * cos + x1 * sin], axis=-1).astype(x.dtype)
    return jnp.concatenate([rot, x[..., rot_dim:]], axis=-1)


def mla_attention(q, k, v):
    B, S, H, Dq = q.shape
    nb = S // Q_BLOCK
    scale = Dq ** -0.5
    qb = q.reshape(B, nb, Q_BLOCK, H, Dq).transpose(1, 0, 2, 3, 4)

    def block(qblk):
        s = jnp.einsum('bqhd,bkhd->bhqk', qblk, k, preferred_element_type=jnp.float32) * scale
        p = jax.nn.softmax(s, axis=-1)
        return jnp.einsum('bhqk,bkhd->bqhd', p.astype(v.dtype), v)

    o = lax.map(block, qb)
    return o.transpose(1, 0, 2, 3, 4).reshape(B, S, H, v.shape[-1])


def dilated_group(q, k, v, dilation, steps):
    B, S, H, Dh = q.shape
    unit = dilation * steps
    Sp = -(-S // unit) * unit
    pad = Sp - S
    L = Sp // dilation
    nb = L // steps

    def split(t):
        t = jnp.pad(t, ((0, 0), (0, pad), (0, 0), (0, 0)))
        t = t.reshape(B, L, dilation, H, Dh).transpose(0, 2, 1, 3, 4)
        return t.reshape(B, dilation, nb, steps, H, Dh)

    def neighbours(t):
        z = jnp.zeros_like(t[:, :, :1])
        prev = jnp.concatenate([z, t[:, :, :-1]], axis=2)
        nxt = jnp.concatenate([t[:, :, 1:], z], axis=2)
        return jnp.concatenate([prev, t, nxt], axis=3)

    qs = split(q)
    kn = neighbours(split(k))
    vn = neighbours(split(v))
    valid = (jnp.arange(Sp) < S).reshape(L, dilation).T.reshape(dilation, nb, steps)
    vmask = neighbours(valid[None])[0]
    rel = jnp.arange(3 * steps)[None, :] - steps - jnp.arange(steps)[:, None]
    band = jnp.abs(rel) <= steps
    mask = band[None, None] & vmask[:, :, None, :]

    s = jnp.einsum('bdgqhe,bdgkhe->bdghqk', qs, kn, preferred_element_type=jnp.float32) * (Dh ** -0.5)
    s = jnp.where(mask[None, :, :, None], s, NEG_BIG)
    m = jnp.max(s, axis=-1, keepdims=True)
    e = jnp.exp(s - m)
    den = jnp.sum(e, axis=-1, keepdims=True)
    o = jnp.einsum('bdghqk,bdgkhe->bdgqhe', e / den, vn.astype(jnp.float32))
    lse = (m + jnp.log(den))[..., 0]

    o = o.reshape(B, dilation, L, H, Dh).transpose(0, 2, 1, 3, 4).reshape(B, Sp, H, Dh)[:, :S]
    lse = lse.transpose(0, 1, 2, 4, 3).reshape(B, dilation, L, H).transpose(0, 2, 1, 3)
    lse = lse.reshape(B, Sp, H)[:, :S]
    return o, lse


def attention_mixer(h, w_in, q_norm, w_uq, kv_norm, w_ukv, w_out):
    B, S, _ = h.shape
    z = h @ w_in
    c_q = z[..., :MLA_Q_RANK]
    c_kv = z[..., MLA_Q_RANK:MLA_Q_RANK + MLA_KV_RANK]
    k_r = z[..., MLA_Q_RANK + MLA_KV_RANK:MLA_IN]
    qkv = z[..., MLA_IN:]

    q = (rmsnorm(c_q, q_norm) @ w_uq).reshape(B, S, MLA_HEADS, MLA_NOPE + MLA_ROPE)
    q = jnp.concatenate([q[..., :MLA_NOPE], rope(q[..., MLA_NOPE:], MLA_ROPE)], axis=-1)
    kv = (rmsnorm(c_kv, kv_norm) @ w_ukv).reshape(B, S, MLA_HEADS, MLA_NOPE + MLA_V)
    k_nope, v = kv[..., :MLA_NOPE], kv[..., MLA_NOPE:]
    k_r = rope(k_r, MLA_ROPE)
    k = jnp.concatenate([k_nope, jnp.broadcast_to(k_r[:, :, None], (B, S, MLA_HEADS, MLA_ROPE))], axis=-1)
    o_mla = mla_attention(q, k, v).reshape(B, S, MLA_HEADS * MLA_V)

    qkv = qkv.reshape(B, S, 3, DIL_GROUPS, DIL_HEADS, HEAD_DIM)
    dq = rope(qkv[:, :, 0], DIL_ROT)
    dk = rope(qkv[:, :, 1], DIL_ROT)
    dv = qkv[:, :, 2]
    outs, lses = [], []
    for g, (window, dilation) in enumerate(DIL_PATTERNS):
        o_g, l_g = dilated_group(dq[:, :, g], dk[:, :, g], dv[:, :, g], dilation, window // (2 * dilation))
        outs.append(o_g)
        lses.append(l_g)
    wts = jax.nn.softmax(jnp.stack(lses, axis=0), axis=0)[..., None]
    o_dil = jnp.sum(wts * jnp.stack(outs, axis=0), axis=0).reshape(B, S, DIL_HEADS * HEAD_DIM)

    o = jnp.concatenate([o_mla, o_dil.astype(h.dtype)], axis=-1)
    return (o @ w_out).astype(h.dtype)


def rglru_scan(x, w_gate, b_gate, lam, reverse):
    B, S, _ = x.shape
    xb = x.reshape(B, S, LRU_BLOCKS, LRU_BW)
    g = jnp.einsum('bsnc,kncd->kbsnd', xb, w_gate).reshape(2, B, S, D_RNN) + b_gate[:, None, None, :]
    r = jax.nn.sigmoid(g[0].astype(jnp.float32))
    i = jax.nn.sigmoid(g[1].astype(jnp.float32))
    log_a = -LRU_C * r * jax.nn.softplus(-lam.astype(jnp.float32))
    a = jnp.exp(log_a)
    u = jnp.sqrt(-jnp.expm1(2.0 * log_a)) * (i * x.astype(jnp.float32))

    def comb(p, q):
        a1, b1 = p
        a2, b2 = q
        return a1 * a2, a2 * b1 + b2

    _, hs = lax.associative_scan(comb, (a, u), reverse=reverse, axis=1)
    return hs


def recurrent_mixer(h, w_in, conv_w, conv_b, w_gate, b_gate, lam, w_out):
    z = h @ w_in
    y = jax.nn.gelu(z[..., :D_RNN])
    xr = z[..., D_RNN:]
    xc = lax.conv_general_dilated(
        xr, conv_w[:, None, :], window_strides=(1,),
        padding=[(CONV_LEFT, CONV_W - 1 - CONV_LEFT)],
        dimension_numbers=('NWC', 'WIO', 'NWC'), feature_group_count=D_RNN) + conv_b
    hf = rglru_scan(xc, w_gate[0], b_gate[0], lam[0], reverse=False)
    hb = rglru_scan(xc, w_gate[1], b_gate[1], lam[1], reverse=True)
    o = ((hf + hb) * y.astype(jnp.float32)).astype(h.dtype)
    return (o @ w_out).astype(h.dtype)


def swiglu(h, w_gu, w_down):
    gu = h @ w_gu
    return ((jax.nn.silu(gu[..., :D_FF]) * gu[..., D_FF:]) @ w_down).astype(h.dtype)


def trunk(x, norm_mix, w_in_a, q_norm, w_uq, kv_norm, w_ukv, w_out_a,
          w_in_r, conv_w, conv_b, lru_w_gate, lru_b_gate, lru_lambda, w_out_r,
          norm_ffn, w_gu, w_down, norm_final):
    for layer in range(DEPTH):
        h = rmsnorm(x, norm_mix[layer])
        j = layer // 2
        if layer % 2 == 0:
            x = x + attention_mixer(h, w_in_a[j], q_norm[j], w_uq[j], kv_norm[j], w_ukv[j], w_out_a[j])
        else:
            x = x + recurrent_mixer(h, w_in_r[j], conv_w[j], conv_b[j], lru_w_gate[j],
                                    lru_b_gate[j], lru_lambda[j], w_out_r[j])
        x = x + swiglu(rmsnorm(x, norm_ffn[layer]), w_gu[layer], w_down[layer])
    return rmsnorm(x, norm_final)


def setup_inputs(seed: int = 0) -> dict:
    key = jax.random.key(seed)
    ks = jax.random.split(key, 24)
    f32 = jnp.float32

    def nrm(k, shape, fan_in):
        return jax.random.normal(k, shape, f32) * (fan_in ** -0.5)

    def gain(k, shape):
        return 1.0 + 0.01 * jax.random.normal(k, shape, f32)

    u = jax.random.uniform(ks[20], (N_ODD, 2, D_RNN), f32, 0.9, 0.999)
    s = u ** (1.0 / LRU_C)
    lam = jnp.log(s) - jnp.log1p(-s)
    return {
        "x_prompt": jax.random.normal(ks[0], (BATCH, SEQ, D_MODEL), f32),
        "x_sample": jax.random.normal(ks[1], (DEC_BATCH, DEC_SEQ, D_MODEL), f32),
        "norm_mix": gain(ks[2], (DEPTH, D_MODEL)),
        "w_in_a": nrm(ks[3], (N_EVEN, D_MODEL, MIX_IN), D_MODEL),
        "q_norm": gain(ks[4], (N_EVEN, MLA_Q_RANK)),
        "w_uq": nrm(ks[5], (N_EVEN, MLA_Q_RANK, MLA_HEADS * (MLA_NOPE + MLA_ROPE)), MLA_Q_RANK),
        "kv_norm": gain(ks[6], (N_EVEN, MLA_KV_RANK)),
        "w_ukv": nrm(ks[7], (N_EVEN, MLA_KV_RANK, MLA_HEADS * (MLA_NOPE + MLA_V)), MLA_KV_RANK),
        "w_out_a": nrm(ks[8], (N_EVEN, MIX_OUT, D_MODEL), MIX_OUT),
        "w_in_r": nrm(ks[9], (N_ODD, D_MODEL, 2 * D_RNN), D_MODEL),
        "conv_w": nrm(ks[10], (N_ODD, CONV_W, D_RNN), CONV_W),
        "conv_b": 0.01 * jax.random.normal(ks[11], (N_ODD, D_RNN), f32),
        "lru_w_gate": nrm(ks[12], (N_ODD, 2, 2, LRU_BLOCKS, LRU_BW, LRU_BW), LRU_BW),
        "lru_b_gate": 0.01 * jax.random.normal(ks[13], (N_ODD, 2, 2, D_RNN), f32),
        "lru_lambda": lam,
        "w_out_r": nrm(ks[14], (N_ODD, D_RNN, D_MODEL), D_RNN),
        "norm_ffn": gain(ks[15], (DEPTH, D_MODEL)),
        "w_gu": nrm(ks[16], (DEPTH, D_MODEL, 2 * D_FF), D_MODEL),
        "w_down": nrm(ks[17], (DEPTH, D_FF, D_MODEL), D_FF),
        "norm_final": gain(ks[18], (D_MODEL,)),
    }


def reference(x_prompt, x_sample, norm_mix, w_in_a, q_norm, w_uq, kv_norm, w_ukv, w_out_a,
              w_in_r, conv_w, conv_b, lru_w_gate, lru_b_gate, lru_lambda, w_out_r,
              norm_ffn, w_gu, w_down, norm_final):
    y_prompt = trunk(x_prompt, norm_mix, w_in_a, q_norm, w_uq, kv_norm, w_ukv, w_out_a,
                     w_in_r, conv_w, conv_b, lru_w_gate, lru_b_gate, lru_lambda, w_out_r,
                     norm_ffn, w_gu, w_down, norm_final)
    y_sample = trunk(x_sample, norm_mix, w_in_a, q_norm, w_uq, kv_norm, w_ukv, w_out_a,
                     w_in_r, conv_w, conv_b, lru_w_gate, lru_b_gate, lru_lambda, w_out_r,
                     norm_ffn, w_gu, w_down, norm_final)
    return (y_prompt, y_sample)
```

```python
import numpy as np
import ml_dtypes
from contextlib import ExitStack
import concourse.bass as bass
import concourse.mybir as mybir
from concourse.bass_utils import run_bass_kernel_spmd

F32 = mybir.dt.float32
BF16 = mybir.dt.bfloat16
AF = mybir.ActivationFunctionType
ALU = mybir.AluOpType
AX = mybir.AxisListType

D = 1024
PAD = 1024
MIX_IN = 5024
D_RNN = 1536
D_FF = 2816
EPS = 1e-6
NEG = -30000.0


class Sem:
    def __init__(s, h):
        s.h = h
        s.cnt = 0


class Buf:
    def __init__(s, name):
        s.name = name
        s.w = {}
        s.r = {}
        s.sem = None
        s.excl = False


class KB:
    ENG = ('pe', 'act', 'dve', 'pool', 'sp')

    def __init__(s, nc, es):
        s.nc = nc
        s.es = es
        s.esem = {e: Sem(es.enter_context(nc.semaphore('s_' + e))) for e in ('pe', 'act', 'dve', 'pool')}
        s.n = {e: 0 for e in s.esem}
        s.waited = {e: {} for e in s.ENG}
        s.prog = {e: [] for e in s.ENG}
        s.free_dsem = []
        s.all_dsem = []
        s.ninst = 0
        s.qhist = {}
        s.QDEPTH = 6
        s.pending = []
        s.pend_reads = set()
        import os
        s.MAXOPS = int(os.environ.get("MAXOPS") or "100000000")
        s.TRACE = bool(os.environ.get("KTRACE", ""))
        s.trace = []

    def dsem(s):
        if s.free_dsem:
            return s.free_dsem.pop()
        sm = Sem(s.es.enter_context(s.nc.semaphore('d%d' % len(s.all_dsem))))
        s.all_dsem.append(sm)
        return sm

    def _wait(s, eng, need):
        for sm, v in need.items():
            if s.waited[eng].get(sm, 0) >= v:
                continue
            s.waited[eng][sm] = v
            s.prog[eng].append(lambda e, h=sm.h, v=v: e.wait_ge(h, v))

    def op(s, eng, fn, reads=(), writes=(), inc=True):
        if s.ninst >= s.MAXOPS:
            return
        s._autoflush(writes)
        need = {}
        own = s.esem[eng]

        def add(d, war):
            for sm, v in d.items():
                if sm is own and eng == 'pe':
                    continue
                if need.get(sm, 0) < v:
                    need[sm] = v
        for b in reads:
            add(b.w, False)
            if b.excl:
                for sm, v in b.r.items():
                    if sm is not own and need.get(sm, 0) < v:
                        need[sm] = v
        for b in writes:
            add(b.w, False)
            add(b.r, True)
        s._wait(eng, need)
        s.ninst += 1
        if s.TRACE:
            import traceback
            fr = traceback.extract_stack(limit=5)
            s.trace.append((s.ninst, eng, [(f.lineno) for f in fr[:-1]]))
        if inc:
            s.n[eng] += 1
            v = s.n[eng]
            s.prog[eng].append(lambda e, fn=fn, h=own.h: fn(e).then_inc(h, 1))
        else:
            v = s.n[eng] + 1
            s.prog[eng].append(lambda e, fn=fn: fn(e))
        for b in reads:
            b.r[own] = max(b.r.get(own, 0), v)
        for b in writes:
            b.w = {own: v}
            b.r = {}

    def dma(s, q, out, in_, sb, reads=(), writes=(), dr=(), dw=()):
        if q == 'pool':
            s.pending.append((out, in_, sb, tuple(reads), tuple(writes), tuple(dr), tuple(dw)))
            for b in reads:
                s.pend_reads.add(id(b))
            for b in dw:
                s.pend_reads.add(id(b))
            return
        s._autoflush(tuple(writes) + tuple(dr))
        s._dma(q, out, in_, sb, reads, writes, dr, dw)

    def _autoflush(s, writes):
        if s.pending:
            for b in writes:
                if id(b) in s.pend_reads:
                    s.flush_stores()
                    return

    def flush_stores(s):
        pend = s.pending
        s.pending = []
        s.pend_reads = set()
        for (out, in_, sb, reads, writes, dr, dw) in pend:
            s._dma('sp', out, in_, sb, reads, writes, dr, dw)

    def _dma(s, q, out, in_, sb, reads=(), writes=(), dr=(), dw=()):
        if s.ninst >= s.MAXOPS:
            return
        if sb.sem is None:
            sb.sem = s.dsem()
        sm = sb.sem
        need = {}

        def add(d, skip_same=False):
            for x, v in d.items():
                if skip_same and x is sm:
                    continue
                if need.get(x, 0) < v:
                    need[x] = v
        for b in reads:
            add(b.w)
        for b in writes:
            add(b.w, True)
            add(b.r)
        for b in dr:
            add(b.w)
        for b in dw:
            add(b.r)
        s._wait(q, need)
        hist = s.qhist.setdefault(q, [])
        if len(hist) >= s.QDEPTH:
            osm, ov = hist[-s.QDEPTH]
            s._wait(q, {osm: 16 * osm.cnt})
        sm.cnt += 1
        v = 16 * sm.cnt
        hist.append((sm, v))
        if len(hist) > 64:
            del hist[:32]
        s.ninst += 1
        s.prog[q].append(lambda e, o=out, i=in_, h=sm.h: e.dma_start(out=o, in_=i).then_inc(h, 16))
        for b in reads:
            b.r[sm] = v
        for b in writes:
            b.w = {sm: v}
            b.r = {}
        for b in dr:
            b.r[sm] = v
        for b in dw:
            b.w[sm] = v

    def barrier(s):
        s.flush_stores()
        toks = {}
        for e, sm in s.esem.items():
            if s.n[e] > 0:
                toks[sm] = s.n[e]
        for sm in s.all_dsem:
            if sm.cnt > 0:
                toks[sm] = 16 * sm.cnt
        for e in s.ENG:
            s._wait(e, dict(toks))
        s.free_dsem = list(s.all_dsem)

    def flush(s):
        return

    def real_flush(s):
        nc = s.nc
        prog = s.prog
        with nc.allow_non_contiguous_dma(reason="small parameter gathers"), nc.Block() as blk:
            @blk.tensor
            def _(e):
                for f in prog['pe']:
                    f(e)

            @blk.scalar
            def _(e):
                for f in prog['act']:
                    f(e)

            @blk.vector
            def _(e):
                for f in prog['dve']:
                    f(e)

            @blk.gpsimd
            def _(e):
                for f in prog['pool']:
                    f(e)

            @blk.sync
            def _(e):
                for f in prog['sp']:
                    f(e)
        s.prog = {e: [] for e in s.ENG}


class Ctx:
    pass


_UID = [0]
NAMES = {}


def mk_alloc(nc, es):
    def sb(name, shape, dt):
        _UID[0] += 1
        NAMES[name] = "t%d_%s" % (_UID[0], name)
        t = es.enter_context(nc.sbuf_tensor("t%d_%s" % (_UID[0], name), shape, dt))
        return t, Buf(name)

    def ps(name, shape=(128, 512), dt=F32):
        _UID[0] += 1
        t = es.enter_context(nc.psum_tensor("p%d_%s" % (_UID[0], name), list(shape), dt))
        b = Buf(name)
        b.excl = True
        return t, b
    return sb, ps


def mm(K, out, lhsT, rhs, start, stop, reads, writes, inc=False):
    K.op('pe', lambda e: e.matmul(out, lhsT=lhsT, rhs=rhs, start=start, stop=stop, skip_group_check=True),
         reads, writes, inc)


def actf(K, out, in_, func, reads, writes, scale=1.0, bias=None, accum=None):
    kw = {}
    if bias is not None:
        kw['bias'] = bias
    if accum is not None:
        kw['accum_out'] = accum
    K.op('act', lambda e: e.activation(out=out, in_=in_, func=func, scale=scale, **kw), reads, writes)


def tt(K, eng, out, in0, in1, op, reads, writes):
    K.op(eng, lambda e: e.tensor_tensor(out=out, in0=in0, in1=in1, op=op), reads, writes)


def ts(K, eng, out, in0, s1, s2, op0, op1, reads, writes):
    if s2 is None:
        K.op(eng, lambda e: e.tensor_scalar(out=out, in0=in0, scalar1=s1, scalar2=None, op0=op0), reads, writes)
    else:
        K.op(eng, lambda e: e.tensor_scalar(out=out, in0=in0, scalar1=s1, scalar2=s2, op0=op0, op1=op1),
             reads, writes)


def cp(K, eng, out, in_, reads, writes):
    if eng == 'act':
        K.op('act', lambda e: e.copy(out=out, in_=in_), reads, writes)
    else:
        K.op(eng, lambda e: e.tensor_copy(out=out, in_=in_), reads, writes)


def load_w(K, sbf, dst, dstb, src, kch, ncols, gain, gainb, tag, colchunk=1024):
    st0, stb0 = sbf(tag + "_st0", [128, colchunk], F32)
    st1, stb1 = sbf(tag + "_st1", [128, colchunk], F32)
    sts = [(st0, stb0), (st1, stb1)]
    i = 0
    engs = ['dve', 'pool', 'act']
    for kc in range(kch):
        for c0 in range(0, ncols, colchunk):
            c1 = min(ncols, c0 + colchunk)
            st, stb = sts[i % 2]
            K.dma('sp', st[:, 0:c1 - c0], src[kc * 128:(kc + 1) * 128, c0:c1], stb, writes=[stb])
            eng = engs[i % 3]
            o = dst(kc, c0, c1)
            if gain is None:
                cp(K, eng, o, st[:, 0:c1 - c0], [stb], [dstb])
            elif eng == 'act':
                K.op('act', lambda e, o=o, a=st[:, 0:c1 - c0], g=gain[:, kc:kc + 1]: e.activation(
                    out=o, in_=a, func=AF.Copy, scale=g), [stb, gainb], [dstb])
            else:
                ts(K, eng, o, st[:, 0:c1 - c0], gain[:, kc:kc + 1], None, ALU.mult, None, [stb, gainb], [dstb])
            i += 1


def rmsnorm_tok(K, x4, xb, ntile, width, h4, hb, tmp, scr, scrb, eng_mul='pool'):
    ss, ssb = tmp['ss']
    ms, msb = tmp['ms']
    sd, sdb = tmp['sd']
    rs, rsb = tmp['rs']
    xbl = xb if isinstance(xb, (list, tuple)) else [xb] * ntile
    for j in range(ntile):
        actf(K, scr[:, 0:width], x4[:, j, :], AF.Square, [xbl[j]], [scrb, ssb], accum=ss[:, j:j + 1])
    ts(K, 'dve', ms[:, 0:ntile], ss[:, 0:ntile], 1.0 / width, EPS, ALU.mult, ALU.add, [ssb], [msb])
    actf(K, sd[:, 0:ntile], ms[:, 0:ntile], AF.Sqrt, [msb], [sdb])
    K.op('dve', lambda e: e.reciprocal(out=rs[:, 0:ntile], in_=sd[:, 0:ntile]), [sdb], [rsb])
    for j in range(ntile):
        ts(K, eng_mul if j % 2 == 0 else 'dve', h4[:, j, :], x4[:, j, :], rs[:, j:j + 1], None, ALU.mult, None,
           [xbl[j], rsb], [hb])


def transpose_blk(K, h4, hb, ntile, nchunk, hT, hTb, psT, psTb, ident, identb):
    for j in range(ntile):
        pt, ptb = psT[j % len(psT)], psTb[j % len(psT)]
        for kc in range(nchunk):
            K.op('pe', lambda e, o=pt[:, kc * 128:(kc + 1) * 128], i=h4[:, j, kc * 128:(kc + 1) * 128]:
                 e.transpose(out=o, in_=i, identity=ident[:]), [hb, identb], [ptb], inc=(kc == nchunk - 1))
        cp(K, 'act' if j % 2 == 0 else 'dve', hT[:, :, j * 128:(j + 1) * 128],
           pt[:, 0:nchunk * 128].rearrange("p (c n) -> p c n", c=nchunk), [ptb], [hTb])


def colnorm(K, psl, pslb, nch, width, sq, sqb, ones, onesb, pss, pssb, rq, rqb, t1, t1b, outT, outb):
    for c in range(nch):
        actf(K, sq[:, c, :], psl[c][:, :], AF.Square, [pslb[c]], [sqb])
    for c in range(nch):
        mm(K, pss[:, :], ones[:, :], sq[:, c, :], c == 0, c == nch - 1, [onesb, sqb], [pssb], inc=(c == nch - 1))
    ts(K, 'dve', t1[:, :], pss[:, :], 1.0 / width, EPS, ALU.mult, ALU.add, [pssb], [t1b])
    actf(K, t1[:, :], t1[:, :], AF.Sqrt, [t1b], [t1b])
    K.op('dve', lambda e: e.reciprocal(out=rq[:, :], in_=t1[:, :]), [t1b], [rqb])
    for c in range(nch):
        tt(K, 'dve', outT[:, c, :], psl[c][:, :], rq[:, :], ALU.mult, [pslb[c], rqb], [outb])


def phase1(K, nc, C):
    SEQS = C.seqs
    with ExitStack() as es:
        sbf, psf = mk_alloc(nc, es)
        W = C.W
        ident, identb = sbf("ident", [128, 128], BF16)
        r96, r96b = sbf("r96", [128, 128], BF16)
        r128, r128b = sbf("r128", [128, 128], BF16)
        ones, onesb = sbf("ones", [128, 128], BF16)
        K.dma('sp', ident[:], C.cst['ident'][:, :], identb, writes=[identb])
        K.dma('sp', r96[:], C.cst['r96'][:, :], r96b, writes=[r96b])
        K.dma('sp', r128[:], C.cst['r128'][:, :], r128b, writes=[r128b])
        K.op('dve', lambda e: e.memset(ones[:], 1.0), [], [onesb])
        zt, ztb = sbf("zt", [128, 3072], BF16)
        K.op('pool', lambda e: e.memset(zt[:], 0.0), [], [ztb])
        for (off, S, poff) in SEQS:
            for p0 in (poff - PAD, poff + S):
                for g in range(3):
                    for h in range(8):
                        K.dma('pool', C.dkT[g, h, :, p0:p0 + PAD], zt[0:64, 0:PAD], ztb, reads=[ztb], dw=[C.dkTb])
                for r0 in range(0, PAD, 128):
                    K.dma('pool', C.dvA[p0 + r0:p0 + r0 + 128].rearrange("p g h e -> p (g h e)"), zt[:, :], ztb,
                          reads=[ztb], dw=[C.dvAb])
        g0, g0b = sbf("g0", [128, 8], F32)
        K.dma('sp', g0[:], W['norm_mix'][0].rearrange("(c p) -> p c", p=128), g0b, writes=[g0b])
        gq, gqb = sbf("gq", [128, 2], F32)
        K.dma('sp', gq[:], W['q_norm'][0].rearrange("(c p) -> p c", p=128), gqb, writes=[gqb])
        gk, gkb = sbf("gk", [128, 1], F32)
        K.dma('sp', gk[:], W['kv_norm'][0].rearrange("(c p) -> p c", p=128), gkb, writes=[gkb])
        w_in, w_inb = sbf("w_in", [128, 8, MIX_IN], BF16)
        wkr, wkrb = sbf("wkr", [128, 8, 128], BF16)
        w_uq, w_uqb = sbf("w_uq", [128, 2, 800], BF16)
        K.op('pool', lambda e: e.memset(w_uq[:], 0.0), [], [w_uqb])
        w_ukv, w_ukvb = sbf("w_ukv", [128, 1024], BF16)
        wvc, wvcb = sbf("wvc", [128, 512], BF16)
        with ExitStack() as es2:
            sbf2, _ = mk_alloc(nc, es2)
            load_w(K, sbf2, lambda kc, c0, c1: w_in[:, kc, c0:c1], w_inb, W['w_in_a'][0], 8, MIX_IN, g0, g0b, "wi")
            load_w(K, sbf2, lambda kc, c0, c1: w_uq[:, kc, c0:c1], w_uqb, W['w_uq'][0], 2, 768, gq, gqb, "wq")
            load_w(K, sbf2, lambda kc, c0, c1: w_ukv[:, c0:c1], w_ukvb, W['w_ukv'][0], 1, 1024, gk, gkb, "wk")
            for h in range(8):
                cp(K, 'dve', wvc[:, h * 64:(h + 1) * 64], w_ukv[:, h * 128 + 64:(h + 1) * 128], [w_ukvb], [wvcb])
            K.op('pool', lambda e: e.memset(wkr[:], 0.0), [], [wkrb])
            for kc in range(8):
                cp(K, 'dve', wkr[:, kc, 64:96], w_in[:, kc, 384:416], [w_inb], [wkrb])
            K.barrier()
            K.flush()
        if getattr(C, 'dbg', '') == 'init':
            return
        xblk, xblkb = sbf("xblk", [128, 4, D], F32)
        h4, h4b = sbf("h4", [128, 4, D], BF16)
        hT, hTb = sbf("hT", [128, 8, 512], BF16)
        scr, scrb = sbf("scr", [128, D], BF16)
        tmp = {k: sbf("n_" + k, [128, 4], F32) for k in ('ss', 'ms', 'sd', 'rs')}
        sq, sqb = sbf("sq", [128, 2, 512], BF16)
        cqn, cqnb = sbf("cqn", [128, 2, 512], BF16)
        ckvn, ckvnb = sbf("ckvn", [128, 1, 512], BF16)
        t1s = [sbf("t1_%d" % i, [128, 512], F32) for i in range(2)]
        t2, t2b = sbf("t2", [128, 512], F32)
        t3, t3b = sbf("t3", [128, 512], F32)
        rq, rqb = sbf("rq", [128, 512], F32)
        qas = [sbf("qa_%d" % i, [128, 512], BF16) for i in range(2)]
        rk = [0]
        qTb_, qTbb = sbf("qTblk", [96, 8, 512], BF16)
        kTb_, kTbb = sbf("kTblk", [96, 8, 512], BF16)
        krf, krfb = sbf("krf", [128, 512], BF16)
        vblk, vblkb = sbf("vblk", [128, 4, 8, 128], BF16)
        dblk, dblkb = sbf("dblk", [128, 12, 512], BF16)
        c96, c96b = sbf("c96", [128, 512], F32)
        s96, s96b = sbf("s96", [128, 512], F32)
        c128, c128b = sbf("c128", [128, 512], F32)
        s128, s128b = sbf("s128", [128, 512], F32)
        K.op('pool', lambda e: e.memset(vblk[:], 1.0), [], [vblkb])
        psT = []
        psTb = []
        for i in range(2):
            a, b = psf("psT%d" % i, (128, 1024), BF16)
            psT.append(a)
            psTb.append(b)
        PS = []
        PSb = []
        for i in range(6):
            a, b = psf("ps%d" % i)
            PS.append(a)
            PSb.append(b)
        rr = [0]

        def nps():
            rr[0] = (rr[0] + 1) % 4
            return PS[2 + rr[0]], PSb[2 + rr[0]]

        def rope_p1(psa, psab, ct, ctb):
            k = rk[0] % 2
            rk[0] += 1
            qa, qab = qas[k]
            t1, t1b = t1s[k]
            cp(K, 'act', qa[:, :], psa[:, :], [psab], [qab])
            tt(K, 'dve', t1[:, :], psa[:, :], ct[:, :], ALU.mult, [psab, ctb], [t1b])
            return k

        def rope_p2(k, nout, rmat, rmatb, st_, stb_, dst, dstw):
            qa, qab = qas[k]
            t1, t1b = t1s[k]
            pr, prb = nps()
            mm(K, pr[:, :], rmat[:, :], qa[:, :], True, True, [rmatb, qab], [prb], inc=True)
            tt(K, 'dve', t2[:, :], pr[:, :], st_[:, :], ALU.mult, [prb, stb_], [t2b])
            tt(K, 'pool', dst, t1[0:nout, :], t2[0:nout, :], ALU.add, [t1b, t2b], dstw)

        def rope_fm(psa, psab, nout, rmat, rmatb, ct, ctb, st_, stb_, dst, dstb, dstw):
            k = rope_p1(psa, psab, ct, ctb)
            rope_p2(k, nout, rmat, rmatb, st_, stb_, dst, dstw)

        for (off, S, poff) in SEQS:
            for blk in range(S // 512):
                t = off + blk * 512
                p = blk * 512
                tp = poff + blk * 512
                K.dma('sp', xblk[:], C.x[t:t + 512, :].rearrange("(j p) d -> p j d", p=128), xblkb, writes=[xblkb])
                K.dma('sp', c96[:], C.cst['c96'][:, p:p + 512], c96b, writes=[c96b])
                K.dma('sp', s96[:], C.cst['s96'][:, p:p + 512], s96b, writes=[s96b])
                K.dma('sp', c128[:], C.cst['c128'][:, p:p + 512], c128b, writes=[c128b])
                K.dma('sp', s128[:], C.cst['s128'][:, p:p + 512], s128b, writes=[s128b])
                rmsnorm_tok(K, xblk, xblkb, 4, D, h4, h4b, tmp, scr, scrb)
                transpose_blk(K, h4, h4b, 4, 8, hT, hTb, psT, psTb, ident, identb)
                if getattr(C, 'dbg', '') == 'norm':
                    K.barrier()
                    K.flush()
                    return
                for c in range(2):
                    for kc in range(8):
                        mm(K, PS[c][:, :], w_in[:, kc, c * 128:(c + 1) * 128], hT[:, kc, :], kc == 0, kc == 7,
                           [w_inb, hTb], [PSb[c]], inc=(kc == 7))
                pss, pssb = nps()
                colnorm(K, [PS[0], PS[1]], [PSb[0], PSb[1]], 2, 256, sq, sqb, ones, onesb, pss, pssb, rq, rqb,
                        t3, t3b, cqn, cqnb)
                pend = None
                for h in range(8):
                    pq, pqb = nps()
                    for c in range(2):
                        mm(K, pq[:, :], w_uq[:, c, h * 96:h * 96 + 128], cqn[:, c, :], c == 0, c == 1,
                           [w_uqb, cqnb], [pqb], inc=(c == 1))
                    k_ = rope_p1(pq, pqb, c96, c96b)
                    if pend is not None:
                        rope_p2(pend[0], 96, r96, r96b, s96, s96b, qTb_[0:96, pend[1], :], [qTbb])
                    pend = (k_, h)
                rope_p2(pend[0], 96, r96, r96b, s96, s96b, qTb_[0:96, pend[1], :], [qTbb])
                import os
                if os.environ.get("DBG3", "") != "nostore":
                    K.dma('pool', C.qT[:, :, t:t + 512].rearrange("h r n -> r h n"), qTb_[:], qTbb, reads=[qTbb],
                          dw=[C.qTb])
                if getattr(C, 'dbg', '') == 'q':
                    K.barrier()
                    K.flush()
                    return
                for kc in range(8):
                    mm(K, PS[0][:, :], w_in[:, kc, 256:384], hT[:, kc, :], kc == 0, kc == 7, [w_inb, hTb], [PSb[0]],
                       inc=(kc == 7))
                pss, pssb = nps()
                colnorm(K, [PS[0]], [PSb[0]], 1, 128, sq, sqb, ones, onesb, pss, pssb, rq, rqb, t3, t3b, ckvn, ckvnb)
                if getattr(C, 'dbg', '') == 'kv1':
                    K.barrier()
                    return
                for kc in range(8):
                    mm(K, PS[1][:, :], wkr[:, kc, :], hT[:, kc, :], kc == 0, kc == 7, [wkrb, hTb], [PSb[1]],
                       inc=(kc == 7))
                rope_fm(PS[1], PSb[1], 96, r96, r96b, c96, c96b, s96, s96b, krf[0:96, :], krfb, [krfb])
                if getattr(C, 'dbg', '') == 'kv2':
                    K.barrier()
                    return
                for h in range(8):
                    cp(K, 'pool', kTb_[64:96, h, :], krf[64:96, :], [krfb], [kTbb])
                    pk, pkb = nps()
                    mm(K, pk[:, :], w_ukv[:, h * 128:(h + 1) * 128], ckvn[:, 0, :], True, True, [w_ukvb, ckvnb],
                       [pkb], inc=True)
                    cp(K, 'act' if h % 2 == 0 else 'dve', kTb_[0:64, h, :], pk[0:64, :], [pkb], [kTbb])
                if getattr(C, 'dbg', '') == 'kv3':
                    K.barrier()
                    return
                K.dma('pool', C.kT[:, :, t:t + 512].rearrange("h r n -> r h n"), kTb_[:], kTbb, reads=[kTbb],
                      dw=[C.kTb])
                if getattr(C, 'dbg', '') == 'kv4':
                    K.barrier()
                    return
                for j in range(4):
                    pv, pvb = nps()
                    mm(K, pv[:, :], ckvn[:, 0, j * 128:(j + 1) * 128], wvc[:, :],
                       True, True, [wvcb, ckvnb], [pvb], inc=True)
                    cp(K, 'act' if j % 2 == 0 else 'dve', vblk[:, j, :, 0:64],
                       pv[:, :].rearrange("p (h d) -> p h d", d=64), [pvb], [vblkb])
                K.dma('pool', C.vA[t:t + 512].rearrange("(j p) h e -> p j h e", p=128), vblk[:], vblkb,
                      reads=[vblkb], dw=[C.vAb])
                if getattr(C, 'dbg', '') == 'kv':
                    K.barrier()
                    K.flush()
                    return
                for qk in range(2):
                    pend = None
                    for c in range(12):
                        col0 = 416 + (qk * 12 + c) * 128
                        pa, pab = nps()
                        for kc in range(8):
                            mm(K, pa[:, :], w_in[:, kc, col0:col0 + 128], hT[:, kc, :], kc == 0, kc == 7,
                               [w_inb, hTb], [pab], inc=(kc == 7))
                        k_ = rope_p1(pa, pab, c128, c128b)
                        if pend is not None:
                            rope_p2(pend[0], 128, r128, r128b, s128, s128b, dblk[:, pend[1], :], [dblkb])
                        pend = (k_, c)
                    rope_p2(pend[0], 128, r128, r128b, s128, s128b, dblk[:, pend[1], :], [dblkb])
                    dst = C.dqT if qk == 0 else C.dkT
                    dstb = C.dqTb if qk == 0 else C.dkTb
                    tt0 = t if qk == 0 else tp
                    for h2 in range(2):
                        dv = dst.rearrange("g (hp h2) d n -> h2 d (g hp) n", h2=2)[h2, :, :, tt0:tt0 + 512]
                        K.dma('pool', dv, dblk[h2 * 64:(h2 + 1) * 64, :, :], dblkb, reads=[dblkb], dw=[dstb])
                for g in range(3):
                    vc0 = 416 + 3072 + g * 512
                    for j in range(4):
                        pv, pvb = nps()
                        for kc in range(8):
                            mm(K, pv[:, :], hT[:, kc, j * 128:(j + 1) * 128], w_in[:, kc, vc0:vc0 + 512], kc == 0,
                               kc == 7, [w_inb, hTb], [pvb], inc=(kc == 7))
                        cp(K, 'act' if j % 2 == 0 else 'dve', vblk[:, j, :, 0:64],
                           pv[:, :].rearrange("p (h d) -> p h d", d=64), [pvb], [vblkb])
                    K.dma('pool', C.dvA[tp:tp + 512, g].rearrange("(j p) h e -> p j h e", p=128), vblk[:], vblkb,
                          reads=[vblkb], dw=[C.dvAb])
        K.barrier()
        K.flush()


def phase2_mla(K, nc, C):
    scale = 96.0 ** -0.5
    with ExitStack() as es:
        sbf, psf = mk_alloc(nc, es)
        SMAX = max(S for (_, S, _) in C.seqs)
        kt, ktb = sbf("kt", [128, SMAX], BF16)
        K.op('pool', lambda e: e.memset(kt[64:128, :], 0.0), [], [ktb])
        va, vab = sbf("va", [128, SMAX // 128, 128], BF16)
        qs = [sbf("q%d" % i, [128, 512], BF16) for i in range(2)]
        for q_, qb__ in qs:
            K.op('pool', lambda e, q_=q_: e.memset(q_[64:128, :], 0.0), [], [qb__])
        pts = [sbf("pt%d" % i, [128, 1024], BF16) for i in range(3)]
        rc, rcb = sbf("rc", [128, 512], F32)
        ob = [sbf("ob%d" % i, [64, 512], BF16) for i in range(2)]
        PSs = [psf("pss%d" % i, (128, 1024)) for i in range(3)]
        PSo = [psf("pso%d" % i) for i in range(2)]
        qi = 0
        si = 0
        for (off, S, poff) in C.seqs:
            nkt = S // 128
            for h in range(8):
                K.dma('sp', kt[0:96, 0:S], C.kT[h, :, off:off + S], ktb, writes=[ktb], dr=[C.kTb])
                K.dma('sp', va[:, 0:nkt, :], C.vA[off:off + S, h, :].rearrange("(m p) e -> p m e", p=128), vab,
                      writes=[vab], dr=[C.vAb])
                for qb in range(S // 512):
                    t = off + qb * 512
                    q, qb_ = qs[qi % 2]
                    po, pob = PSo[qi % 2]
                    o_, ob_ = ob[qi % 2]
                    qi += 1
                    K.dma('sp', q[0:96, :], C.qT[h, :, t:t + 512], qb_, writes=[qb_], dr=[C.qTb])
                    prev = None
                    npair = nkt // 2
                    for mp in range(npair + 1):
                        cur = None
                        if mp < npair:
                            ps_, psb_ = PSs[si % 3]
                            pt, ptb = pts[si % 3]
                            si += 1
                            for hf in range(2):
                                m = 2 * mp + hf
                                mm(K, ps_[:, hf * 512:(hf + 1) * 512], kt[:, m * 128:(m + 1) * 128], q[:, :], True,
                                   True, [ktb, qb_], [psb_], inc=(hf == 1))
                            actf(K, pt[:, :], ps_[:, :], AF.Exp, [psb_], [ptb], scale=scale)
                            cur = (mp, pt, ptb)
                        if prev is not None:
                            pm, ppt, pptb = prev
                            for hf in range(2):
                                m = 2 * pm + hf
                                mm(K, po[:, :], va[:, m, :], ppt[:, hf * 512:(hf + 1) * 512], m == 0, m == nkt - 1,
                                   [vab, pptb], [pob], inc=(m == nkt - 1))
                        prev = cur
                    K.op('dve', lambda e, o=rc[64:128, :], i=po[64:128, :]: e.reciprocal(out=o, in_=i), [pob], [rcb])
                    tt(K, 'dve', o_[0:64, :], po[0:64, :], rc[64:128, :], ALU.mult, [pob, rcb], [ob_])
                    K.dma('pool', C.oT[h * 64:(h + 1) * 64, t:t + 512], o_[:], ob_, reads=[ob_], dw=[C.oTb])
        K.barrier()
        K.flush()


def phase2_dil(K, nc, C):
    scale = 0.125
    DIL = (1, 4, 16)
    with ExitStack() as es:
        sbf, psf = mk_alloc(nc, es)
        mk = {}
        for name in ('A128', 'B128', 'A128f', 'B128l', 'A1f', 'B1l', 'A32', 'A32u0', 'A32u32', 'A32l', 'B32', 'ALL'):
            mk[name] = sbf("m_" + name, [128, 512], BF16)
            K.dma('sp', mk[name][0][:], C.cst['m_' + name][:, :], mk[name][1], writes=[mk[name][1]])
        ident, identb = sbf("ident", [128, 128], BF16)
        K.dma('sp', ident[:], C.cst['ident'][:, :], identb, writes=[identb])
        zl, zlb = sbf("zl", [128, 128], BF16)
        K.op('pool', lambda e: e.memset(zl[:], 0.0), [], [zlb])
        zr, zrb = sbf("zr", [128, 512], BF16)
        K.op('pool', lambda e: e.memset(zr[:], 0.0), [], [zrb])
        spans = [512 + 128 * d for d in DIL]
        ntile = [5, 8, 32]
        nbuf = [2, 2, 1]
        ktl = [[sbf("kt%d_%d" % (g, i), [64, 4, spans[g]], BF16) for i in range(nbuf[g])] for g in range(3)]
        vtl = [[sbf("vt%d_%d" % (g, i), [128, ntile[g], 4, 128], BF16) for i in range(nbuf[g])] for g in range(3)]
        qtl = [sbf("dq%d" % i, [64, 4, 512], BF16) for i in range(6)]
        pts = [sbf("dpt%d" % i, [128, 512], BF16) for i in range(4)]
        sidx = [0]
        oidx = [0]
        acc = [sbf("acc%d" % i, [128, 512], F32) for i in range(8)]
        rc, rcb = sbf("drc", [128, 512], F32)
        obs = [sbf("dob%d" % i, [64, 512], BF16) for i in range(8)]
        PSs = [psf("dpss%d" % i) for i in range(4)]
        PSo = [psf("dpso%d" % i) for i in range(3)]
        ui = [0, 0, 0]
        qi = 0
        si = 0
        oi = 0
        ai = 0
        for (off, S, poff) in C.seqs:
            for qc in range(S // 512):
                t0 = qc * 512
                t = off + t0
                for hh in range(2):
                    accs = [acc[(ai % 2) * 4 + hl] + obs[(ai % 2) * 4 + hl] for hl in range(4)]
                    ai += 1
                    units = []
                    for g in range(3):
                        d = DIL[g]
                        L = S // d
                        k_, kb_ = ktl[g][ui[g] % nbuf[g]]
                        v_, vb_ = vtl[g][ui[g] % nbuf[g]]
                        ui[g] += 1
                        q_, qb_ = qtl[((ai - 1) % 2) * 3 + g]
                        ks = poff + t0 - 64 * d
                        K.dma('sp', k_[:], C.dkT[g, hh * 4:(hh + 1) * 4, :, ks:ks + spans[g]].rearrange(
                            "h d n -> d h n"), kb_, writes=[kb_], dr=[C.dkTb])
                        K.dma('sp', q_[:], C.dqT[g, hh * 4:(hh + 1) * 4, :, t:t + 512].rearrange("h d n -> d h n"),
                              qb_, writes=[qb_], dr=[C.dqTb])
                        if g == 0:
                            src = C.dvA[ks:ks + 640, g, hh * 4:(hh + 1) * 4, :].rearrange(
                                "(m p) h e -> p m h e", p=128)
                            K.dma('sp', v_[:], src, vb_, writes=[vb_], dr=[C.dvAb])
                        else:
                            for mi in range(2):
                                base = ks + mi * 128 * d
                                nk_ = 32 if (g == 2 and mi == 1) else 128
                                src = C.dvA[base:base + nk_ * d, g, hh * 4:(hh + 1) * 4, :].rearrange(
                                    "(p r) h e -> p r h e", r=d)
                                K.dma('sp', v_[0:nk_, mi * d:(mi + 1) * d, :, :], src, vb_, writes=[vb_],
                                      dr=[C.dvAb])
                        for hl in range(4):
                            units.append((g, d, L, hl, k_, kb_, v_, vb_, q_, qb_))
                    def stage1(u):
                        g, d, L, hl, k_, kb_, v_, vb_, q_, qb_ = u
                        nsub, nq = (4, 128) if g < 2 else (16, 32)
                        u0c = t0 // d
                        res = []
                        for ab in range(2):
                            ps_, psb_ = PSs[sidx[0] % 4]
                            pt, ptb = pts[sidx[0] % 4]
                            sidx[0] += 1
                            if g == 2:
                                if ab == 0:
                                    nm = 'A32u0' if u0c == 0 else ('A32u32' if u0c == 32 else (
                                        'A32l' if u0c == L - 32 else 'A32'))
                                else:
                                    nm = 'ALL' if u0c >= L - 64 else 'B32'
                            elif g == 1:
                                if ab == 0:
                                    nm = 'A128f' if u0c == 0 else 'A128'
                                else:
                                    nm = 'B128l' if u0c == L - 128 else 'B128'
                            else:
                                if ab == 0:
                                    nm = 'A1f' if t0 == 0 else 'A128'
                                else:
                                    nm = 'B1l' if t0 == S - 512 else 'B128'
                            mt, mtb = mk[nm]
                            nk_ = 32 if (g == 2 and ab == 1) else 128
                            mm(K, ps_[0:nk_, :], ident[0:nk_, 0:nk_], mt[0:nk_, :], True, False, [identb, mtb],
                               [psb_])
                            for s_ in range(nsub):
                                if g == 0:
                                    kk = k_[:, hl, (s_ + ab) * 128:(s_ + ab + 1) * 128]
                                    qq = q_[:, hl, s_ * 128:(s_ + 1) * 128]
                                else:
                                    kk = k_[:, hl, ab * 128 * d + s_:ab * 128 * d + s_ + (nk_ - 1) * d + 1:d]
                                    qq = q_[:, hl, s_:512:d]
                                mm(K, ps_[0:nk_, s_ * nq:(s_ + 1) * nq], kk, qq, False, s_ == nsub - 1,
                                   [kb_, qb_], [psb_], inc=(s_ == nsub - 1))
                            actf(K, pt[0:nk_, :], ps_[0:nk_, :], AF.Exp, [psb_], [ptb], scale=scale)
                            res.append((pt, ptb, nk_))
                        return res

                    def stage2(u, res):
                        g, d, L, hl, k_, kb_, v_, vb_, q_, qb_ = u
                        h = hh * 4 + hl
                        nsub, nq = (4, 128) if g < 2 else (16, 32)
                        po, pob = PSo[oidx[0] % 3]
                        oidx[0] += 1
                        mm(K, po[:, :], zl[:, :], zr[:, :], True, False, [zlb, zrb], [pob])
                        for ab in range(2):
                            pt, ptb, nk_ = res[ab]
                            for s_ in range(nsub):
                                if g == 0:
                                    vv = v_[:, s_ + ab, hl, :]
                                else:
                                    vv = v_[0:nk_, ab * d + s_, hl, :]
                                last = (ab == 1 and s_ == nsub - 1)
                                mm(K, po[:, s_ * nq:(s_ + 1) * nq], vv, pt[0:nk_, s_ * nq:(s_ + 1) * nq], False, last,
                                   [vb_, ptb], [pob], inc=last)
                        a_, ab_, o_, ob_ = accs[hl]
                        if g == 0:
                            cp(K, 'act', a_[:, :], po[:, :], [pob], [ab_])
                        else:
                            src = po[:, :].rearrange("p (r i) -> p i r", r=d)
                            tt(K, 'dve', a_[:, :].rearrange("p (i r) -> p i r", r=d),
                               a_[:, :].rearrange("p (i r) -> p i r", r=d), src, ALU.add, [pob, ab_], [ab_])
                        if g == 2:
                            K.op('dve', lambda e, o=rc[0:64, :], i=a_[64:128, :]: e.reciprocal(out=o, in_=i),
                                 [ab_], [rcb])
                            tt(K, 'dve', o_[0:64, :], a_[0:64, :], rc[0:64, :], ALU.mult, [ab_, rcb], [ob_])
                            K.dma('pool', C.oT[512 + h * 64:512 + (h + 1) * 64, t:t + 512], o_[:], ob_,
                                  reads=[ob_], dw=[C.oTb])

                    prev = None
                    for u in units:
                        r_ = stage1(u)
                        if prev is not None:
                            stage2(*prev)
                        prev = (u, r_)
                    stage2(*prev)
        K.barrier()
        K.flush()


def phase_proj(K, nc, C, inT, inTb, nk, wsrc, xin, xinb, xout, xoutb, tag, seqs=None):
    seqs = seqs or C.seqs
    with ExitStack() as es:
        sbf, psf = mk_alloc(nc, es)
        w, wb = sbf(tag + "w", [128, nk, D], BF16)
        with ExitStack() as es2:
            sbf2, _ = mk_alloc(nc, es2)
            load_w(K, sbf2, lambda kc, c0, c1: w[:, kc, c0:c1], wb, wsrc, nk, D, None, None, tag + "l")
            K.barrier()
            K.flush()
        its = [sbf(tag + "in%d" % i, [128, nk, 512], BF16) for i in range(2)]
        xs = [sbf(tag + "x%d" % i, [128, 4, D], F32) for i in range(2)]
        PS = [psf(tag + "ps%d" % i) for i in range(4)]
        bi = 0
        pi = 0
        for (off, S, poff) in seqs:
            for blk in range(S // 512):
                t = off + blk * 512
                it, itb = its[bi % 2]
                x_, xb_ = xs[bi % 2]
                bi += 1
                K.dma('sp', it[:], inT[:, t:t + 512].rearrange("(c p) n -> p c n", p=128), itb, writes=[itb],
                      dr=[inTb])
                K.dma('sp', x_[:], xin[t:t + 512, :].rearrange("(j p) d -> p j d", p=128), xb_, writes=[xb_],
                      dr=[xinb])
                for j in range(4):
                    for hf in range(2):
                        ps_, psb_ = PS[pi % 4]
                        pi += 1
                        for kc in range(nk):
                            mm(K, ps_[:, :], it[:, kc, j * 128:(j + 1) * 128], w[:, kc, hf * 512:(hf + 1) * 512],
                               kc == 0, kc == nk - 1, [itb, wb], [psb_], inc=(kc == nk - 1))
                        tt(K, 'dve', x_[:, j, hf * 512:(hf + 1) * 512], x_[:, j, hf * 512:(hf + 1) * 512], ps_[:, :],
                           ALU.add, [xb_, psb_], [xb_])
                K.dma('pool', xout[t:t + 512, :].rearrange("(j p) d -> p j d", p=128), x_[:], xb_, reads=[xb_],
                      dw=[xoutb])
        K.barrier()
        K.flush()


def phase_ffn(K, nc, C, layer, xin, xinb, xout, xoutb, final, tag, seqs=None):
    W = C.W
    seqs = seqs or C.seqs
    with ExitStack() as es:
        sbf, psf = mk_alloc(nc, es)
        ident, identb = sbf(tag + "ident", [128, 128], BF16)
        K.dma('sp', ident[:], C.cst['ident'][:, :], identb, writes=[identb])
        gf, gfb = sbf(tag + "gf", [128, 8], F32)
        K.dma('sp', gf[:], W['norm_ffn'][layer].rearrange("(c p) -> p c", p=128), gfb, writes=[gfb])
        wgu, wgub = sbf(tag + "wgu", [128, 8, 2 * D_FF], BF16)
        wdn, wdnb = sbf(tag + "wdn", [128, 22, D], BF16)
        with ExitStack() as es2:
            sbf2, _ = mk_alloc(nc, es2)
            load_w(K, sbf2, lambda kc, c0, c1: wgu[:, kc, c0:c1], wgub, W['w_gu'][layer], 8, 2 * D_FF, gf, gfb,
                   tag + "lg")
            load_w(K, sbf2, lambda kc, c0, c1: wdn[:, kc, c0:c1], wdnb, W['w_down'][layer], 22, D, None, None,
                   tag + "ld")
            K.barrier()
            K.flush()
        if final:
            gfin, gfinb = sbf(tag + "gfin", [128, D], F32)
            K.dma('sp', gfin[:], W['norm_final'].partition_broadcast(128), gfinb, writes=[gfinb])
        x_, xb_ = sbf(tag + "x", [128, 4, D], F32)
        xbj = [Buf(tag + "x%d" % j) for j in range(4)]
        h4, h4b = sbf(tag + "h4", [128, 4, D], BF16)
        hT, hTb = sbf(tag + "hT", [128, 8, 512], BF16)
        scr, scrb = sbf(tag + "scr", [128, D], BF16)
        tmp = {k: sbf(tag + "n_" + k, [128, 4], F32) for k in ('ss', 'ms', 'sd', 'rs')}
        aT, aTb = sbf(tag + "aT", [128, 22, 512], BF16)
        sg = [sbf(tag + "sg%d" % i, [128, 512], F32) for i in range(2)]
        psT = [psf(tag + "psT%d" % i, (128, 1024), BF16) for i in range(2)]
        PS = [psf(tag + "ps%d" % i) for i in range(6)]
        pi = 0
        gi = 0
        for (off, S, poff) in seqs:
            for blk in range(S // 512):
                t = off + blk * 512
                for j in range(4):
                    K.dma('sp', x_[:, j, :], xin[t + j * 128:t + (j + 1) * 128, :], xbj[j], writes=[xbj[j]],
                          dr=[xinb])
                rmsnorm_tok(K, x_, xbj, 4, D, h4, h4b, tmp, scr, scrb)
                transpose_blk(K, h4, h4b, 4, 8, hT, hTb, [p[0] for p in psT], [p[1] for p in psT], ident, identb)
                for c in range(22):
                    pg, pgb = PS[pi % 6]
                    pu, pub = PS[(pi + 1) % 6]
                    pi += 2
                    for kc in range(8):
                        mm(K, pg[:, :], wgu[:, kc, c * 128:(c + 1) * 128], hT[:, kc, :], kc == 0, kc == 7,
                           [wgub, hTb], [pgb], inc=(kc == 7))
                    for kc in range(8):
                        mm(K, pu[:, :], wgu[:, kc, D_FF + c * 128:D_FF + (c + 1) * 128], hT[:, kc, :], kc == 0,
                           kc == 7, [wgub, hTb], [pub], inc=(kc == 7))
                    s_, sb_ = sg[gi % 2]
                    gi += 1
                    actf(K, s_[:, :], pg[:, :], AF.Silu, [pgb], [sb_])
                    tt(K, 'dve', aT[:, c, :], s_[:, :], pu[:, :], ALU.mult, [sb_, pub], [aTb])
                for j in range(4):
                    for hf in range(2):
                        ps_, psb_ = PS[pi % 6]
                        pi += 1
                        for c in range(22):
                            mm(K, ps_[:, :], aT[:, c, j * 128:(j + 1) * 128], wdn[:, c, hf * 512:(hf + 1) * 512],
                               c == 0, c == 21, [aTb, wdnb], [psb_], inc=(c == 21))
                        tt(K, 'dve', x_[:, j, hf * 512:(hf + 1) * 512], x_[:, j, hf * 512:(hf + 1) * 512], ps_[:, :],
                           ALU.add, [xbj[j], psb_], [xbj[j]])
                    if not final:
                        K.dma('pool', xout[t + j * 128:t + (j + 1) * 128, :], x_[:, j, :], xbj[j], reads=[xbj[j]],
                              dw=[xoutb])
                if final:
                    ss, ssb = tmp['ss']
                    ms, msb = tmp['ms']
                    sd, sdb = tmp['sd']
                    rs, rsb = tmp['rs']
                    for j in range(4):
                        actf(K, scr[:, :], x_[:, j, :], AF.Square, [xbj[j]], [scrb, ssb], accum=ss[:, j:j + 1])
                    ts(K, 'dve', ms[:, 0:4], ss[:, 0:4], 1.0 / D, EPS, ALU.mult, ALU.add, [ssb], [msb])
                    actf(K, sd[:, 0:4], ms[:, 0:4], AF.Sqrt, [msb], [sdb])
                    K.op('dve', lambda e: e.reciprocal(out=rs[:, 0:4], in_=sd[:, 0:4]), [sdb], [rsb])
                    for j in range(4):
                        K.op('dve', lambda e, o=x_[:, j, :], r=rs[:, j:j + 1]: e.scalar_tensor_tensor(
                            out=o, in0=o, scalar=r, in1=gfin[:, :], op0=ALU.mult, op1=ALU.mult),
                            [xbj[j], rsb, gfinb], [xbj[j]])
                        K.dma('pool', xout[t + j * 128:t + (j + 1) * 128, :], x_[:, j, :], xbj[j], reads=[xbj[j]],
                              dw=[xoutb])
        K.barrier()
        K.flush()


def phase4(K, nc, C):
    W = C.W
    with ExitStack() as es:
        sbf, psf = mk_alloc(nc, es)
        ident, identb = sbf("p4ident", [128, 128], BF16)
        K.dma('sp', ident[:], C.cst['ident'][:, :], identb, writes=[identb])
        g1, g1b = sbf("p4g", [128, 8], F32)
        K.dma('sp', g1[:], W['norm_mix'][1].rearrange("(c p) -> p c", p=128), g1b, writes=[g1b])
        w, wb = sbf("p4w", [128, 8, 2 * D_RNN], BF16)
        with ExitStack() as es2:
            sbf2, _ = mk_alloc(nc, es2)
            load_w(K, sbf2, lambda kc, c0, c1: w[:, kc, c0:c1], wb, W['w_in_r'][0], 8, 2 * D_RNN, g1, g1b, "p4l")
            K.barrier()
            K.flush()
        zt, ztb = sbf("p4z", [128, 12, 2], F32)
        K.op('pool', lambda e: e.memset(zt[:], 0.0), [], [ztb])
        for si, (off, S, poff) in enumerate(C.seqs):
            b0 = off + 4 * si
            K.dma('pool', C.xr[:, b0:b0 + 2].rearrange("(c p) n -> p c n", p=128), zt[:], ztb, reads=[ztb],
                  dw=[C.xrb])
            K.dma('pool', C.xr[:, b0 + 2 + S:b0 + 4 + S].rearrange("(c p) n -> p c n", p=128), zt[:], ztb,
                  reads=[ztb], dw=[C.xrb])
        x_, xb_ = sbf("p4x", [128, 4, D], F32)
        h4, h4b = sbf("p4h4", [128, 4, D], BF16)
        hT, hTb = sbf("p4hT", [128, 8, 512], BF16)
        scr, scrb = sbf("p4scr", [128, D], BF16)
        tmp = {k: sbf("p4n_" + k, [128, 4], F32) for k in ('ss', 'ms', 'sd', 'rs')}
        yb, ybb = sbf("p4y", [128, 12, 512], BF16)
        xrb_, xrbb = sbf("p4xr", [128, 12, 512], F32)
        psT = [psf("p4psT%d" % i, (128, 1024), BF16) for i in range(2)]
        PS = [psf("p4ps%d" % i) for i in range(6)]
        pi = 0
        for si, (off, S, poff) in enumerate(C.seqs):
            for blk in range(S // 512):
                t = off + blk * 512
                tx = off + 4 * si + 2 + blk * 512
                K.dma('sp', x_[:], C.x1[t:t + 512, :].rearrange("(j p) d -> p j d", p=128), xb_, writes=[xb_],
                      dr=[C.x1b])
                rmsnorm_tok(K, x_, xb_, 4, D, h4, h4b, tmp, scr, scrb)
                transpose_blk(K, h4, h4b, 4, 8, hT, hTb, [p[0] for p in psT], [p[1] for p in psT], ident, identb)
                for c in range(24):
                    ps_, psb_ = PS[pi % 6]
                    pi += 1
                    for kc in range(8):
                        mm(K, ps_[:, :], w[:, kc, c * 128:(c + 1) * 128], hT[:, kc, :], kc == 0, kc == 7, [wb, hTb],
                           [psb_], inc=(kc == 7))
                    if c < 12:
                        actf(K, yb[:, c, :], ps_[:, :], AF.Gelu_apprx_tanh, [psb_], [ybb])
                    else:
                        cp(K, 'dve', xrb_[:, c - 12, :], ps_[:, :], [psb_], [xrbb])
                K.dma('pool', C.yg[:, t:t + 512].rearrange("(c p) n -> p c n", p=128), yb[:], ybb, reads=[ybb],
                      dw=[C.ygb])
                K.dma('pool', C.xr[:, tx:tx + 512].rearrange("(c p) n -> p c n", p=128), xrb_[:], xrbb,
                      reads=[xrbb], dw=[C.xrb])
        K.barrier()
        K.flush()


def phase5(K, nc, C):
    W = C.W
    TC = 2048
    with ExitStack() as es:
        sbf, psf = mk_alloc(nc, es)
        cw, cwb = sbf("cw", [128, 12, 4], F32)
        cb, cbb = sbf("cb", [128, 12], F32)
        gb, gbb = sbf("gb", [128, 12, 4], F32)
        lam, lamb = sbf("lam", [128, 12, 2], F32)
        cc, ccb = sbf("cc", [128, 12, 2], F32)
        gw, gwb = sbf("gw", [128, 12, 4, 128], BF16)
        for j in range(4):
            K.dma('sp', cw[:, :, j], W['conv_w'][0][j].rearrange("(n p) -> p n", p=128), cwb, writes=[cwb])
        K.dma('sp', cb[:], W['conv_b'][0].rearrange("(n p) -> p n", p=128), cbb, writes=[cbb])
        for a in range(2):
            for k in range(2):
                K.dma('sp', gb[:, :, a * 2 + k], W['lru_b_gate'][0][a, k].rearrange("(n p) -> p n", p=128), gbb,
                      writes=[gbb])
            K.dma('sp', lam[:, :, a], W['lru_lambda'][0][a].rearrange("(n p) -> p n", p=128), lamb, writes=[lamb])
        with ExitStack() as es2:
            sbf2, _ = mk_alloc(nc, es2)
            gst, gstb = sbf2("gst", [128, 12, 4, 128], F32)
            for a in range(2):
                for k in range(2):
                    K.dma('sp', gst[:, :, a * 2 + k, :], W['lru_w_gate'][0][a, k].rearrange("n c d -> c n d"), gstb,
                          writes=[gstb])
            cp(K, 'dve', gw[:], gst[:], [gstb], [gwb])
            ex, exb = sbf2("sp_x", [128, 12, 2], F32)
            lnv, lnvb = sbf2("sp_ln", [128, 12, 2], F32)
            ser, serb = sbf2("sp_ser", [128, 12, 2], F32)
            msk, mskb = sbf2("sp_m", [128, 12, 2], F32)
            actf(K, ex[:], lam[:], AF.Exp, [lamb], [exb], scale=-1.0)
            actf(K, lnv[:], ex[:], AF.Ln, [exb], [lnvb], bias=1.0)
            ts(K, 'dve', ser[:], ex[:], -0.25, 1.0 / 3.0, ALU.mult, ALU.add, [exb], [serb])
            tt(K, 'dve', ser[:], ser[:], ex[:], ALU.mult, [serb, exb], [serb])
            ts(K, 'dve', ser[:], ser[:], -1.0, 0.5, ALU.mult, ALU.add, [serb], [serb])
            tt(K, 'dve', ser[:], ser[:], ex[:], ALU.mult, [serb, exb], [serb])
            ts(K, 'dve', ser[:], ser[:], -1.0, 1.0, ALU.mult, ALU.add, [serb], [serb])
            tt(K, 'dve', ser[:], ser[:], ex[:], ALU.mult, [serb, exb], [serb])
            K.op('dve', lambda e: e.tensor_single_scalar(out=msk[:], in_=ex[:], scalar=0.05, op=ALU.is_lt),
                 [exb], [mskb])
            tt(K, 'dve', ser[:], ser[:], lnv[:], ALU.subtract, [serb, lnvb], [serb])
            tt(K, 'dve', ser[:], ser[:], msk[:], ALU.mult, [serb, mskb], [serb])
            tt(K, 'dve', cc[:], ser[:], lnv[:], ALU.add, [serb, lnvb], [ccb])
            ts(K, 'dve', cc[:], cc[:], -8.0, None, ALU.mult, None, [ccb], [ccb])
            K.barrier()
            K.flush()
        sets = []
        for i in range(2):
            st_ = {}
            for nm, shp, dt in (("xrt", [128, TC + 4], F32), ("xc", [128, TC], F32), ("xcbf", [128, TC], BF16),
                                ("rt", [128, TC], F32), ("it", [128, TC], F32), ("at", [128, TC], F32),
                                ("ml", [128, TC], F32), ("ut", [128, TC], F32), ("hs", [128, TC], F32),
                                ("hfl", [128, TC], F32), ("yl", [128, TC], BF16), ("ot", [128, TC], BF16)):
                st_[nm] = sbf("%s_%d" % (nm, i), shp, dt)
            sets.append(st_)
        uidx = 0
        car, carb = sbf("car", [128, 1], F32)
        PSG = [psf("p5pg%d" % i, (128, 2048)) for i in range(2)]
        pi = 0
        units = []
        for si, (off, S, poff) in enumerate(C.seqs):
            ntc = S // TC
            xb0 = off + 4 * si + 2
            for n in range(12):
                for a in range(2):
                    order = range(ntc) if a == 0 else range(ntc - 1, -1, -1)
                    for oi, tc in enumerate(order):
                        units.append((off, ntc, xb0, n, a, oi, tc))

        def tiles(k):
            st_ = sets[k % 2]
            return [st_[nm] for nm in ("xrt", "xc", "xcbf", "rt", "it", "at", "ml", "ut", "hs", "hfl", "yl", "ot")]

        def stage_a(u, k):
            off, ntc, xb0, n, a, oi, tc = u
            (xrt, xrtb), (xc, xcb), (xcbf, xcbfb), (rt, rtb), (it, itb), (at, atb), (ml, mlb), (ut, utb) = tiles(k)[0:8]
            tx = xb0 + tc * TC
            K.dma('sp', xrt[:, 0:TC + 3], C.xr[n * 128:(n + 1) * 128, tx - 2:tx + TC + 1], xrtb,
                  writes=[xrtb], dr=[C.xrb])
            ts(K, 'dve', xc[:, :], xrt[:, 0:TC], cw[:, n, 0:1], cb[:, n:n + 1], ALU.mult, ALU.add,
               [xrtb, cwb, cbb], [xcb])
            for j in range(1, 4):
                K.op('dve', lambda e, j=j, n=n, xc=xc, xrt=xrt: e.scalar_tensor_tensor(
                    out=xc[:, :], in0=xrt[:, j:j + TC], scalar=cw[:, n, j:j + 1], in1=xc[:, :],
                    op0=ALU.mult, op1=ALU.add), [xrtb, cwb, xcb], [xcb])
            cp(K, 'act', xcbf[:, :], xc[:, :], [xcb], [xcbfb])
            for kk_, dst, dstb in ((0, rt, rtb), (1, it, itb)):
                ps_, psb_ = PSG[kk_]
                for q in range(4):
                    mm(K, ps_[:, q * 512:(q + 1) * 512], gw[:, n, a * 2 + kk_, :],
                       xcbf[:, q * 512:(q + 1) * 512], True, True, [gwb, xcbfb], [psb_], inc=(q == 3))
                actf(K, dst[:, :], ps_[:, :], AF.Sigmoid, [psb_, gbb], [dstb],
                     bias=gb[:, n, a * 2 + kk_:a * 2 + kk_ + 1])
            actf(K, at[:, :], rt[:, :], AF.Exp, [rtb, ccb], [atb], scale=cc[:, n, a:a + 1])

        def stage_a2(u, k):
            (xrt, xrtb), (xc, xcb), (xcbf, xcbfb), (rt, rtb), (it, itb), (at, atb), (ml, mlb), (ut, utb) = tiles(k)[0:8]
            tt(K, 'pool', ut[:, :], it[:, :], xc[:, :], ALU.mult, [itb, xcb], [utb])
            tt(K, 'pool', ml[:, :], at[:, :], at[:, :], ALU.mult, [atb], [mlb])
            actf(K, ml[:, :], ml[:, :], AF.Sqrt, [mlb], [mlb], scale=-1.0, bias=1.0)
            tt(K, 'pool', ut[:, :], ut[:, :], ml[:, :], ALU.mult, [utb, mlb], [utb])

        def stage_b(u, k):
            off, ntc, xb0, n, a, oi, tc = u
            tl = tiles(k)
            (at, atb), (ml, mlb), (ut, utb), (hs, hsb), (hfl, hflb), (yl, ylb), (ot, otb) = tl[5:12]
            t = off + tc * TC
            if a == 1:
                K.dma('sp', hfl[:], C.hf[n * 128:(n + 1) * 128, t:t + TC], hflb, writes=[hflb], dr=[C.hfb])
                K.dma('sp', yl[:], C.yg[n * 128:(n + 1) * 128, t:t + TC], ylb, writes=[ylb], dr=[C.ygb])
            init = 0.0 if oi == 0 else car[:, 0:1]
            rdc = [atb, utb] + ([] if oi == 0 else [carb])
            if a == 0:
                K.op('dve', lambda e, init=init, hs=hs, at=at, ut=ut: e.tensor_tensor_scan(
                    hs[:, :], at[:, :], ut[:, :], init, ALU.mult, ALU.add), rdc, [hsb])
                if ntc > 1:
                    cp(K, 'act', car[:, 0:1], hs[:, TC - 1:TC], [hsb], [carb])
                K.dma('pool', C.hf[n * 128:(n + 1) * 128, t:t + TC], hs[:, :], hsb, reads=[hsb], dw=[C.hfb])
            else:
                K.op('dve', lambda e, init=init, hs=hs, at=at, ut=ut: e.tensor_tensor_scan(
                    hs[:, ::-1], at[:, ::-1], ut[:, ::-1], init, ALU.mult, ALU.add), rdc, [hsb])
                if ntc > 1:
                    cp(K, 'act', car[:, 0:1], hs[:, 0:1], [hsb], [carb])
                tt(K, 'pool', hs[:, :], hs[:, :], hfl[:, :], ALU.add, [hsb, hflb], [hsb])
                tt(K, 'dve', ot[:, :], hs[:, :], yl[:, :], ALU.mult, [hsb, ylb], [otb])
                K.dma('pool', C.hT[n * 128:(n + 1) * 128, t:t + TC], ot[:, :], otb, reads=[otb], dw=[C.hTb])

        for i, u in enumerate(units):
            stage_a(u, i)
            if i >= 1:
                stage_b(units[i - 1], i - 1)
            stage_a2(u, i)
        stage_b(units[-1], len(units) - 1)
        K.barrier()
        K.flush()


def phase_select(K, nc, C):
    (off0, S0, _) = C.seqs[0]
    nch = S0 // 2048
    with ExitStack() as es:
        sbf, psf = mk_alloc(nc, es)
        sel, selb = sbf("sel", [128, nch], F32)
        K.dma('sp', sel[:], C.sel[:, :], selb, writes=[selb])
        hts = [sbf("sl_h%d" % i, [128, 2048], BF16) for i in range(3)]
        hacc = [sbf("sl_ha%d" % i, [128, 2048], F32) for i in range(2)]
        hob = [sbf("sl_ho%d" % i, [128, 2048], BF16) for i in range(2)]
        li = 0
        for n in range(12):
            a_, ab_ = hacc[n % 2]
            o_, ob_ = hob[n % 2]
            for c in range(nch):
                t_, tb_ = hts[li % 3]
                li += 1
                K.dma('sp', t_[:], C.hT[n * 128:(n + 1) * 128, off0 + c * 2048:off0 + (c + 1) * 2048], tb_,
                      writes=[tb_], dr=[C.hTb])
                if c == 0:
                    ts(K, 'dve', a_[:, :], t_[:, :], sel[:, 0:1], None, ALU.mult, None, [tb_, selb], [ab_])
                else:
                    K.op('dve', lambda e, a_=a_, t_=t_, c=c: e.scalar_tensor_tensor(
                        out=a_[:, :], in0=t_[:, :], scalar=sel[:, c:c + 1], in1=a_[:, :], op0=ALU.mult,
                        op1=ALU.add), [tb_, selb, ab_], [ab_])
            cp(K, 'act', o_[:, :], a_[:, :], [ab_], [ob_])
            K.dma('pool', C.hT2[n * 128:(n + 1) * 128, 0:2048], o_[:, :], ob_, reads=[ob_], dw=[C.hT2b])
        xts = [sbf("sl_x%d" % i, [128, D], F32) for i in range(3)]
        xacc = [sbf("sl_xa%d" % i, [128, D], F32) for i in range(2)]
        for j in range(16):
            a_, ab_ = xacc[j % 2]
            for c in range(nch):
                t_, tb_ = xts[li % 3]
                li += 1
                r0 = off0 + c * 2048 + j * 128
                K.dma('sp', t_[:], C.x1[r0:r0 + 128, :], tb_, writes=[tb_], dr=[C.x1b])
                eng = 'dve' if (li % 2 == 0) else 'pool'
                if c == 0:
                    ts(K, 'dve', a_[:, :], t_[:, :], sel[:, 0:1], None, ALU.mult, None, [tb_, selb], [ab_])
                else:
                    K.op('dve', lambda e, a_=a_, t_=t_, c=c: e.scalar_tensor_tensor(
                        out=a_[:, :], in0=t_[:, :], scalar=sel[:, c:c + 1], in1=a_[:, :], op0=ALU.mult,
                        op1=ALU.add), [tb_, selb, ab_], [ab_])
            K.dma('pool', C.x1c[j * 128:(j + 1) * 128, :], a_[:, :], ab_, reads=[ab_], dw=[C.x1cb])
        cb_ = Buf("selcopy")
        o2 = 2048
        for (off, S, poff) in C.seqs[1:]:
            for n in range(12):
                K.dma('sp', C.hT2[n * 128:(n + 1) * 128, o2:o2 + S], C.hT[n * 128:(n + 1) * 128, off:off + S], cb_,
                      dr=[C.hTb], dw=[C.hT2b])
            for r0 in range(0, S, 512):
                K.dma('sp', C.x1c[o2 + r0:o2 + r0 + 512, :], C.x1[off + r0:off + r0 + 512, :], cb_, dr=[C.x1b],
                      dw=[C.x1cb])
            o2 += S
        K.barrier()
        K.flush()


def _bf(a):
    return np.asarray(a, np.float32).astype(ml_dtypes.bfloat16)


def host_consts(smax):
    c = {}
    c['ident'] = _bf(np.eye(128))

    def rot(n, blocks):
        r = np.zeros((n, n), np.float32)
        for (b0, half) in blocks:
            for m in range(half):
                r[b0 + m + half, b0 + m] = -1.0
                r[b0 + m, b0 + m + half] = 1.0
        return r
    c['r96'] = _bf(rot(128, [(64, 16)]))
    c['r128'] = _bf(rot(128, [(0, 8), (64, 8)]))
    pos = np.arange(smax, dtype=np.float32)

    def tables(half):
        inv = np.power(np.float32(500000.0), -(np.arange(half, dtype=np.float32) / np.float32(half))).astype(np.float32)
        ang = (pos[:, None] * inv[None, :]).astype(np.float32).astype(np.float64)
        return np.cos(ang).T.astype(np.float32), np.sin(ang).T.astype(np.float32)
    co, si = tables(16)
    c96 = np.ones((128, smax), np.float32)
    s96 = np.zeros((128, smax), np.float32)
    c96[64:80] = co
    c96[80:96] = co
    s96[64:80] = si
    s96[80:96] = si
    c['c96'], c['s96'] = c96, s96
    co, si = tables(8)
    c128 = np.ones((128, smax), np.float32)
    s128 = np.zeros((128, smax), np.float32)
    for b0 in (0, 64):
        c128[b0:b0 + 8] = co
        c128[b0 + 8:b0 + 16] = co
        s128[b0:b0 + 8] = si
        s128[b0 + 8:b0 + 16] = si
    c['c128'], c['s128'] = c128, s128
    j = np.arange(128)[:, None]
    i = np.arange(128)[None, :]

    def tab(allowed, nq):
        m = np.where(allowed[:, :nq], 0.0, NEG).astype(np.float32)
        return _bf(np.tile(m, (1, 512 // nq)))
    A = j >= i
    B = j <= i
    c['m_A128'] = tab(A, 128)
    c['m_B128'] = tab(B, 128)
    c['m_A128f'] = tab(A & (j >= 64), 128)
    c['m_B128l'] = tab(B & (j < 64), 128)
    a1f = np.array(c['m_A128'])
    a1f[:, 0:128] = np.array(c['m_A128f'])[:, 0:128]
    c['m_A1f'] = a1f
    b1l = np.array(c['m_B128'])
    b1l[:, 384:512] = np.array(c['m_B128l'])[:, 0:128]
    c['m_B1l'] = b1l
    c['m_A32'] = tab(A, 32)
    c['m_A32u0'] = tab(A & (j >= 64), 32)
    c['m_A32u32'] = tab(A & (j >= 32), 32)
    c['m_A32l'] = tab(A & (j < 96), 32)
    c['m_B32'] = tab(B, 32)
    c['m_ALL'] = _bf(np.full((128, 512), NEG, np.float32))
    return c


WSHAPES = {
    "norm_mix": (2, 1024), "w_in_a": (1, 1024, 5024), "q_norm": (1, 256), "w_uq": (1, 256, 768),
    "kv_norm": (1, 128), "w_ukv": (1, 128, 1024), "w_out_a": (1, 1024, 1024), "w_in_r": (1, 1024, 3072),
    "conv_w": (1, 4, 1536), "conv_b": (1, 1536), "lru_w_gate": (1, 2, 2, 12, 128, 128),
    "lru_b_gate": (1, 2, 2, 1536), "lru_lambda": (1, 2, 1536), "w_out_r": (1, 1536, 1024),
    "norm_ffn": (2, 1024), "w_gu": (2, 1024, 5632), "w_down": (2, 2816, 1024), "norm_final": (1024,),
}


def build(seq_lens, stop_after=99, debug=False, dbg=''):
    nc = bass.Bass("TRN2", target_bir_lowering=False)
    C = Ctx()
    C.dbg = dbg
    seqs = []
    off = 0
    poff = PAD
    for S in seq_lens:
        seqs.append((off, S, poff))
        off += S
        poff += S + 2 * PAD
    T = off
    TP = poff - PAD
    C.seqs = seqs
    smax = max(seq_lens)
    C.x = nc.dram_tensor("x", [T, D], F32, kind="ExternalInput").ap()
    C.W = {k: nc.dram_tensor(k, list(v), F32, kind="ExternalInput").ap() for k, v in WSHAPES.items()}
    hc = host_consts(smax)
    C.cst = {}
    for k, v in hc.items():
        dt = BF16 if v.dtype == ml_dtypes.bfloat16 else F32
        C.cst[k] = nc.dram_tensor("c_" + k, list(v.shape), dt, kind="ExternalInput").ap()
    compact = stop_after >= 5 and len(seq_lens) >= 1 and seq_lens[0] % 2048 == 0
    C.compact = compact
    nch = seq_lens[0] // 2048
    seqs2 = [(0, 2048, 0)]
    o2 = 2048
    for S in seq_lens[1:]:
        seqs2.append((o2, S, 0))
        o2 += S
    T2 = o2
    C.seqs2 = seqs2
    C.y = nc.dram_tensor("y", [T2 if compact else T, D], F32, kind="ExternalOutput").ap()
    C.yb = Buf("y")
    C.sel = nc.dram_tensor("sel", [128, nch], F32, kind="ExternalInput").ap()

    def scratch(name, shape, dt):
        setattr(C, name, nc.dram_tensor("s_" + name, shape, dt).ap())
        setattr(C, name + "b", Buf(name))
    C.xb = Buf("x")
    scratch("qT", [8, 96, T], BF16)
    scratch("kT", [8, 96, T], BF16)
    scratch("vA", [T, 8, 128], BF16)
    scratch("dqT", [3, 8, 64, T], BF16)
    scratch("dkT", [3, 8, 64, TP], BF16)
    scratch("dvA", [TP, 3, 8, 128], BF16)
    scratch("oT", [1024, T], BF16)
    scratch("xm", [T, D], F32)
    scratch("x1", [T, D], F32)
    scratch("yg", [D_RNN, T], BF16)
    scratch("xr", [D_RNN, T + 4 * len(seq_lens)], F32)
    scratch("hf", [D_RNN, T], F32)
    scratch("hT", [D_RNN, T], BF16)
    scratch("xm2", [T, D], F32)
    scratch("hT2", [D_RNN, T2], BF16)
    scratch("x1c", [T2, D], F32)
    with ExitStack() as es:
        K = KB(nc, es)
        phase1(K, nc, C)
        if stop_after >= 2:
            phase2_mla(K, nc, C)
            phase2_dil(K, nc, C)
        if stop_after >= 3:
            phase_proj(K, nc, C, C.oT, C.oTb, 8, C.W['w_out_a'][0], C.x, C.xb, C.xm, C.xmb, "pa")
            phase_ffn(K, nc, C, 0, C.xm, C.xmb, C.x1 if stop_after > 3 else C.y, C.x1b if stop_after > 3 else C.yb,
                      False, "f0")
        if stop_after >= 4:
            phase4(K, nc, C)
            phase5(K, nc, C)
        if stop_after >= 5:
            phase_select(K, nc, C)
            phase_proj(K, nc, C, C.hT2, C.hT2b, 12, C.W['w_out_r'][0], C.x1c, C.x1cb, C.xm2, C.xm2b, "pb",
                       seqs=C.seqs2)
            phase_ffn(K, nc, C, 1, C.xm2, C.xm2b, C.y, C.yb, True, "f1", seqs=C.seqs2)
        K.barrier()
        K.real_flush()
        print("instructions recorded:", K.ninst, "dma sems:", len(K.all_dsem))
        if K.TRACE:
            for t in K.trace:
                print("TR", t)
    return nc, hc


SEQ_LENS = [16384, 2048, 2048]
_CACHE = {}


def kernel(**inputs):
    xp = np.asarray(inputs["x_prompt"], np.float32).reshape(16384, D)
    xs = np.asarray(inputs["x_sample"], np.float32).reshape(16, 2048, D)
    if "nc" not in _CACHE:
        _CACHE["nc"] = build(SEQ_LENS)
    nc, hc = _CACHE["nc"]
    in_maps = []
    for c in range(8):
        m = {"x": np.ascontiguousarray(np.concatenate([xp, xs[2 * c], xs[2 * c + 1]], axis=0))}
        for k in WSHAPES:
            m[k] = np.ascontiguousarray(np.asarray(inputs[k], np.float32))
        for k, v in hc.items():
            m["c_" + k] = v
        sel = np.zeros((128, 8), np.float32)
        sel[:, c] = 1.0
        m["sel"] = sel
        in_maps.append(m)
    res = run_bass_kernel_spmd(nc, in_maps, core_ids=list(range(8)))
    y_prompt = np.concatenate([np.asarray(res.results[c]["y"][0:2048], np.float32) for c in range(8)],
                              axis=0).reshape(1, 16384, D)
    y_sample = np.stack([np.asarray(res.results[c // 2]["y"][2048 * (1 + c % 2):2048 * (2 + c % 2)], np.float32)
                         for c in range(16)], axis=0)
    return (y_prompt, y_sample)
```

```python
import numpy as np
import ml_dtypes
from contextlib import ExitStack
import concourse.bass as bass
import concourse.mybir as mybir
from concourse.bass_utils import run_bass_kernel_spmd

F32 = mybir.dt.float32
BF16 = mybir.dt.bfloat16
AF = mybir.ActivationFunctionType
ALU = mybir.AluOpType
AX = mybir.AxisListType

D = 1024
PAD = 1024
MIX_IN = 5024
D_RNN = 1536
D_FF = 2816
EPS = 1e-6
NEG = -30000.0


class Sem:
    def __init__(s, h):
        s.h = h
        s.cnt = 0


class Buf:
    def __init__(s, name):
        s.name = name
        s.w = {}
        s.r = {}
        s.sem = None
        s.excl = False


class KB:
    ENG = ('pe', 'act', 'dve', 'pool', 'sp')

    def __init__(s, nc, es):
        s.nc = nc
        s.es = es
        s.esem = {e: Sem(es.enter_context(nc.semaphore('s_' + e))) for e in ('pe', 'act', 'dve', 'pool')}
        s.n = {e: 0 for e in s.esem}
        s.waited = {e: {} for e in s.ENG}
        s.prog = {e: [] for e in s.ENG}
        s.free_dsem = []
        s.all_dsem = []
        s.ninst = 0
        s.qhist = {}
        s.QDEPTH = 6
        s.pending = []
        s.pend_reads = set()
        import os
        s.MAXOPS = int(os.environ.get("MAXOPS") or "100000000")
        s.TRACE = bool(os.environ.get("KTRACE", ""))
        s.trace = []

    def dsem(s):
        if s.free_dsem:
            return s.free_dsem.pop()
        sm = Sem(s.es.enter_context(s.nc.semaphore('d%d' % len(s.all_dsem))))
        s.all_dsem.append(sm)
        return sm

    def _wait(s, eng, need):
        for sm, v in need.items():
            if s.waited[eng].get(sm, 0) >= v:
                continue
            s.waited[eng][sm] = v
            s.prog[eng].append(lambda e, h=sm.h, v=v: e.wait_ge(h, v))

    def op(s, eng, fn, reads=(), writes=(), inc=True):
        if s.ninst >= s.MAXOPS:
            return
        s._autoflush(writes)
        need = {}
        own = s.esem[eng]

        def add(d, war):
            for sm, v in d.items():
                if sm is own and eng == 'pe':
                    continue
                if need.get(sm, 0) < v:
                    need[sm] = v
        for b in reads:
            add(b.w, False)
            if b.excl:
                for sm, v in b.r.items():
                    if sm is not own and need.get(sm, 0) < v:
                        need[sm] = v
        for b in writes:
            add(b.w, False)
            add(b.r, True)
        s._wait(eng, need)
        s.ninst += 1
        if s.TRACE:
            import traceback
            fr = traceback.extract_stack(limit=5)
            s.trace.append((s.ninst, eng, [(f.lineno) for f in fr[:-1]]))
        if inc:
            s.n[eng] += 1
            v = s.n[eng]
            s.prog[eng].append(lambda e, fn=fn, h=own.h: fn(e).then_inc(h, 1))
        else:
            v = s.n[eng] + 1
            s.prog[eng].append(lambda e, fn=fn: fn(e))
        for b in reads:
            b.r[own] = max(b.r.get(own, 0), v)
        for b in writes:
            b.w = {own: v}
            b.r = {}

    def dma(s, q, out, in_, sb, reads=(), writes=(), dr=(), dw=()):
        if q == 'pool':
            s.pending.append((out, in_, sb, tuple(reads), tuple(writes), tuple(dr), tuple(dw)))
            for b in reads:
                s.pend_reads.add(id(b))
            for b in dw:
                s.pend_reads.add(id(b))
            return
        s._autoflush(tuple(writes) + tuple(dr))
        s._dma(q, out, in_, sb, reads, writes, dr, dw)

    def _autoflush(s, writes):
        if s.pending:
            for b in writes:
                if id(b) in s.pend_reads:
                    s.flush_stores()
                    return

    def flush_stores(s):
        pend = s.pending
        s.pending = []
        s.pend_reads = set()
        for (out, in_, sb, reads, writes, dr, dw) in pend:
            s._dma('sp', out, in_, sb, reads, writes, dr, dw)

    def _dma(s, q, out, in_, sb, reads=(), writes=(), dr=(), dw=()):
        if s.ninst >= s.MAXOPS:
            return
        if sb.sem is None:
            sb.sem = s.dsem()
        sm = sb.sem
        need = {}

        def add(d, skip_same=False):
            for x, v in d.items():
                if skip_same and x is sm:
                    continue
                if need.get(x, 0) < v:
                    need[x] = v
        for b in reads:
            add(b.w)
        for b in writes:
            add(b.w, True)
            add(b.r)
        for b in dr:
            add(b.w)
        for b in dw:
            add(b.r)
        s._wait(q, need)
        hist = s.qhist.setdefault(q, [])
        if len(hist) >= s.QDEPTH:
            osm, ov = hist[-s.QDEPTH]
            s._wait(q, {osm: 16 * osm.cnt})
        sm.cnt += 1
        v = 16 * sm.cnt
        hist.append((sm, v))
        if len(hist) > 64:
            del hist[:32]
        s.ninst += 1
        s.prog[q].append(lambda e, o=out, i=in_, h=sm.h: e.dma_start(out=o, in_=i).then_inc(h, 16))
        for b in reads:
            b.r[sm] = v
        for b in writes:
            b.w = {sm: v}
            b.r = {}
        for b in dr:
            b.r[sm] = v
        for b in dw:
            b.w[sm] = v

    def barrier(s):
        s.flush_stores()
        toks = {}
        for e, sm in s.esem.items():
            if s.n[e] > 0:
                toks[sm] = s.n[e]
        for sm in s.all_dsem:
            if sm.cnt > 0:
                toks[sm] = 16 * sm.cnt
        for e in s.ENG:
            s._wait(e, dict(toks))
        s.free_dsem = list(s.all_dsem)

    def flush(s):
        return

    def real_flush(s):
        nc = s.nc
        prog = s.prog
        with nc.allow_non_contiguous_dma(reason="small parameter gathers"), nc.Block() as blk:
            @blk.tensor
            def _(e):
                for f in prog['pe']:
                    f(e)

            @blk.scalar
            def _(e):
                for f in prog['act']:
                    f(e)

            @blk.vector
            def _(e):
                for f in prog['dve']:
                    f(e)

            @blk.gpsimd
            def _(e):
                for f in prog['pool']:
                    f(e)

            @blk.sync
            def _(e):
                for f in prog['sp']:
                    f(e)
        s.prog = {e: [] for e in s.ENG}


class Ctx:
    pass


_UID = [0]
NAMES = {}


def mk_alloc(nc, es):
    def sb(name, shape, dt):
        _UID[0] += 1
        NAMES[name] = "t%d_%s" % (_UID[0], name)
        t = es.enter_context(nc.sbuf_tensor("t%d_%s" % (_UID[0], name), shape, dt))
        return t, Buf(name)

    def ps(name, shape=(128, 512), dt=F32):
        _UID[0] += 1
        t = es.enter_context(nc.psum_tensor("p%d_%s" % (_UID[0], name), list(shape), dt))
        b = Buf(name)
        b.excl = True
        return t, b
    return sb, ps


def mm(K, out, lhsT, rhs, start, stop, reads, writes, inc=False):
    K.op('pe', lambda e: e.matmul(out, lhsT=lhsT, rhs=rhs, start=start, stop=stop, skip_group_check=True),
         reads, writes, inc)


def actf(K, out, in_, func, reads, writes, scale=1.0, bias=None, accum=None):
    kw = {}
    if bias is not None:
        kw['bias'] = bias
    if accum is not None:
        kw['accum_out'] = accum
    K.op('act', lambda e: e.activation(out=out, in_=in_, func=func, scale=scale, **kw), reads, writes)


def tt(K, eng, out, in0, in1, op, reads, writes):
    K.op(eng, lambda e: e.tensor_tensor(out=out, in0=in0, in1=in1, op=op), reads, writes)


def ts(K, eng, out, in0, s1, s2, op0, op1, reads, writes):
    if s2 is None:
        K.op(eng, lambda e: e.tensor_scalar(out=out, in0=in0, scalar1=s1, scalar2=None, op0=op0), reads, writes)
    else:
        K.op(eng, lambda e: e.tensor_scalar(out=out, in0=in0, scalar1=s1, scalar2=s2, op0=op0, op1=op1),
             reads, writes)


def cp(K, eng, out, in_, reads, writes):
    if eng == 'act':
        K.op('act', lambda e: e.copy(out=out, in_=in_), reads, writes)
    else:
        K.op(eng, lambda e: e.tensor_copy(out=out, in_=in_), reads, writes)


def load_w(K, sbf, dst, dstb, src, kch, ncols, gain, gainb, tag, colchunk=1024):
    st0, stb0 = sbf(tag + "_st0", [128, colchunk], F32)
    st1, stb1 = sbf(tag + "_st1", [128, colchunk], F32)
    sts = [(st0, stb0), (st1, stb1)]
    i = 0
    engs = ['dve', 'act', 'dve']
    for kc in range(kch):
        for c0 in range(0, ncols, colchunk):
            c1 = min(ncols, c0 + colchunk)
            st, stb = sts[i % 2]
            K.dma('sp', st[:, 0:c1 - c0], src[kc * 128:(kc + 1) * 128, c0:c1], stb, writes=[stb])
            eng = engs[i % 3]
            o = dst(kc, c0, c1)
            if gain is None:
                cp(K, eng, o, st[:, 0:c1 - c0], [stb], [dstb])
            elif eng == 'act':
                K.op('act', lambda e, o=o, a=st[:, 0:c1 - c0], g=gain[:, kc:kc + 1]: e.activation(
                    out=o, in_=a, func=AF.Copy, scale=g), [stb, gainb], [dstb])
            else:
                ts(K, eng, o, st[:, 0:c1 - c0], gain[:, kc:kc + 1], None, ALU.mult, None, [stb, gainb], [dstb])
            i += 1


def rmsnorm_tok(K, x4, xb, ntile, width, h4, hb, tmp, scr, scrb, eng_mul='pool'):
    ss, ssb = tmp['ss']
    ms, msb = tmp['ms']
    sd, sdb = tmp['sd']
    rs, rsb = tmp['rs']
    xbl = xb if isinstance(xb, (list, tuple)) else [xb] * ntile
    for j in range(ntile):
        actf(K, scr[:, 0:width], x4[:, j, :], AF.Square, [xbl[j]], [scrb, ssb], accum=ss[:, j:j + 1])
    ts(K, 'dve', ms[:, 0:ntile], ss[:, 0:ntile], 1.0 / width, EPS, ALU.mult, ALU.add, [ssb], [msb])
    actf(K, sd[:, 0:ntile], ms[:, 0:ntile], AF.Sqrt, [msb], [sdb])
    K.op('dve', lambda e: e.reciprocal(out=rs[:, 0:ntile], in_=sd[:, 0:ntile]), [sdb], [rsb])
    for j in range(ntile):
        if j % 2 == 0:
            K.op('act', lambda e, o=h4[:, j, :], a=x4[:, j, :], g=rs[:, j:j + 1]: e.activation(
                out=o, in_=a, func=AF.Copy, scale=g), [xbl[j], rsb], [hb])
        else:
            ts(K, 'dve', h4[:, j, :], x4[:, j, :], rs[:, j:j + 1], None, ALU.mult, None, [xbl[j], rsb], [hb])


def transpose_blk(K, h4, hb, ntile, nchunk, hT, hTb, psT, psTb, ident, identb):
    for j in range(ntile):
        pt, ptb = psT[j % len(psT)], psTb[j % len(psT)]
        for kc in range(nchunk):
            K.op('pe', lambda e, o=pt[:, kc * 128:(kc + 1) * 128], i=h4[:, j, kc * 128:(kc + 1) * 128]:
                 e.transpose(out=o, in_=i, identity=ident[:]), [hb, identb], [ptb], inc=(kc == nchunk - 1))
        cp(K, 'act' if j % 2 == 0 else 'dve', hT[:, :, j * 128:(j + 1) * 128],
           pt[:, 0:nchunk * 128].rearrange("p (c n) -> p c n", c=nchunk), [ptb], [hTb])


def colnorm(K, psl, pslb, nch, width, sq, sqb, ones, onesb, pss, pssb, rq, rqb, t1, t1b, outT, outb):
    for c in range(nch):
        actf(K, sq[:, c, :], psl[c][:, :], AF.Square, [pslb[c]], [sqb])
    for c in range(nch):
        mm(K, pss[:, :], ones[:, :], sq[:, c, :], c == 0, c == nch - 1, [onesb, sqb], [pssb], inc=(c == nch - 1))
    ts(K, 'dve', t1[:, :], pss[:, :], 1.0 / width, EPS, ALU.mult, ALU.add, [pssb], [t1b])
    actf(K, t1[:, :], t1[:, :], AF.Sqrt, [t1b], [t1b])
    K.op('dve', lambda e: e.reciprocal(out=rq[:, :], in_=t1[:, :]), [t1b], [rqb])
    for c in range(nch):
        tt(K, 'dve', outT[:, c, :], psl[c][:, :], rq[:, :], ALU.mult, [pslb[c], rqb], [outb])


def phase1(K, nc, C):
    SEQS = C.seqs
    with ExitStack() as es:
        sbf, psf = mk_alloc(nc, es)
        W = C.W
        ident, identb = sbf("ident", [128, 128], BF16)
        r96, r96b = sbf("r96", [128, 128], BF16)
        r128, r128b = sbf("r128", [128, 128], BF16)
        ones, onesb = sbf("ones", [128, 128], BF16)
        K.dma('sp', ident[:], C.cst['ident'][:, :], identb, writes=[identb])
        K.dma('sp', r96[:], C.cst['r96'][:, :], r96b, writes=[r96b])
        K.dma('sp', r128[:], C.cst['r128'][:, :], r128b, writes=[r128b])
        K.op('dve', lambda e: e.memset(ones[:], 1.0), [], [onesb])
        zt, ztb = sbf("zt", [128, 3072], BF16)
        K.op('pool', lambda e: e.memset(zt[:], 0.0), [], [ztb])
        for (off, S, poff) in SEQS:
            for p0 in (poff - PAD, poff + S):
                for g in range(3):
                    for h in range(8):
                        K.dma('pool', C.dkT[g, h, :, p0:p0 + PAD], zt[0:64, 0:PAD], ztb, reads=[ztb], dw=[C.dkTb])
                for r0 in range(0, PAD, 128):
                    K.dma('pool', C.dvA[p0 + r0:p0 + r0 + 128].rearrange("p g h e -> p (g h e)"), zt[:, :], ztb,
                          reads=[ztb], dw=[C.dvAb])
        g0, g0b = sbf("g0", [128, 8], F32)
        K.dma('sp', g0[:], W['norm_mix'][0].rearrange("(c p) -> p c", p=128), g0b, writes=[g0b])
        gq, gqb = sbf("gq", [128, 2], F32)
        K.dma('sp', gq[:], W['q_norm'][0].rearrange("(c p) -> p c", p=128), gqb, writes=[gqb])
        gk, gkb = sbf("gk", [128, 1], F32)
        K.dma('sp', gk[:], W['kv_norm'][0].rearrange("(c p) -> p c", p=128), gkb, writes=[gkb])
        w_in, w_inb = sbf("w_in", [128, 8, MIX_IN], BF16)
        wkr, wkrb = sbf("wkr", [128, 8, 128], BF16)
        w_uq, w_uqb = sbf("w_uq", [128, 2, 800], BF16)
        K.op('pool', lambda e: e.memset(w_uq[:], 0.0), [], [w_uqb])
        w_ukv, w_ukvb = sbf("w_ukv", [128, 1024], BF16)
        wvc, wvcb = sbf("wvc", [128, 512], BF16)
        with ExitStack() as es2:
            sbf2, _ = mk_alloc(nc, es2)
            load_w(K, sbf2, lambda kc, c0, c1: w_in[:, kc, c0:c1], w_inb, W['w_in_a'][0], 8, MIX_IN, g0, g0b, "wi")
            load_w(K, sbf2, lambda kc, c0, c1: w_uq[:, kc, c0:c1], w_uqb, W['w_uq'][0], 2, 768, gq, gqb, "wq")
            load_w(K, sbf2, lambda kc, c0, c1: w_ukv[:, c0:c1], w_ukvb, W['w_ukv'][0], 1, 1024, gk, gkb, "wk")
            for h in range(8):
                cp(K, 'dve', wvc[:, h * 64:(h + 1) * 64], w_ukv[:, h * 128 + 64:(h + 1) * 128], [w_ukvb], [wvcb])
            K.op('pool', lambda e: e.memset(wkr[:], 0.0), [], [wkrb])
            for kc in range(8):
                cp(K, 'dve', wkr[:, kc, 64:96], w_in[:, kc, 384:416], [w_inb], [wkrb])
            K.barrier()
            K.flush()
        if getattr(C, 'dbg', '') == 'init':
            return
        xblk, xblkb = sbf("xblk", [128, 4, D], F32)
        h4, h4b = sbf("h4", [128, 4, D], BF16)
        hT, hTb = sbf("hT", [128, 8, 512], BF16)
        scr, scrb = sbf("scr", [128, D], BF16)
        tmp = {k: sbf("n_" + k, [128, 4], F32) for k in ('ss', 'ms', 'sd', 'rs')}
        sq, sqb = sbf("sq", [128, 2, 512], BF16)
        cqn, cqnb = sbf("cqn", [128, 2, 512], BF16)
        ckvn, ckvnb = sbf("ckvn", [128, 1, 512], BF16)
        t1s = [sbf("t1_%d" % i, [128, 512], F32) for i in range(2)]
        t2, t2b = sbf("t2", [128, 512], F32)
        t3, t3b = sbf("t3", [128, 512], F32)
        rq, rqb = sbf("rq", [128, 512], F32)
        qas = [sbf("qa_%d" % i, [128, 512], BF16) for i in range(2)]
        rk = [0]
        qTb_, qTbb = sbf("qTblk", [96, 8, 512], BF16)
        kTb_, kTbb = sbf("kTblk", [96, 8, 512], BF16)
        krf, krfb = sbf("krf", [128, 512], BF16)
        vblk, vblkb = sbf("vblk", [128, 4, 8, 128], BF16)
        dblk, dblkb = sbf("dblk", [128, 12, 512], BF16)
        c96, c96b = sbf("c96", [128, 512], F32)
        s96, s96b = sbf("s96", [128, 512], F32)
        c128, c128b = sbf("c128", [128, 512], F32)
        s128, s128b = sbf("s128", [128, 512], F32)
        K.op('pool', lambda e: e.memset(vblk[:], 1.0), [], [vblkb])
        psT = []
        psTb = []
        for i in range(2):
            a, b = psf("psT%d" % i, (128, 1024), BF16)
            psT.append(a)
            psTb.append(b)
        PS = []
        PSb = []
        for i in range(6):
            a, b = psf("ps%d" % i)
            PS.append(a)
            PSb.append(b)
        rr = [0]

        def nps():
            rr[0] = (rr[0] + 1) % 4
            return PS[2 + rr[0]], PSb[2 + rr[0]]

        def rope_p1(psa, psab, ct, ctb):
            k = rk[0] % 2
            rk[0] += 1
            qa, qab = qas[k]
            t1, t1b = t1s[k]
            cp(K, 'act', qa[:, :], psa[:, :], [psab], [qab])
            tt(K, 'dve', t1[:, :], psa[:, :], ct[:, :], ALU.mult, [psab, ctb], [t1b])
            return k

        def rope_p2(k, nout, rmat, rmatb, st_, stb_, dst, dstw):
            qa, qab = qas[k]
            t1, t1b = t1s[k]
            pr, prb = nps()
            mm(K, pr[:, :], rmat[:, :], qa[:, :], True, True, [rmatb, qab], [prb], inc=True)
            tt(K, 'dve', t2[:, :], pr[:, :], st_[:, :], ALU.mult, [prb, stb_], [t2b])
            tt(K, 'pool', dst, t1[0:nout, :], t2[0:nout, :], ALU.add, [t1b, t2b], dstw)

        def rope_fm(psa, psab, nout, rmat, rmatb, ct, ctb, st_, stb_, dst, dstb, dstw):
            k = rope_p1(psa, psab, ct, ctb)
            rope_p2(k, nout, rmat, rmatb, st_, stb_, dst, dstw)

        for (off, S, poff) in SEQS:
            for blk in range(S // 512):
                t = off + blk * 512
                p = blk * 512
                tp = poff + blk * 512
                K.dma('sp', xblk[:], C.x[t:t + 512, :].rearrange("(j p) d -> p j d", p=128), xblkb, writes=[xblkb])
                K.dma('sp', c96[:], C.cst['c96'][:, p:p + 512], c96b, writes=[c96b])
                K.dma('sp', s96[:], C.cst['s96'][:, p:p + 512], s96b, writes=[s96b])
                K.dma('sp', c128[:], C.cst['c128'][:, p:p + 512], c128b, writes=[c128b])
                K.dma('sp', s128[:], C.cst['s128'][:, p:p + 512], s128b, writes=[s128b])
                rmsnorm_tok(K, xblk, xblkb, 4, D, h4, h4b, tmp, scr, scrb)
                transpose_blk(K, h4, h4b, 4, 8, hT, hTb, psT, psTb, ident, identb)
                if getattr(C, 'dbg', '') == 'norm':
                    K.barrier()
                    K.flush()
                    return
                for c in range(2):
                    for kc in range(8):
                        mm(K, PS[c][:, :], w_in[:, kc, c * 128:(c + 1) * 128], hT[:, kc, :], kc == 0, kc == 7,
                           [w_inb, hTb], [PSb[c]], inc=(kc == 7))
                pss, pssb = nps()
                colnorm(K, [PS[0], PS[1]], [PSb[0], PSb[1]], 2, 256, sq, sqb, ones, onesb, pss, pssb, rq, rqb,
                        t3, t3b, cqn, cqnb)
                pend = None
                for h in range(8):
                    pq, pqb = nps()
                    for c in range(2):
                        mm(K, pq[:, :], w_uq[:, c, h * 96:h * 96 + 128], cqn[:, c, :], c == 0, c == 1,
                           [w_uqb, cqnb], [pqb], inc=(c == 1))
                    k_ = rope_p1(pq, pqb, c96, c96b)
                    if pend is not None:
                        rope_p2(pend[0], 96, r96, r96b, s96, s96b, qTb_[0:96, pend[1], :], [qTbb])
                    pend = (k_, h)
                rope_p2(pend[0], 96, r96, r96b, s96, s96b, qTb_[0:96, pend[1], :], [qTbb])
                import os
                if os.environ.get("DBG3", "") != "nostore":
                    K.dma('pool', C.qT[:, :, t:t + 512].rearrange("h r n -> r h n"), qTb_[:], qTbb, reads=[qTbb],
                          dw=[C.qTb])
                if getattr(C, 'dbg', '') == 'q':
                    K.barrier()
                    K.flush()
                    return
                for kc in range(8):
                    mm(K, PS[0][:, :], w_in[:, kc, 256:384], hT[:, kc, :], kc == 0, kc == 7, [w_inb, hTb], [PSb[0]],
                       inc=(kc == 7))
                pss, pssb = nps()
                colnorm(K, [PS[0]], [PSb[0]], 1, 128, sq, sqb, ones, onesb, pss, pssb, rq, rqb, t3, t3b, ckvn, ckvnb)
                if getattr(C, 'dbg', '') == 'kv1':
                    K.barrier()
                    return
                for kc in range(8):
                    mm(K, PS[1][:, :], wkr[:, kc, :], hT[:, kc, :], kc == 0, kc == 7, [wkrb, hTb], [PSb[1]],
                       inc=(kc == 7))
                rope_fm(PS[1], PSb[1], 96, r96, r96b, c96, c96b, s96, s96b, krf[0:96, :], krfb, [krfb])
                if getattr(C, 'dbg', '') == 'kv2':
                    K.barrier()
                    return
                for h in range(8):
                    cp(K, 'pool', kTb_[64:96, h, :], krf[64:96, :], [krfb], [kTbb])
                    pk, pkb = nps()
                    mm(K, pk[:, :], w_ukv[:, h * 128:(h + 1) * 128], ckvn[:, 0, :], True, True, [w_ukvb, ckvnb],
                       [pkb], inc=True)
                    cp(K, 'act' if h % 2 == 0 else 'dve', kTb_[0:64, h, :], pk[0:64, :], [pkb], [kTbb])
                if getattr(C, 'dbg', '') == 'kv3':
                    K.barrier()
                    return
                K.dma('pool', C.kT[:, :, t:t + 512].rearrange("h r n -> r h n"), kTb_[:], kTbb, reads=[kTbb],
                      dw=[C.kTb])
                if getattr(C, 'dbg', '') == 'kv4':
                    K.barrier()
                    return
                for j in range(4):
                    pv, pvb = nps()
                    mm(K, pv[:, :], ckvn[:, 0, j * 128:(j + 1) * 128], wvc[:, :],
                       True, True, [wvcb, ckvnb], [pvb], inc=True)
                    cp(K, 'act' if j % 2 == 0 else 'dve', vblk[:, j, :, 0:64],
                       pv[:, :].rearrange("p (h d) -> p h d", d=64), [pvb], [vblkb])
                K.dma('pool', C.vA[t:t + 512].rearrange("(j p) h e -> p j h e", p=128), vblk[:], vblkb,
                      reads=[vblkb], dw=[C.vAb])
                if getattr(C, 'dbg', '') == 'kv':
                    K.barrier()
                    K.flush()
                    return
                for qk in range(2):
                    pend = None
                    for c in range(12):
                        col0 = 416 + (qk * 12 + c) * 128
                        pa, pab = nps()
                        for kc in range(8):
                            mm(K, pa[:, :], w_in[:, kc, col0:col0 + 128], hT[:, kc, :], kc == 0, kc == 7,
                               [w_inb, hTb], [pab], inc=(kc == 7))
                        k_ = rope_p1(pa, pab, c128, c128b)
                        if pend is not None:
                            rope_p2(pend[0], 128, r128, r128b, s128, s128b, dblk[:, pend[1], :], [dblkb])
                        pend = (k_, c)
                    rope_p2(pend[0], 128, r128, r128b, s128, s128b, dblk[:, pend[1], :], [dblkb])
                    dst = C.dqT if qk == 0 else C.dkT
                    dstb = C.dqTb if qk == 0 else C.dkTb
                    tt0 = t if qk == 0 else tp
                    for h2 in range(2):
                        dv = dst.rearrange("g (hp h2) d n -> h2 d (g hp) n", h2=2)[h2, :, :, tt0:tt0 + 512]
                        K.dma('pool', dv, dblk[h2 * 64:(h2 + 1) * 64, :, :], dblkb, reads=[dblkb], dw=[dstb])
                for g in range(3):
                    vc0 = 416 + 3072 + g * 512
                    for j in range(4):
                        pv, pvb = nps()
                        for kc in range(8):
                            mm(K, pv[:, :], hT[:, kc, j * 128:(j + 1) * 128], w_in[:, kc, vc0:vc0 + 512], kc == 0,
                               kc == 7, [w_inb, hTb], [pvb], inc=(kc == 7))
                        cp(K, 'act' if j % 2 == 0 else 'dve', vblk[:, j, :, 0:64],
                           pv[:, :].rearrange("p (h d) -> p h d", d=64), [pvb], [vblkb])
                    K.dma('pool', C.dvA[tp:tp + 512, g].rearrange("(j p) h e -> p j h e", p=128), vblk[:], vblkb,
                          reads=[vblkb], dw=[C.dvAb])
        K.barrier()
        K.flush()


def phase2_mla(K, nc, C):
    scale = 96.0 ** -0.5
    with ExitStack() as es:
        sbf, psf = mk_alloc(nc, es)
        SMAX = max(S for (_, S, _) in C.seqs)
        kt, ktb = sbf("kt", [128, SMAX], BF16)
        K.op('pool', lambda e: e.memset(kt[64:128, :], 0.0), [], [ktb])
        va, vab = sbf("va", [128, SMAX // 128, 128], BF16)
        qs = [sbf("q%d" % i, [128, 512], BF16) for i in range(2)]
        for q_, qb__ in qs:
            K.op('pool', lambda e, q_=q_: e.memset(q_[64:128, :], 0.0), [], [qb__])
        pts = [sbf("pt%d" % i, [128, 1024], BF16) for i in range(3)]
        rc, rcb = sbf("rc", [128, 512], F32)
        ob = [sbf("ob%d" % i, [64, 512], BF16) for i in range(2)]
        PSs = [psf("pss%d" % i, (128, 1024)) for i in range(3)]
        PSo = [psf("pso%d" % i) for i in range(2)]
        qi = 0
        si = 0
        for (off, S, poff) in C.seqs:
            nkt = S // 128
            for h in range(8):
                K.dma('sp', kt[0:96, 0:S], C.kT[h, :, off:off + S], ktb, writes=[ktb], dr=[C.kTb])
                K.dma('sp', va[:, 0:nkt, :], C.vA[off:off + S, h, :].rearrange("(m p) e -> p m e", p=128), vab,
                      writes=[vab], dr=[C.vAb])
                for qb in range(S // 512):
                    t = off + qb * 512
                    q, qb_ = qs[qi % 2]
                    po, pob = PSo[qi % 2]
                    o_, ob_ = ob[qi % 2]
                    qi += 1
                    K.dma('sp', q[0:96, :], C.qT[h, :, t:t + 512], qb_, writes=[qb_], dr=[C.qTb])
                    prev = None
                    npair = nkt // 2
                    for mp in range(npair + 1):
                        cur = None
                        if mp < npair:
                            ps_, psb_ = PSs[si % 3]
                            pt, ptb = pts[si % 3]
                            si += 1
                            for hf in range(2):
                                m = 2 * mp + hf
                                mm(K, ps_[:, hf * 512:(hf + 1) * 512], kt[:, m * 128:(m + 1) * 128], q[:, :], True,
                                   True, [ktb, qb_], [psb_], inc=(hf == 1))
                            actf(K, pt[:, :], ps_[:, :], AF.Exp, [psb_], [ptb], scale=scale)
                            cur = (mp, pt, ptb)
                        if prev is not None:
                            pm, ppt, pptb = prev
                            for hf in range(2):
                                m = 2 * pm + hf
                                mm(K, po[:, :], va[:, m, :], ppt[:, hf * 512:(hf + 1) * 512], m == 0, m == nkt - 1,
                                   [vab, pptb], [pob], inc=(m == nkt - 1))
                        prev = cur
                    K.op('dve', lambda e, o=rc[64:128, :], i=po[64:128, :]: e.reciprocal(out=o, in_=i), [pob], [rcb])
                    tt(K, 'dve', o_[0:64, :], po[0:64, :], rc[64:128, :], ALU.mult, [pob, rcb], [ob_])
                    K.dma('pool', C.oT[h * 64:(h + 1) * 64, t:t + 512], o_[:], ob_, reads=[ob_], dw=[C.oTb])
        K.barrier()
        K.flush()


def phase2_dil(K, nc, C):
    scale = 0.125
    DIL = (1, 4, 16)
    with ExitStack() as es:
        sbf, psf = mk_alloc(nc, es)
        mk = {}
        for name in ('A128', 'B128', 'A128f', 'B128l', 'A1f', 'B1l', 'A32', 'A32u0', 'A32u32', 'A32l', 'B32', 'ALL'):
            mk[name] = sbf("m_" + name, [128, 512], BF16)
            K.dma('sp', mk[name][0][:], C.cst['m_' + name][:, :], mk[name][1], writes=[mk[name][1]])
        ident, identb = sbf("ident", [128, 128], BF16)
        K.dma('sp', ident[:], C.cst['ident'][:, :], identb, writes=[identb])
        zl, zlb = sbf("zl", [128, 128], BF16)
        K.op('pool', lambda e: e.memset(zl[:], 0.0), [], [zlb])
        zr, zrb = sbf("zr", [128, 512], BF16)
        K.op('pool', lambda e: e.memset(zr[:], 0.0), [], [zrb])
        spans = [512 + 128 * d for d in DIL]
        ntile = [5, 8, 32]
        nbuf = [2, 2, 1]
        ktl = [[sbf("kt%d_%d" % (g, i), [64, 4, spans[g]], BF16) for i in range(nbuf[g])] for g in range(3)]
        vtl = [[sbf("vt%d_%d" % (g, i), [128, ntile[g], 4, 128], BF16) for i in range(nbuf[g])] for g in range(3)]
        qtl = [sbf("dq%d" % i, [64, 4, 512], BF16) for i in range(6)]
        pts = [sbf("dpt%d" % i, [128, 512], BF16) for i in range(4)]
        sidx = [0]
        oidx = [0]
        acc = [sbf("acc%d" % i, [128, 512], F32) for i in range(8)]
        rc, rcb = sbf("drc", [128, 512], F32)
        obs = [sbf("dob%d" % i, [64, 512], BF16) for i in range(8)]
        PSs = [psf("dpss%d" % i) for i in range(4)]
        PSo = [psf("dpso%d" % i) for i in range(3)]
        ui = [0, 0, 0]
        qi = 0
        si = 0
        oi = 0
        ai = 0
        for (off, S, poff) in C.seqs:
            for qc in range(S // 512):
                t0 = qc * 512
                t = off + t0
                for hh in range(2):
                    accs = [acc[(ai % 2) * 4 + hl] + obs[(ai % 2) * 4 + hl] for hl in range(4)]
                    ai += 1
                    units = []
                    for g in range(3):
                        d = DIL[g]
                        L = S // d
                        k_, kb_ = ktl[g][ui[g] % nbuf[g]]
                        v_, vb_ = vtl[g][ui[g] % nbuf[g]]
                        ui[g] += 1
                        q_, qb_ = qtl[((ai - 1) % 2) * 3 + g]
                        ks = poff + t0 - 64 * d
                        K.dma('sp', k_[:], C.dkT[g, hh * 4:(hh + 1) * 4, :, ks:ks + spans[g]].rearrange(
                            "h d n -> d h n"), kb_, writes=[kb_], dr=[C.dkTb])
                        K.dma('sp', q_[:], C.dqT[g, hh * 4:(hh + 1) * 4, :, t:t + 512].rearrange("h d n -> d h n"),
                              qb_, writes=[qb_], dr=[C.dqTb])
                        if g == 0:
                            src = C.dvA[ks:ks + 640, g, hh * 4:(hh + 1) * 4, :].rearrange(
                                "(m p) h e -> p m h e", p=128)
                            K.dma('sp', v_[:], src, vb_, writes=[vb_], dr=[C.dvAb])
                        else:
                            for mi in range(2):
                                base = ks + mi * 128 * d
                                nk_ = 32 if (g == 2 and mi == 1) else 128
                                src = C.dvA[base:base + nk_ * d, g, hh * 4:(hh + 1) * 4, :].rearrange(
                                    "(p r) h e -> p r h e", r=d)
                                K.dma('sp', v_[0:nk_, mi * d:(mi + 1) * d, :, :], src, vb_, writes=[vb_],
                                      dr=[C.dvAb])
                        for hl in range(4):
                            units.append((g, d, L, hl, k_, kb_, v_, vb_, q_, qb_))
                    def stage1(u):
                        g, d, L, hl, k_, kb_, v_, vb_, q_, qb_ = u
                        nsub, nq = (4, 128) if g < 2 else (16, 32)
                        u0c = t0 // d
                        res = []
                        for ab in range(2):
                            ps_, psb_ = PSs[sidx[0] % 4]
                            pt, ptb = pts[sidx[0] % 4]
                            sidx[0] += 1
                            if g == 2:
                                if ab == 0:
                                    nm = 'A32u0' if u0c == 0 else ('A32u32' if u0c == 32 else (
                                        'A32l' if u0c == L - 32 else 'A32'))
                                else:
                                    nm = 'ALL' if u0c >= L - 64 else 'B32'
                            elif g == 1:
                                if ab == 0:
                                    nm = 'A128f' if u0c == 0 else 'A128'
                                else:
                                    nm = 'B128l' if u0c == L - 128 else 'B128'
                            else:
                                if ab == 0:
                                    nm = 'A1f' if t0 == 0 else 'A128'
                                else:
                                    nm = 'B1l' if t0 == S - 512 else 'B128'
                            mt, mtb = mk[nm]
                            nk_ = 32 if (g == 2 and ab == 1) else 128
                            mm(K, ps_[0:nk_, :], ident[0:nk_, 0:nk_], mt[0:nk_, :], True, False, [identb, mtb],
                               [psb_])
                            for s_ in range(nsub):
                                if g == 0:
                                    kk = k_[:, hl, (s_ + ab) * 128:(s_ + ab + 1) * 128]
                                    qq = q_[:, hl, s_ * 128:(s_ + 1) * 128]
                                else:
                                    kk = k_[:, hl, ab * 128 * d + s_:ab * 128 * d + s_ + (nk_ - 1) * d + 1:d]
                                    qq = q_[:, hl, s_:512:d]
                                mm(K, ps_[0:nk_, s_ * nq:(s_ + 1) * nq], kk, qq, False, s_ == nsub - 1,
                                   [kb_, qb_], [psb_], inc=(s_ == nsub - 1))
                            actf(K, pt[0:nk_, :], ps_[0:nk_, :], AF.Exp, [psb_], [ptb], scale=scale)
                            res.append((pt, ptb, nk_))
                        return res

                    def stage2(u, res):
                        g, d, L, hl, k_, kb_, v_, vb_, q_, qb_ = u
                        h = hh * 4 + hl
                        nsub, nq = (4, 128) if g < 2 else (16, 32)
                        po, pob = PSo[oidx[0] % 3]
                        oidx[0] += 1
                        mm(K, po[:, :], zl[:, :], zr[:, :], True, False, [zlb, zrb], [pob])
                        for ab in range(2):
                            pt, ptb, nk_ = res[ab]
                            for s_ in range(nsub):
                                if g == 0:
                                    vv = v_[:, s_ + ab, hl, :]
                                else:
                                    vv = v_[0:nk_, ab * d + s_, hl, :]
                                last = (ab == 1 and s_ == nsub - 1)
                                mm(K, po[:, s_ * nq:(s_ + 1) * nq], vv, pt[0:nk_, s_ * nq:(s_ + 1) * nq], False, last,
                                   [vb_, ptb], [pob], inc=last)
                        a_, ab_, o_, ob_ = accs[hl]
                        if g == 0:
                            cp(K, 'act', a_[:, :], po[:, :], [pob], [ab_])
                        else:
                            src = po[:, :].rearrange("p (r i) -> p i r", r=d)
                            tt(K, 'dve', a_[:, :].rearrange("p (i r) -> p i r", r=d),
                               a_[:, :].rearrange("p (i r) -> p i r", r=d), src, ALU.add, [pob, ab_], [ab_])
                        if g == 2:
                            K.op('dve', lambda e, o=rc[0:64, :], i=a_[64:128, :]: e.reciprocal(out=o, in_=i),
                                 [ab_], [rcb])
                            tt(K, 'dve', o_[0:64, :], a_[0:64, :], rc[0:64, :], ALU.mult, [ab_, rcb], [ob_])
                            K.dma('pool', C.oT[512 + h * 64:512 + (h + 1) * 64, t:t + 512], o_[:], ob_,
                                  reads=[ob_], dw=[C.oTb])

                    prev = None
                    for u in units:
                        r_ = stage1(u)
                        if prev is not None:
                            stage2(*prev)
                        prev = (u, r_)
                    stage2(*prev)
        K.barrier()
        K.flush()


def phase_proj(K, nc, C, inT, inTb, nk, wsrc, xin, xinb, xout, xoutb, tag, seqs=None):
    seqs = seqs or C.seqs
    with ExitStack() as es:
        sbf, psf = mk_alloc(nc, es)
        w, wb = sbf(tag + "w", [128, nk, D], BF16)
        with ExitStack() as es2:
            sbf2, _ = mk_alloc(nc, es2)
            load_w(K, sbf2, lambda kc, c0, c1: w[:, kc, c0:c1], wb, wsrc, nk, D, None, None, tag + "l")
            K.barrier()
            K.flush()
        its = [sbf(tag + "in%d" % i, [128, nk, 512], BF16) for i in range(2)]
        xs = [sbf(tag + "x%d" % i, [128, 4, D], F32) for i in range(2)]
        PS = [psf(tag + "ps%d" % i) for i in range(4)]
        bi = 0
        pi = 0
        for (off, S, poff) in seqs:
            for blk in range(S // 512):
                t = off + blk * 512
                it, itb = its[bi % 2]
                x_, xb_ = xs[bi % 2]
                bi += 1
                K.dma('sp', it[:], inT[:, t:t + 512].rearrange("(c p) n -> p c n", p=128), itb, writes=[itb],
                      dr=[inTb])
                K.dma('sp', x_[:], xin[t:t + 512, :].rearrange("(j p) d -> p j d", p=128), xb_, writes=[xb_],
                      dr=[xinb])
                for j in range(4):
                    for hf in range(2):
                        ps_, psb_ = PS[pi % 4]
                        pi += 1
                        for kc in range(nk):
                            mm(K, ps_[:, :], it[:, kc, j * 128:(j + 1) * 128], w[:, kc, hf * 512:(hf + 1) * 512],
                               kc == 0, kc == nk - 1, [itb, wb], [psb_], inc=(kc == nk - 1))
                        tt(K, 'dve', x_[:, j, hf * 512:(hf + 1) * 512], x_[:, j, hf * 512:(hf + 1) * 512], ps_[:, :],
                           ALU.add, [xb_, psb_], [xb_])
                K.dma('pool', xout[t:t + 512, :].rearrange("(j p) d -> p j d", p=128), x_[:], xb_, reads=[xb_],
                      dw=[xoutb])
        K.barrier()
        K.flush()


def phase_ffn(K, nc, C, layer, xin, xinb, xout, xoutb, final, tag, seqs=None):
    W = C.W
    seqs = seqs or C.seqs
    with ExitStack() as es:
        sbf, psf = mk_alloc(nc, es)
        ident, identb = sbf(tag + "ident", [128, 128], BF16)
        K.dma('sp', ident[:], C.cst['ident'][:, :], identb, writes=[identb])
        gf, gfb = sbf(tag + "gf", [128, 8], F32)
        K.dma('sp', gf[:], W['norm_ffn'][layer].rearrange("(c p) -> p c", p=128), gfb, writes=[gfb])
        wgu, wgub = sbf(tag + "wgu", [128, 8, 2 * D_FF], BF16)
        wdn, wdnb = sbf(tag + "wdn", [128, 22, D], BF16)
        with ExitStack() as es2:
            sbf2, _ = mk_alloc(nc, es2)
            load_w(K, sbf2, lambda kc, c0, c1: wgu[:, kc, c0:c1], wgub, W['w_gu'][layer], 8, 2 * D_FF, gf, gfb,
                   tag + "lg")
            load_w(K, sbf2, lambda kc, c0, c1: wdn[:, kc, c0:c1], wdnb, W['w_down'][layer], 22, D, None, None,
                   tag + "ld")
            K.barrier()
            K.flush()
        if final:
            gfin, gfinb = sbf(tag + "gfin", [128, D], F32)
            K.dma('sp', gfin[:], W['norm_final'].partition_broadcast(128), gfinb, writes=[gfinb])
        x_, xb_ = sbf(tag + "x", [128, 4, D], F32)
        xbj = [Buf(tag + "x%d" % j) for j in range(4)]
        h4, h4b = sbf(tag + "h4", [128, 4, D], BF16)
        hT, hTb = sbf(tag + "hT", [128, 8, 512], BF16)
        scr, scrb = sbf(tag + "scr", [128, D], BF16)
        tmp = {k: sbf(tag + "n_" + k, [128, 4], F32) for k in ('ss', 'ms', 'sd', 'rs')}
        aT, aTb = sbf(tag + "aT", [128, 22, 512], BF16)
        sg = [sbf(tag + "sg%d" % i, [128, 512], F32) for i in range(2)]
        psT = [psf(tag + "psT%d" % i, (128, 1024), BF16) for i in range(2)]
        PS = [psf(tag + "ps%d" % i) for i in range(6)]
        pi = 0
        gi = 0
        for (off, S, poff) in seqs:
            for blk in range(S // 512):
                t = off + blk * 512
                for j in range(4):
                    K.dma('sp', x_[:, j, :], xin[t + j * 128:t + (j + 1) * 128, :], xbj[j], writes=[xbj[j]],
                          dr=[xinb])
                rmsnorm_tok(K, x_, xbj, 4, D, h4, h4b, tmp, scr, scrb)
                transpose_blk(K, h4, h4b, 4, 8, hT, hTb, [p[0] for p in psT], [p[1] for p in psT], ident, identb)
                for c in range(22):
                    pg, pgb = PS[pi % 6]
                    pu, pub = PS[(pi + 1) % 6]
                    pi += 2
                    for kc in range(8):
                        mm(K, pg[:, :], wgu[:, kc, c * 128:(c + 1) * 128], hT[:, kc, :], kc == 0, kc == 7,
                           [wgub, hTb], [pgb], inc=(kc == 7))
                    for kc in range(8):
                        mm(K, pu[:, :], wgu[:, kc, D_FF + c * 128:D_FF + (c + 1) * 128], hT[:, kc, :], kc == 0,
                           kc == 7, [wgub, hTb], [pub], inc=(kc == 7))
                    s_, sb_ = sg[gi % 2]
                    gi += 1
                    actf(K, s_[:, :], pg[:, :], AF.Silu, [pgb], [sb_])
                    tt(K, 'dve', aT[:, c, :], s_[:, :], pu[:, :], ALU.mult, [sb_, pub], [aTb])
                for j in range(4):
                    for hf in range(2):
                        ps_, psb_ = PS[pi % 6]
                        pi += 1
                        for c in range(22):
                            mm(K, ps_[:, :], aT[:, c, j * 128:(j + 1) * 128], wdn[:, c, hf * 512:(hf + 1) * 512],
                               c == 0, c == 21, [aTb, wdnb], [psb_], inc=(c == 21))
                        tt(K, 'dve', x_[:, j, hf * 512:(hf + 1) * 512], x_[:, j, hf * 512:(hf + 1) * 512], ps_[:, :],
                           ALU.add, [xbj[j], psb_], [xbj[j]])
                    if not final:
                        K.dma('pool', xout[t + j * 128:t + (j + 1) * 128, :], x_[:, j, :], xbj[j], reads=[xbj[j]],
                              dw=[xoutb])
                if final:
                    ss, ssb = tmp['ss']
                    ms, msb = tmp['ms']
                    sd, sdb = tmp['sd']
                    rs, rsb = tmp['rs']
                    for j in range(4):
                        actf(K, scr[:, :], x_[:, j, :], AF.Square, [xbj[j]], [scrb, ssb], accum=ss[:, j:j + 1])
                    ts(K, 'dve', ms[:, 0:4], ss[:, 0:4], 1.0 / D, EPS, ALU.mult, ALU.add, [ssb], [msb])
                    actf(K, sd[:, 0:4], ms[:, 0:4], AF.Sqrt, [msb], [sdb])
                    K.op('dve', lambda e: e.reciprocal(out=rs[:, 0:4], in_=sd[:, 0:4]), [sdb], [rsb])
                    for j in range(4):
                        K.op('dve', lambda e, o=x_[:, j, :], r=rs[:, j:j + 1]: e.scalar_tensor_tensor(
                            out=o, in0=o, scalar=r, in1=gfin[:, :], op0=ALU.mult, op1=ALU.mult),
                            [xbj[j], rsb, gfinb], [xbj[j]])
                        K.dma('pool', xout[t + j * 128:t + (j + 1) * 128, :], x_[:, j, :], xbj[j], reads=[xbj[j]],
                              dw=[xoutb])
        K.barrier()
        K.flush()


def phase4(K, nc, C):
    W = C.W
    with ExitStack() as es:
        sbf, psf = mk_alloc(nc, es)
        ident, identb = sbf("p4ident", [128, 128], BF16)
        K.dma('sp', ident[:], C.cst['ident'][:, :], identb, writes=[identb])
        g1, g1b = sbf("p4g", [128, 8], F32)
        K.dma('sp', g1[:], W['norm_mix'][1].rearrange("(c p) -> p c", p=128), g1b, writes=[g1b])
        w, wb = sbf("p4w", [128, 8, 2 * D_RNN], BF16)
        with ExitStack() as es2:
            sbf2, _ = mk_alloc(nc, es2)
            load_w(K, sbf2, lambda kc, c0, c1: w[:, kc, c0:c1], wb, W['w_in_r'][0], 8, 2 * D_RNN, g1, g1b, "p4l")
            K.barrier()
            K.flush()
        zt, ztb = sbf("p4z", [128, 12, 2], F32)
        K.op('pool', lambda e: e.memset(zt[:], 0.0), [], [ztb])
        for si, (off, S, poff) in enumerate(C.seqs):
            b0 = off + 4 * si
            K.dma('pool', C.xr[:, b0:b0 + 2].rearrange("(c p) n -> p c n", p=128), zt[:], ztb, reads=[ztb],
                  dw=[C.xrb])
            K.dma('pool', C.xr[:, b0 + 2 + S:b0 + 4 + S].rearrange("(c p) n -> p c n", p=128), zt[:], ztb,
                  reads=[ztb], dw=[C.xrb])
        x_, xb_ = sbf("p4x", [128, 4, D], F32)
        h4, h4b = sbf("p4h4", [128, 4, D], BF16)
        hT, hTb = sbf("p4hT", [128, 8, 512], BF16)
        scr, scrb = sbf("p4scr", [128, D], BF16)
        tmp = {k: sbf("p4n_" + k, [128, 4], F32) for k in ('ss', 'ms', 'sd', 'rs')}
        yb, ybb = sbf("p4y", [128, 12, 512], BF16)
        xrb_, xrbb = sbf("p4xr", [128, 12, 512], F32)
        psT = [psf("p4psT%d" % i, (128, 1024), BF16) for i in range(2)]
        PS = [psf("p4ps%d" % i) for i in range(6)]
        pi = 0
        for si, (off, S, poff) in enumerate(C.seqs):
            for blk in range(S // 512):
                t = off + blk * 512
                tx = off + 4 * si + 2 + blk * 512
                K.dma('sp', x_[:], C.x1[t:t + 512, :].rearrange("(j p) d -> p j d", p=128), xb_, writes=[xb_],
                      dr=[C.x1b])
                rmsnorm_tok(K, x_, xb_, 4, D, h4, h4b, tmp, scr, scrb)
                transpose_blk(K, h4, h4b, 4, 8, hT, hTb, [p[0] for p in psT], [p[1] for p in psT], ident, identb)
                for c in range(24):
                    ps_, psb_ = PS[pi % 6]
                    pi += 1
                    for kc in range(8):
                        mm(K, ps_[:, :], w[:, kc, c * 128:(c + 1) * 128], hT[:, kc, :], kc == 0, kc == 7, [wb, hTb],
                           [psb_], inc=(kc == 7))
                    if c < 12:
                        actf(K, yb[:, c, :], ps_[:, :], AF.Gelu_apprx_tanh, [psb_], [ybb])
                    else:
                        cp(K, 'dve', xrb_[:, c - 12, :], ps_[:, :], [psb_], [xrbb])
                K.dma('pool', C.yg[:, t:t + 512].rearrange("(c p) n -> p c n", p=128), yb[:], ybb, reads=[ybb],
                      dw=[C.ygb])
                K.dma('pool', C.xr[:, tx:tx + 512].rearrange("(c p) n -> p c n", p=128), xrb_[:], xrbb,
                      reads=[xrbb], dw=[C.xrb])
        K.barrier()
        K.flush()


def phase5(K, nc, C):
    W = C.W
    TC = 2048
    with ExitStack() as es:
        sbf, psf = mk_alloc(nc, es)
        cw, cwb = sbf("cw", [128, 12, 4], F32)
        cb, cbb = sbf("cb", [128, 12], F32)
        gb, gbb = sbf("gb", [128, 12, 4], F32)
        lam, lamb = sbf("lam", [128, 12, 2], F32)
        cc, ccb = sbf("cc", [128, 12, 2], F32)
        gw, gwb = sbf("gw", [128, 12, 4, 128], BF16)
        for j in range(4):
            K.dma('sp', cw[:, :, j], W['conv_w'][0][j].rearrange("(n p) -> p n", p=128), cwb, writes=[cwb])
        K.dma('sp', cb[:], W['conv_b'][0].rearrange("(n p) -> p n", p=128), cbb, writes=[cbb])
        for a in range(2):
            for k in range(2):
                K.dma('sp', gb[:, :, a * 2 + k], W['lru_b_gate'][0][a, k].rearrange("(n p) -> p n", p=128), gbb,
                      writes=[gbb])
            K.dma('sp', lam[:, :, a], W['lru_lambda'][0][a].rearrange("(n p) -> p n", p=128), lamb, writes=[lamb])
        with ExitStack() as es2:
            sbf2, _ = mk_alloc(nc, es2)
            gst, gstb = sbf2("gst", [128, 12, 4, 128], F32)
            for a in range(2):
                for k in range(2):
                    K.dma('sp', gst[:, :, a * 2 + k, :], W['lru_w_gate'][0][a, k].rearrange("n c d -> c n d"), gstb,
                          writes=[gstb])
            cp(K, 'dve', gw[:], gst[:], [gstb], [gwb])
            ex, exb = sbf2("sp_x", [128, 12, 2], F32)
            lnv, lnvb = sbf2("sp_ln", [128, 12, 2], F32)
            ser, serb = sbf2("sp_ser", [128, 12, 2], F32)
            msk, mskb = sbf2("sp_m", [128, 12, 2], F32)
            actf(K, ex[:], lam[:], AF.Exp, [lamb], [exb], scale=-1.0)
            actf(K, lnv[:], ex[:], AF.Ln, [exb], [lnvb], bias=1.0)
            ts(K, 'dve', ser[:], ex[:], -0.25, 1.0 / 3.0, ALU.mult, ALU.add, [exb], [serb])
            tt(K, 'dve', ser[:], ser[:], ex[:], ALU.mult, [serb, exb], [serb])
            ts(K, 'dve', ser[:], ser[:], -1.0, 0.5, ALU.mult, ALU.add, [serb], [serb])
            tt(K, 'dve', ser[:], ser[:], ex[:], ALU.mult, [serb, exb], [serb])
            ts(K, 'dve', ser[:], ser[:], -1.0, 1.0, ALU.mult, ALU.add, [serb], [serb])
            tt(K, 'dve', ser[:], ser[:], ex[:], ALU.mult, [serb, exb], [serb])
            K.op('dve', lambda e: e.tensor_single_scalar(out=msk[:], in_=ex[:], scalar=0.05, op=ALU.is_lt),
                 [exb], [mskb])
            tt(K, 'dve', ser[:], ser[:], lnv[:], ALU.subtract, [serb, lnvb], [serb])
            tt(K, 'dve', ser[:], ser[:], msk[:], ALU.mult, [serb, mskb], [serb])
            tt(K, 'dve', cc[:], ser[:], lnv[:], ALU.add, [serb, lnvb], [ccb])
            ts(K, 'dve', cc[:], cc[:], -8.0, None, ALU.mult, None, [ccb], [ccb])
            K.barrier()
            K.flush()
        sets = []
        for i in range(2):
            st_ = {}
            for nm, shp, dt in (("xrt", [128, TC + 4], F32), ("xc", [128, TC], F32), ("xcbf", [128, TC], BF16),
                                ("rt", [128, TC], F32), ("it", [128, TC], F32), ("at", [128, TC], F32),
                                ("ml", [128, TC], F32), ("ut", [128, TC], F32), ("hs", [128, TC], F32),
                                ("hfl", [128, TC], F32), ("yl", [128, TC], BF16), ("ot", [128, TC], BF16)):
                st_[nm] = sbf("%s_%d" % (nm, i), shp, dt)
            sets.append(st_)
        uidx = 0
        car, carb = sbf("car", [128, 1], F32)
        PSG = [psf("p5pg%d" % i, (128, 2048)) for i in range(2)]
        pi = 0
        units = []
        for si, (off, S, poff) in enumerate(C.seqs):
            ntc = S // TC
            xb0 = off + 4 * si + 2
            for n in range(12):
                for a in range(2):
                    order = range(ntc) if a == 0 else range(ntc - 1, -1, -1)
                    for oi, tc in enumerate(order):
                        units.append((off, ntc, xb0, n, a, oi, tc))

        def tiles(k):
            st_ = sets[k % 2]
            return [st_[nm] for nm in ("xrt", "xc", "xcbf", "rt", "it", "at", "ml", "ut", "hs", "hfl", "yl", "ot")]

        def stage_a(u, k):
            off, ntc, xb0, n, a, oi, tc = u
            (xrt, xrtb), (xc, xcb), (xcbf, xcbfb), (rt, rtb), (it, itb), (at, atb), (ml, mlb), (ut, utb) = tiles(k)[0:8]
            tx = xb0 + tc * TC
            K.dma('sp', xrt[:, 0:TC + 3], C.xr[n * 128:(n + 1) * 128, tx - 2:tx + TC + 1], xrtb,
                  writes=[xrtb], dr=[C.xrb])
            ts(K, 'dve', xc[:, :], xrt[:, 0:TC], cw[:, n, 0:1], cb[:, n:n + 1], ALU.mult, ALU.add,
               [xrtb, cwb, cbb], [xcb])
            for j in range(1, 4):
                K.op('dve', lambda e, j=j, n=n, xc=xc, xrt=xrt: e.scalar_tensor_tensor(
                    out=xc[:, :], in0=xrt[:, j:j + TC], scalar=cw[:, n, j:j + 1], in1=xc[:, :],
                    op0=ALU.mult, op1=ALU.add), [xrtb, cwb, xcb], [xcb])
            cp(K, 'act', xcbf[:, :], xc[:, :], [xcb], [xcbfb])
            for kk_, dst, dstb in ((0, rt, rtb), (1, it, itb)):
                ps_, psb_ = PSG[kk_]
                for q in range(4):
                    mm(K, ps_[:, q * 512:(q + 1) * 512], gw[:, n, a * 2 + kk_, :],
                       xcbf[:, q * 512:(q + 1) * 512], True, True, [gwb, xcbfb], [psb_], inc=(q == 3))
                actf(K, dst[:, :], ps_[:, :], AF.Sigmoid, [psb_, gbb], [dstb],
                     bias=gb[:, n, a * 2 + kk_:a * 2 + kk_ + 1])
            actf(K, at[:, :], rt[:, :], AF.Exp, [rtb, ccb], [atb], scale=cc[:, n, a:a + 1])

        def stage_a2(u, k):
            (xrt, xrtb), (xc, xcb), (xcbf, xcbfb), (rt, rtb), (it, itb), (at, atb), (ml, mlb), (ut, utb) = tiles(k)[0:8]
            tt(K, 'dve', ml[:, :], at[:, :], at[:, :], ALU.mult, [atb], [mlb])
            actf(K, ml[:, :], ml[:, :], AF.Sqrt, [mlb], [mlb], scale=-1.0, bias=1.0)
            tt(K, 'dve', ut[:, :], it[:, :], xc[:, :], ALU.mult, [itb, xcb], [utb])
            tt(K, 'dve', ut[:, :], ut[:, :], ml[:, :], ALU.mult, [utb, mlb], [utb])

        def stage_b(u, k):
            off, ntc, xb0, n, a, oi, tc = u
            tl = tiles(k)
            (at, atb), (ml, mlb), (ut, utb), (hs, hsb), (hfl, hflb), (yl, ylb), (ot, otb) = tl[5:12]
            t = off + tc * TC
            if a == 1:
                K.dma('sp', hfl[:], C.hf[n * 128:(n + 1) * 128, t:t + TC], hflb, writes=[hflb], dr=[C.hfb])
                K.dma('sp', yl[:], C.yg[n * 128:(n + 1) * 128, t:t + TC], ylb, writes=[ylb], dr=[C.ygb])
            init = 0.0 if oi == 0 else car[:, 0:1]
            rdc = [atb, utb] + ([] if oi == 0 else [carb])
            if a == 0:
                K.op('dve', lambda e, init=init, hs=hs, at=at, ut=ut: e.tensor_tensor_scan(
                    hs[:, :], at[:, :], ut[:, :], init, ALU.mult, ALU.add), rdc, [hsb])
                if ntc > 1:
                    cp(K, 'act', car[:, 0:1], hs[:, TC - 1:TC], [hsb], [carb])
                K.dma('pool', C.hf[n * 128:(n + 1) * 128, t:t + TC], hs[:, :], hsb, reads=[hsb], dw=[C.hfb])
            else:
                K.op('dve', lambda e, init=init, hs=hs, at=at, ut=ut: e.tensor_tensor_scan(
                    hs[:, ::-1], at[:, ::-1], ut[:, ::-1], init, ALU.mult, ALU.add), rdc, [hsb])
                if ntc > 1:
                    cp(K, 'act', car[:, 0:1], hs[:, 0:1], [hsb], [carb])
                tt(K, 'dve', hs[:, :], hs[:, :], hfl[:, :], ALU.add, [hsb, hflb], [hsb])
                tt(K, 'dve', ot[:, :], hs[:, :], yl[:, :], ALU.mult, [hsb, ylb], [otb])
                K.dma('pool', C.hT[n * 128:(n + 1) * 128, t:t + TC], ot[:, :], otb, reads=[otb], dw=[C.hTb])

        for i, u in enumerate(units):
            stage_a(u, i)
            if i >= 1:
                stage_b(units[i - 1], i - 1)
            stage_a2(u, i)
        stage_b(units[-1], len(units) - 1)
        K.barrier()
        K.flush()


def phase_select(K, nc, C):
    (off0, S0, _) = C.seqs[0]
    nch = S0 // 2048
    with ExitStack() as es:
        sbf, psf = mk_alloc(nc, es)
        sel, selb = sbf("sel", [128, nch], F32)
        K.dma('sp', sel[:], C.sel[:, :], selb, writes=[selb])
        hts = [sbf("sl_h%d" % i, [128, 2048], BF16) for i in range(3)]
        hacc = [sbf("sl_ha%d" % i, [128, 2048], F32) for i in range(2)]
        hob = [sbf("sl_ho%d" % i, [128, 2048], BF16) for i in range(2)]
        li = 0
        for n in range(12):
            a_, ab_ = hacc[n % 2]
            o_, ob_ = hob[n % 2]
            for c in range(nch):
                t_, tb_ = hts[li % 3]
                li += 1
                K.dma('sp', t_[:], C.hT[n * 128:(n + 1) * 128, off0 + c * 2048:off0 + (c + 1) * 2048], tb_,
                      writes=[tb_], dr=[C.hTb])
                if c == 0:
                    ts(K, 'dve', a_[:, :], t_[:, :], sel[:, 0:1], None, ALU.mult, None, [tb_, selb], [ab_])
                else:
                    K.op('dve', lambda e, a_=a_, t_=t_, c=c: e.scalar_tensor_tensor(
                        out=a_[:, :], in0=t_[:, :], scalar=sel[:, c:c + 1], in1=a_[:, :], op0=ALU.mult,
                        op1=ALU.add), [tb_, selb, ab_], [ab_])
            cp(K, 'act', o_[:, :], a_[:, :], [ab_], [ob_])
            K.dma('pool', C.hT2[n * 128:(n + 1) * 128, 0:2048], o_[:, :], ob_, reads=[ob_], dw=[C.hT2b])
        xts = [sbf("sl_x%d" % i, [128, D], F32) for i in range(3)]
        xacc = [sbf("sl_xa%d" % i, [128, D], F32) for i in range(2)]
        for j in range(16):
            a_, ab_ = xacc[j % 2]
            for c in range(nch):
                t_, tb_ = xts[li % 3]
                li += 1
                r0 = off0 + c * 2048 + j * 128
                K.dma('sp', t_[:], C.x1[r0:r0 + 128, :], tb_, writes=[tb_], dr=[C.x1b])
                eng = 'dve' if (li % 2 == 0) else 'pool'
                if c == 0:
                    ts(K, 'dve', a_[:, :], t_[:, :], sel[:, 0:1], None, ALU.mult, None, [tb_, selb], [ab_])
                else:
                    K.op('dve', lambda e, a_=a_, t_=t_, c=c: e.scalar_tensor_tensor(
                        out=a_[:, :], in0=t_[:, :], scalar=sel[:, c:c + 1], in1=a_[:, :], op0=ALU.mult,
                        op1=ALU.add), [tb_, selb, ab_], [ab_])
            K.dma('pool', C.x1c[j * 128:(j + 1) * 128, :], a_[:, :], ab_, reads=[ab_], dw=[C.x1cb])
        cb_ = Buf("selcopy")
        o2 = 2048
        for (off, S, poff) in C.seqs[1:]:
            for n in range(12):
                K.dma('sp', C.hT2[n * 128:(n + 1) * 128, o2:o2 + S], C.hT[n * 128:(n + 1) * 128, off:off + S], cb_,
                      dr=[C.hTb], dw=[C.hT2b])
            for r0 in range(0, S, 512):
                K.dma('sp', C.x1c[o2 + r0:o2 + r0 + 512, :], C.x1[off + r0:off + r0 + 512, :], cb_, dr=[C.x1b],
                      dw=[C.x1cb])
            o2 += S
        K.barrier()
        K.flush()


def _bf(a):
    return np.asarray(a, np.float32).astype(ml_dtypes.bfloat16)


def host_consts(smax):
    c = {}
    c['ident'] = _bf(np.eye(128))

    def rot(n, blocks):
        r = np.zeros((n, n), np.float32)
        for (b0, half) in blocks:
            for m in range(half):
                r[b0 + m + half, b0 + m] = -1.0
                r[b0 + m, b0 + m + half] = 1.0
        return r
    c['r96'] = _bf(rot(128, [(64, 16)]))
    c['r128'] = _bf(rot(128, [(0, 8), (64, 8)]))
    pos = np.arange(smax, dtype=np.float32)

    def tables(half):
        inv = np.power(np.float32(500000.0), -(np.arange(half, dtype=np.float32) / np.float32(half))).astype(np.float32)
        ang = (pos[:, None] * inv[None, :]).astype(np.float32).astype(np.float64)
        return np.cos(ang).T.astype(np.float32), np.sin(ang).T.astype(np.float32)
    co, si = tables(16)
    c96 = np.ones((128, smax), np.float32)
    s96 = np.zeros((128, smax), np.float32)
    c96[64:80] = co
    c96[80:96] = co
    s96[64:80] = si
    s96[80:96] = si
    c['c96'], c['s96'] = c96, s96
    co, si = tables(8)
    c128 = np.ones((128, smax), np.float32)
    s128 = np.zeros((128, smax), np.float32)
    for b0 in (0, 64):
        c128[b0:b0 + 8] = co
        c128[b0 + 8:b0 + 16] = co
        s128[b0:b0 + 8] = si
        s128[b0 + 8:b0 + 16] = si
    c['c128'], c['s128'] = c128, s128
    j = np.arange(128)[:, None]
    i = np.arange(128)[None, :]

    def tab(allowed, nq):
        m = np.where(allowed[:, :nq], 0.0, NEG).astype(np.float32)
        return _bf(np.tile(m, (1, 512 // nq)))
    A = j >= i
    B = j <= i
    c['m_A128'] = tab(A, 128)
    c['m_B128'] = tab(B, 128)
    c['m_A128f'] = tab(A & (j >= 64), 128)
    c['m_B128l'] = tab(B & (j < 64), 128)
    a1f = np.array(c['m_A128'])
    a1f[:, 0:128] = np.array(c['m_A128f'])[:, 0:128]
    c['m_A1f'] = a1f
    b1l = np.array(c['m_B128'])
    b1l[:, 384:512] = np.array(c['m_B128l'])[:, 0:128]
    c['m_B1l'] = b1l
    c['m_A32'] = tab(A, 32)
    c['m_A32u0'] = tab(A & (j >= 64), 32)
    c['m_A32u32'] = tab(A & (j >= 32), 32)
    c['m_A32l'] = tab(A & (j < 96), 32)
    c['m_B32'] = tab(B, 32)
    c['m_ALL'] = _bf(np.full((128, 512), NEG, np.float32))
    return c


WSHAPES = {
    "norm_mix": (2, 1024), "w_in_a": (1, 1024, 5024), "q_norm": (1, 256), "w_uq": (1, 256, 768),
    "kv_norm": (1, 128), "w_ukv": (1, 128, 1024), "w_out_a": (1, 1024, 1024), "w_in_r": (1, 1024, 3072),
    "conv_w": (1, 4, 1536), "conv_b": (1, 1536), "lru_w_gate": (1, 2, 2, 12, 128, 128),
    "lru_b_gate": (1, 2, 2, 1536), "lru_lambda": (1, 2, 1536), "w_out_r": (1, 1536, 1024),
    "norm_ffn": (2, 1024), "w_gu": (2, 1024, 5632), "w_down": (2, 2816, 1024), "norm_final": (1024,),
}


def build(seq_lens, stop_after=99, debug=False, dbg=''):
    nc = bass.Bass("TRN2", target_bir_lowering=False)
    C = Ctx()
    C.dbg = dbg
    seqs = []
    off = 0
    poff = PAD
    for S in seq_lens:
        seqs.append((off, S, poff))
        off += S
        poff += S + 2 * PAD
    T = off
    TP = poff - PAD
    C.seqs = seqs
    smax = max(seq_lens)
    C.x = nc.dram_tensor("x", [T, D], F32, kind="ExternalInput").ap()
    C.W = {k: nc.dram_tensor(k, list(v), F32, kind="ExternalInput").ap() for k, v in WSHAPES.items()}
    hc = host_consts(smax)
    C.cst = {}
    for k, v in hc.items():
        dt = BF16 if v.dtype == ml_dtypes.bfloat16 else F32
        C.cst[k] = nc.dram_tensor("c_" + k, list(v.shape), dt, kind="ExternalInput").ap()
    compact = stop_after >= 5 and len(seq_lens) >= 1 and seq_lens[0] % 2048 == 0
    C.compact = compact
    nch = seq_lens[0] // 2048
    seqs2 = [(0, 2048, 0)]
    o2 = 2048
    for S in seq_lens[1:]:
        seqs2.append((o2, S, 0))
        o2 += S
    T2 = o2
    C.seqs2 = seqs2
    C.y = nc.dram_tensor("y", [T2 if compact else T, D], F32, kind="ExternalOutput").ap()
    C.yb = Buf("y")
    C.sel = nc.dram_tensor("sel", [128, nch], F32, kind="ExternalInput").ap()

    def scratch(name, shape, dt):
        setattr(C, name, nc.dram_tensor("s_" + name, shape, dt).ap())
        setattr(C, name + "b", Buf(name))
    C.xb = Buf("x")
    scratch("qT", [8, 96, T], BF16)
    scratch("kT", [8, 96, T], BF16)
    scratch("vA", [T, 8, 128], BF16)
    scratch("dqT", [3, 8, 64, T], BF16)
    scratch("dkT", [3, 8, 64, TP], BF16)
    scratch("dvA", [TP, 3, 8, 128], BF16)
    scratch("oT", [1024, T], BF16)
    scratch("xm", [T, D], F32)
    scratch("x1", [T, D], F32)
    scratch("yg", [D_RNN, T], BF16)
    scratch("xr", [D_RNN, T + 4 * len(seq_lens)], F32)
    scratch("hf", [D_RNN, T], F32)
    scratch("hT", [D_RNN, T], BF16)
    scratch("xm2", [T, D], F32)
    scratch("hT2", [D_RNN, T2], BF16)
    scratch("x1c", [T2, D], F32)
    with ExitStack() as es:
        K = KB(nc, es)
        phase1(K, nc, C)
        if stop_after >= 2:
            phase2_mla(K, nc, C)
            phase2_dil(K, nc, C)
        if stop_after >= 3:
            phase_proj(K, nc, C, C.oT, C.oTb, 8, C.W['w_out_a'][0], C.x, C.xb, C.xm, C.xmb, "pa")
            phase_ffn(K, nc, C, 0, C.xm, C.xmb, C.x1 if stop_after > 3 else C.y, C.x1b if stop_after > 3 else C.yb,
                      False, "f0")
        if stop_after >= 4:
            phase4(K, nc, C)
            phase5(K, nc, C)
        if stop_after >= 5:
            phase_select(K, nc, C)
            phase_proj(K, nc, C, C.hT2, C.hT2b, 12, C.W['w_out_r'][0], C.x1c, C.x1cb, C.xm2, C.xm2b, "pb",
                       seqs=C.seqs2)
            phase_ffn(K, nc, C, 1, C.xm2, C.xm2b, C.y, C.yb, True, "f1", seqs=C.seqs2)
        K.barrier()
        K.real_flush()
        print("instructions recorded:", K.ninst, "dma sems:", len(K.all_dsem))
        if K.TRACE:
            for t in K.trace:
                print("TR", t)
    return nc, hc


SEQ_LENS = [16384, 2048, 2048]
_CACHE = {}


def kernel(**inputs):
    xp = np.asarray(inputs["x_prompt"], np.float32).reshape(16384, D)
    xs = np.asarray(inputs["x_sample"], np.float32).reshape(16, 2048, D)
    if "nc" not in _CACHE:
        _CACHE["nc"] = build(SEQ_LENS)
    nc, hc = _CACHE["nc"]
    in_maps = []
    for c in range(8):
        m = {"x": np.ascontiguousarray(np.concatenate([xp, xs[2 * c], xs[2 * c + 1]], axis=0))}
        for k in WSHAPES:
            m[k] = np.ascontiguousarray(np.asarray(inputs[k], np.float32))
        for k, v in hc.items():
            m["c_" + k] = v
        sel = np.zeros((128, 8), np.float32)
        sel[:, c] = 1.0
        m["sel"] = sel
        in_maps.append(m)
    res = run_bass_kernel_spmd(nc, in_maps, core_ids=list(range(8)))
    y_prompt = np.concatenate([np.asarray(res.results[c]["y"][0:2048], np.float32) for c in range(8)],
                              axis=0).reshape(1, 16384, D)
    y_sample = np.stack([np.asarray(res.results[c // 2]["y"][2048 * (1 + c % 2):2048 * (2 + c % 2)], np.float32)
                         for c in range(16)], axis=0)
    return (y_prompt, y_sample)
```

```python
import numpy as np
import ml_dtypes
from contextlib import ExitStack
import concourse.bass as bass
import concourse.mybir as mybir
from concourse.bass_utils import run_bass_kernel_spmd

F32 = mybir.dt.float32
BF16 = mybir.dt.bfloat16
AF = mybir.ActivationFunctionType
ALU = mybir.AluOpType
AX = mybir.AxisListType

D = 1024
PAD = 1024
MIX_IN = 5024
D_RNN = 1536
D_FF = 2816
EPS = 1e-6
NEG = -30000.0


class Sem:
    def __init__(s, h):
        s.h = h
        s.cnt = 0


class Buf:
    def __init__(s, name):
        s.name = name
        s.w = {}
        s.r = {}
        s.sem = None
        s.excl = False


class KB:
    ENG = ('pe', 'act', 'dve', 'pool', 'sp')

    def __init__(s, nc, es):
        s.nc = nc
        s.es = es
        s.esem = {e: Sem(es.enter_context(nc.semaphore('s_' + e))) for e in ('pe', 'act', 'dve', 'pool')}
        s.n = {e: 0 for e in s.esem}
        s.waited = {e: {} for e in s.ENG}
        s.prog = {e: [] for e in s.ENG}
        s.free_dsem = []
        s.all_dsem = []
        s.ninst = 0
        s.qhist = {}
        s.QDEPTH = 6
        s.pending = []
        s.pend_reads = set()
        import os
        s.MAXOPS = int(os.environ.get("MAXOPS") or "100000000")
        s.TRACE = bool(os.environ.get("KTRACE", ""))
        s.trace = []

    def dsem(s):
        if s.free_dsem:
            return s.free_dsem.pop()
        sm = Sem(s.es.enter_context(s.nc.semaphore('d%d' % len(s.all_dsem))))
        s.all_dsem.append(sm)
        return sm

    def _wait(s, eng, need):
        for sm, v in need.items():
            if s.waited[eng].get(sm, 0) >= v:
                continue
            s.waited[eng][sm] = v
            s.prog[eng].append(lambda e, h=sm.h, v=v: e.wait_ge(h, v))

    def op(s, eng, fn, reads=(), writes=(), inc=True):
        if s.ninst >= s.MAXOPS:
            return
        s._autoflush(writes)
        need = {}
        own = s.esem[eng]

        def add(d, war):
            for sm, v in d.items():
                if sm is own and eng == 'pe':
                    continue
                if need.get(sm, 0) < v:
                    need[sm] = v
        for b in reads:
            add(b.w, False)
            if b.excl:
                for sm, v in b.r.items():
                    if sm is not own and need.get(sm, 0) < v:
                        need[sm] = v
        for b in writes:
            add(b.w, False)
            add(b.r, True)
        s._wait(eng, need)
        s.ninst += 1
        if s.TRACE:
            import traceback
            fr = traceback.extract_stack(limit=5)
            s.trace.append((s.ninst, eng, [(f.lineno) for f in fr[:-1]]))
        if inc:
            s.n[eng] += 1
            v = s.n[eng]
            s.prog[eng].append(lambda e, fn=fn, h=own.h: fn(e).then_inc(h, 1))
        else:
            v = s.n[eng] + 1
            s.prog[eng].append(lambda e, fn=fn: fn(e))
        for b in reads:
            b.r[own] = max(b.r.get(own, 0), v)
        for b in writes:
            b.w = {own: v}
            b.r = {}

    def dma(s, q, out, in_, sb, reads=(), writes=(), dr=(), dw=()):
        if q == 'pool':
            s.pending.append((out, in_, sb, tuple(reads), tuple(writes), tuple(dr), tuple(dw)))
            for b in reads:
                s.pend_reads.add(id(b))
            for b in dw:
                s.pend_reads.add(id(b))
            return
        s._autoflush(tuple(writes) + tuple(dr))
        s._dma(q, out, in_, sb, reads, writes, dr, dw)

    def _autoflush(s, writes):
        if s.pending:
            for b in writes:
                if id(b) in s.pend_reads:
                    s.flush_stores()
                    return

    def flush_stores(s):
        pend = s.pending
        s.pending = []
        s.pend_reads = set()
        for (out, in_, sb, reads, writes, dr, dw) in pend:
            s._dma('sp', out, in_, sb, reads, writes, dr, dw)

    def _dma(s, q, out, in_, sb, reads=(), writes=(), dr=(), dw=()):
        if s.ninst >= s.MAXOPS:
            return
        if sb.sem is None:
            sb.sem = s.dsem()
        sm = sb.sem
        need = {}

        def add(d, skip_same=False):
            for x, v in d.items():
                if skip_same and x is sm:
                    continue
                if need.get(x, 0) < v:
                    need[x] = v
        for b in reads:
            add(b.w)
        for b in writes:
            add(b.w, True)
            add(b.r)
        for b in dr:
            add(b.w)
        for b in dw:
            add(b.r)
        s._wait(q, need)
        hist = s.qhist.setdefault(q, [])
        if len(hist) >= s.QDEPTH:
            osm, ov = hist[-s.QDEPTH]
            s._wait(q, {osm: 16 * osm.cnt})
        sm.cnt += 1
        v = 16 * sm.cnt
        hist.append((sm, v))
        if len(hist) > 64:
            del hist[:32]
        s.ninst += 1
        s.prog[q].append(lambda e, o=out, i=in_, h=sm.h: e.dma_start(out=o, in_=i).then_inc(h, 16))
        for b in reads:
            b.r[sm] = v
        for b in writes:
            b.w = {sm: v}
            b.r = {}
        for b in dr:
            b.r[sm] = v
        for b in dw:
            b.w[sm] = v

    def barrier(s):
        s.flush_stores()
        toks = {}
        for e, sm in s.esem.items():
            if s.n[e] > 0:
                toks[sm] = s.n[e]
        for sm in s.all_dsem:
            if sm.cnt > 0:
                toks[sm] = 16 * sm.cnt
        for e in s.ENG:
            s._wait(e, dict(toks))
        s.free_dsem = list(s.all_dsem)

    def flush(s):
        return

    def real_flush(s):
        nc = s.nc
        prog = s.prog
        with nc.allow_non_contiguous_dma(reason="small parameter gathers"), nc.Block() as blk:
            @blk.tensor
            def _(e):
                for f in prog['pe']:
                    f(e)

            @blk.scalar
            def _(e):
                for f in prog['act']:
                    f(e)

            @blk.vector
            def _(e):
                for f in prog['dve']:
                    f(e)

            @blk.gpsimd
            def _(e):
                for f in prog['pool']:
                    f(e)

            @blk.sync
            def _(e):
                for f in prog['sp']:
                    f(e)
        s.prog = {e: [] for e in s.ENG}


class Ctx:
    pass


_UID = [0]
NAMES = {}


def mk_alloc(nc, es):
    def sb(name, shape, dt):
        _UID[0] += 1
        NAMES[name] = "t%d_%s" % (_UID[0], name)
        t = es.enter_context(nc.sbuf_tensor("t%d_%s" % (_UID[0], name), shape, dt))
        return t, Buf(name)

    def ps(name, shape=(128, 512), dt=F32):
        _UID[0] += 1
        t = es.enter_context(nc.psum_tensor("p%d_%s" % (_UID[0], name), list(shape), dt))
        b = Buf(name)
        b.excl = True
        return t, b
    return sb, ps


def mm(K, out, lhsT, rhs, start, stop, reads, writes, inc=False):
    K.op('pe', lambda e: e.matmul(out, lhsT=lhsT, rhs=rhs, start=start, stop=stop, skip_group_check=True),
         reads, writes, inc)


def actf(K, out, in_, func, reads, writes, scale=1.0, bias=None, accum=None):
    kw = {}
    if bias is not None:
        kw['bias'] = bias
    if accum is not None:
        kw['accum_out'] = accum
    K.op('act', lambda e: e.activation(out=out, in_=in_, func=func, scale=scale, **kw), reads, writes)


def tt(K, eng, out, in0, in1, op, reads, writes):
    K.op(eng, lambda e: e.tensor_tensor(out=out, in0=in0, in1=in1, op=op), reads, writes)


def ts(K, eng, out, in0, s1, s2, op0, op1, reads, writes):
    if s2 is None:
        K.op(eng, lambda e: e.tensor_scalar(out=out, in0=in0, scalar1=s1, scalar2=None, op0=op0), reads, writes)
    else:
        K.op(eng, lambda e: e.tensor_scalar(out=out, in0=in0, scalar1=s1, scalar2=s2, op0=op0, op1=op1),
             reads, writes)


def cp(K, eng, out, in_, reads, writes):
    if eng == 'act':
        K.op('act', lambda e: e.copy(out=out, in_=in_), reads, writes)
    else:
        K.op(eng, lambda e: e.tensor_copy(out=out, in_=in_), reads, writes)


def load_w(K, sbf, dst, dstb, src, kch, ncols, gain, gainb, tag, colchunk=1024):
    st0, stb0 = sbf(tag + "_st0", [128, colchunk], F32)
    st1, stb1 = sbf(tag + "_st1", [128, colchunk], F32)
    sts = [(st0, stb0), (st1, stb1)]
    i = 0
    engs = ['dve', 'act', 'dve']
    for kc in range(kch):
        for c0 in range(0, ncols, colchunk):
            c1 = min(ncols, c0 + colchunk)
            st, stb = sts[i % 2]
            K.dma('sp', st[:, 0:c1 - c0], src[kc * 128:(kc + 1) * 128, c0:c1], stb, writes=[stb])
            eng = engs[i % 3]
            o = dst(kc, c0, c1)
            if gain is None:
                cp(K, eng, o, st[:, 0:c1 - c0], [stb], [dstb])
            elif eng == 'act':
                K.op('act', lambda e, o=o, a=st[:, 0:c1 - c0], g=gain[:, kc:kc + 1]: e.activation(
                    out=o, in_=a, func=AF.Copy, scale=g), [stb, gainb], [dstb])
            else:
                ts(K, eng, o, st[:, 0:c1 - c0], gain[:, kc:kc + 1], None, ALU.mult, None, [stb, gainb], [dstb])
            i += 1


def rmsnorm_tok(K, x4, xb, ntile, width, h4, hb, tmp, scr, scrb, eng_mul='pool'):
    ss, ssb = tmp['ss']
    ms, msb = tmp['ms']
    sd, sdb = tmp['sd']
    rs, rsb = tmp['rs']
    xbl = xb if isinstance(xb, (list, tuple)) else [xb] * ntile
    for j in range(ntile):
        actf(K, scr[:, 0:width], x4[:, j, :], AF.Square, [xbl[j]], [scrb, ssb], accum=ss[:, j:j + 1])
    ts(K, 'dve', ms[:, 0:ntile], ss[:, 0:ntile], 1.0 / width, EPS, ALU.mult, ALU.add, [ssb], [msb])
    actf(K, sd[:, 0:ntile], ms[:, 0:ntile], AF.Sqrt, [msb], [sdb])
    K.op('dve', lambda e: e.reciprocal(out=rs[:, 0:ntile], in_=sd[:, 0:ntile]), [sdb], [rsb])
    for j in range(ntile):
        if j % 2 == 0:
            K.op('act', lambda e, o=h4[:, j, :], a=x4[:, j, :], g=rs[:, j:j + 1]: e.activation(
                out=o, in_=a, func=AF.Copy, scale=g), [xbl[j], rsb], [hb])
        else:
            ts(K, 'dve', h4[:, j, :], x4[:, j, :], rs[:, j:j + 1], None, ALU.mult, None, [xbl[j], rsb], [hb])


def transpose_blk(K, h4, hb, ntile, nchunk, hT, hTb, psT, psTb, ident, identb):
    for j in range(ntile):
        pt, ptb = psT[j % len(psT)], psTb[j % len(psT)]
        for kc in range(nchunk):
            K.op('pe', lambda e, o=pt[:, kc * 128:(kc + 1) * 128], i=h4[:, j, kc * 128:(kc + 1) * 128]:
                 e.transpose(out=o, in_=i, identity=ident[:]), [hb, identb], [ptb], inc=(kc == nchunk - 1))
        cp(K, 'act' if j % 2 == 0 else 'dve', hT[:, :, j * 128:(j + 1) * 128],
           pt[:, 0:nchunk * 128].rearrange("p (c n) -> p c n", c=nchunk), [ptb], [hTb])


def colnorm(K, psl, pslb, nch, width, sq, sqb, ones, onesb, pss, pssb, rq, rqb, t1, t1b, outT, outb):
    for c in range(nch):
        actf(K, sq[:, c, :], psl[c][:, :], AF.Square, [pslb[c]], [sqb])
    for c in range(nch):
        mm(K, pss[:, :], ones[:, :], sq[:, c, :], c == 0, c == nch - 1, [onesb, sqb], [pssb], inc=(c == nch - 1))
    ts(K, 'dve', t1[:, :], pss[:, :], 1.0 / width, EPS, ALU.mult, ALU.add, [pssb], [t1b])
    actf(K, t1[:, :], t1[:, :], AF.Sqrt, [t1b], [t1b])
    K.op('dve', lambda e: e.reciprocal(out=rq[:, :], in_=t1[:, :]), [t1b], [rqb])
    for c in range(nch):
        tt(K, 'dve', outT[:, c, :], psl[c][:, :], rq[:, :], ALU.mult, [pslb[c], rqb], [outb])


def phase1(K, nc, C):
    SEQS = C.seqs
    with ExitStack() as es:
        sbf, psf = mk_alloc(nc, es)
        W = C.W
        ident, identb = sbf("ident", [128, 128], BF16)
        r96, r96b = sbf("r96", [128, 128], BF16)
        r128, r128b = sbf("r128", [128, 128], BF16)
        ones, onesb = sbf("ones", [128, 128], BF16)
        K.dma('sp', ident[:], C.cst['ident'][:, :], identb, writes=[identb])
        K.dma('sp', r96[:], C.cst['r96'][:, :], r96b, writes=[r96b])
        K.dma('sp', r128[:], C.cst['r128'][:, :], r128b, writes=[r128b])
        K.op('dve', lambda e: e.memset(ones[:], 1.0), [], [onesb])
        zt, ztb = sbf("zt", [128, 3072], BF16)
        K.op('pool', lambda e: e.memset(zt[:], 0.0), [], [ztb])
        for (off, S, poff) in SEQS:
            for p0 in (poff - PAD, poff + S):
                for g in range(3):
                    for h in range(8):
                        K.dma('pool', C.dkT[g, h, :, p0:p0 + PAD], zt[0:64, 0:PAD], ztb, reads=[ztb], dw=[C.dkTb])
                for r0 in range(0, PAD, 128):
                    K.dma('pool', C.dvA[p0 + r0:p0 + r0 + 128].rearrange("p g h e -> p (g h e)"), zt[:, :], ztb,
                          reads=[ztb], dw=[C.dvAb])
        g0, g0b = sbf("g0", [128, 8], F32)
        K.dma('sp', g0[:], W['norm_mix'][0].rearrange("(c p) -> p c", p=128), g0b, writes=[g0b])
        gq, gqb = sbf("gq", [128, 2], F32)
        K.dma('sp', gq[:], W['q_norm'][0].rearrange("(c p) -> p c", p=128), gqb, writes=[gqb])
        gk, gkb = sbf("gk", [128, 1], F32)
        K.dma('sp', gk[:], W['kv_norm'][0].rearrange("(c p) -> p c", p=128), gkb, writes=[gkb])
        w_in, w_inb = sbf("w_in", [128, 8, MIX_IN], BF16)
        wkr, wkrb = sbf("wkr", [128, 8, 128], BF16)
        w_uq, w_uqb = sbf("w_uq", [128, 2, 800], BF16)
        K.op('pool', lambda e: e.memset(w_uq[:], 0.0), [], [w_uqb])
        w_ukv, w_ukvb = sbf("w_ukv", [128, 1024], BF16)
        wvc, wvcb = sbf("wvc", [128, 512], BF16)
        with ExitStack() as es2:
            sbf2, _ = mk_alloc(nc, es2)
            load_w(K, sbf2, lambda kc, c0, c1: w_in[:, kc, c0:c1], w_inb, W['w_in_a'][0], 8, MIX_IN, g0, g0b, "wi")
            load_w(K, sbf2, lambda kc, c0, c1: w_uq[:, kc, c0:c1], w_uqb, W['w_uq'][0], 2, 768, gq, gqb, "wq")
            load_w(K, sbf2, lambda kc, c0, c1: w_ukv[:, c0:c1], w_ukvb, W['w_ukv'][0], 1, 1024, gk, gkb, "wk")
            for h in range(8):
                cp(K, 'dve', wvc[:, h * 64:(h + 1) * 64], w_ukv[:, h * 128 + 64:(h + 1) * 128], [w_ukvb], [wvcb])
            K.op('pool', lambda e: e.memset(wkr[:], 0.0), [], [wkrb])
            for kc in range(8):
                cp(K, 'dve', wkr[:, kc, 64:96], w_in[:, kc, 384:416], [w_inb], [wkrb])
            K.barrier()
            K.flush()
        if getattr(C, 'dbg', '') == 'init':
            return
        xblk, xblkb = sbf("xblk", [128, 4, D], F32)
        h4, h4b = sbf("h4", [128, 4, D], BF16)
        hT, hTb = sbf("hT", [128, 8, 512], BF16)
        scr, scrb = sbf("scr", [128, D], BF16)
        tmp = {k: sbf("n_" + k, [128, 4], F32) for k in ('ss', 'ms', 'sd', 'rs')}
        sq, sqb = sbf("sq", [128, 2, 512], BF16)
        cqn, cqnb = sbf("cqn", [128, 2, 512], BF16)
        ckvn, ckvnb = sbf("ckvn", [128, 1, 512], BF16)
        t1s = [sbf("t1_%d" % i, [128, 512], F32) for i in range(2)]
        t2, t2b = sbf("t2", [128, 512], F32)
        t3, t3b = sbf("t3", [128, 512], F32)
        rq, rqb = sbf("rq", [128, 512], F32)
        qas = [sbf("qa_%d" % i, [128, 512], BF16) for i in range(2)]
        rk = [0]
        qTb_, qTbb = sbf("qTblk", [96, 8, 512], BF16)
        kTb_, kTbb = sbf("kTblk", [96, 8, 512], BF16)
        krf, krfb = sbf("krf", [128, 512], BF16)
        vblk, vblkb = sbf("vblk", [128, 4, 8, 128], BF16)
        dblk, dblkb = sbf("dblk", [128, 12, 512], BF16)
        c96, c96b = sbf("c96", [128, 512], F32)
        s96, s96b = sbf("s96", [128, 512], F32)
        c128, c128b = sbf("c128", [128, 512], F32)
        s128, s128b = sbf("s128", [128, 512], F32)
        K.op('pool', lambda e: e.memset(vblk[:], 1.0), [], [vblkb])
        psT = []
        psTb = []
        for i in range(2):
            a, b = psf("psT%d" % i, (128, 1024), BF16)
            psT.append(a)
            psTb.append(b)
        PS = []
        PSb = []
        for i in range(6):
            a, b = psf("ps%d" % i)
            PS.append(a)
            PSb.append(b)
        rr = [0]

        def nps():
            rr[0] = (rr[0] + 1) % 4
            return PS[2 + rr[0]], PSb[2 + rr[0]]

        def rope_p1(psa, psab, ct, ctb):
            k = rk[0] % 2
            rk[0] += 1
            qa, qab = qas[k]
            t1, t1b = t1s[k]
            cp(K, 'act', qa[:, :], psa[:, :], [psab], [qab])
            tt(K, 'dve', t1[:, :], psa[:, :], ct[:, :], ALU.mult, [psab, ctb], [t1b])
            return k

        def rope_p2(k, nout, rmat, rmatb, st_, stb_, dst, dstw):
            qa, qab = qas[k]
            t1, t1b = t1s[k]
            pr, prb = nps()
            mm(K, pr[:, :], rmat[:, :], qa[:, :], True, True, [rmatb, qab], [prb], inc=True)
            tt(K, 'dve', t2[:, :], pr[:, :], st_[:, :], ALU.mult, [prb, stb_], [t2b])
            tt(K, 'dve', dst, t1[0:nout, :], t2[0:nout, :], ALU.add, [t1b, t2b], dstw)

        def rope_fm(psa, psab, nout, rmat, rmatb, ct, ctb, st_, stb_, dst, dstb, dstw):
            k = rope_p1(psa, psab, ct, ctb)
            rope_p2(k, nout, rmat, rmatb, st_, stb_, dst, dstw)

        for (off, S, poff) in SEQS:
            for blk in range(S // 512):
                t = off + blk * 512
                p = blk * 512
                tp = poff + blk * 512
                K.dma('sp', xblk[:], C.x[t:t + 512, :].rearrange("(j p) d -> p j d", p=128), xblkb, writes=[xblkb])
                K.dma('sp', c96[:], C.cst['c96'][:, p:p + 512], c96b, writes=[c96b])
                K.dma('sp', s96[:], C.cst['s96'][:, p:p + 512], s96b, writes=[s96b])
                K.dma('sp', c128[:], C.cst['c128'][:, p:p + 512], c128b, writes=[c128b])
                K.dma('sp', s128[:], C.cst['s128'][:, p:p + 512], s128b, writes=[s128b])
                rmsnorm_tok(K, xblk, xblkb, 4, D, h4, h4b, tmp, scr, scrb)
                transpose_blk(K, h4, h4b, 4, 8, hT, hTb, psT, psTb, ident, identb)
                if getattr(C, 'dbg', '') == 'norm':
                    K.barrier()
                    K.flush()
                    return
                for c in range(2):
                    for kc in range(8):
                        mm(K, PS[c][:, :], w_in[:, kc, c * 128:(c + 1) * 128], hT[:, kc, :], kc == 0, kc == 7,
                           [w_inb, hTb], [PSb[c]], inc=(kc == 7))
                pss, pssb = nps()
                colnorm(K, [PS[0], PS[1]], [PSb[0], PSb[1]], 2, 256, sq, sqb, ones, onesb, pss, pssb, rq, rqb,
                        t3, t3b, cqn, cqnb)
                pend = None
                for h in range(8):
                    pq, pqb = nps()
                    for c in range(2):
                        mm(K, pq[:, :], w_uq[:, c, h * 96:h * 96 + 128], cqn[:, c, :], c == 0, c == 1,
                           [w_uqb, cqnb], [pqb], inc=(c == 1))
                    k_ = rope_p1(pq, pqb, c96, c96b)
                    if pend is not None:
                        rope_p2(pend[0], 96, r96, r96b, s96, s96b, qTb_[0:96, pend[1], :], [qTbb])
                    pend = (k_, h)
                rope_p2(pend[0], 96, r96, r96b, s96, s96b, qTb_[0:96, pend[1], :], [qTbb])
                import os
                if os.environ.get("DBG3", "") != "nostore":
                    K.dma('pool', C.qT[:, :, t:t + 512].rearrange("h r n -> r h n"), qTb_[:], qTbb, reads=[qTbb],
                          dw=[C.qTb])
                if getattr(C, 'dbg', '') == 'q':
                    K.barrier()
                    K.flush()
                    return
                for kc in range(8):
                    mm(K, PS[0][:, :], w_in[:, kc, 256:384], hT[:, kc, :], kc == 0, kc == 7, [w_inb, hTb], [PSb[0]],
                       inc=(kc == 7))
                pss, pssb = nps()
                colnorm(K, [PS[0]], [PSb[0]], 1, 128, sq, sqb, ones, onesb, pss, pssb, rq, rqb, t3, t3b, ckvn, ckvnb)
                if getattr(C, 'dbg', '') == 'kv1':
                    K.barrier()
                    return
                for kc in range(8):
                    mm(K, PS[1][:, :], wkr[:, kc, :], hT[:, kc, :], kc == 0, kc == 7, [wkrb, hTb], [PSb[1]],
                       inc=(kc == 7))
                rope_fm(PS[1], PSb[1], 96, r96, r96b, c96, c96b, s96, s96b, krf[0:96, :], krfb, [krfb])
                if getattr(C, 'dbg', '') == 'kv2':
                    K.barrier()
                    return
                for h in range(8):
                    cp(K, 'act' if h % 2 == 1 else 'dve', kTb_[64:96, h, :], krf[64:96, :], [krfb], [kTbb])
                    pk, pkb = nps()
                    mm(K, pk[:, :], w_ukv[:, h * 128:(h + 1) * 128], ckvn[:, 0, :], True, True, [w_ukvb, ckvnb],
                       [pkb], inc=True)
                    cp(K, 'act' if h % 2 == 0 else 'dve', kTb_[0:64, h, :], pk[0:64, :], [pkb], [kTbb])
                if getattr(C, 'dbg', '') == 'kv3':
                    K.barrier()
                    return
                K.dma('pool', C.kT[:, :, t:t + 512].rearrange("h r n -> r h n"), kTb_[:], kTbb, reads=[kTbb],
                      dw=[C.kTb])
                if getattr(C, 'dbg', '') == 'kv4':
                    K.barrier()
                    return
                for j in range(4):
                    pv, pvb = nps()
                    mm(K, pv[:, :], ckvn[:, 0, j * 128:(j + 1) * 128], wvc[:, :],
                       True, True, [wvcb, ckvnb], [pvb], inc=True)
                    cp(K, 'act' if j % 2 == 0 else 'dve', vblk[:, j, :, 0:64],
                       pv[:, :].rearrange("p (h d) -> p h d", d=64), [pvb], [vblkb])
                K.dma('pool', C.vA[t:t + 512].rearrange("(j p) h e -> p j h e", p=128), vblk[:], vblkb,
                      reads=[vblkb], dw=[C.vAb])
                if getattr(C, 'dbg', '') == 'kv':
                    K.barrier()
                    K.flush()
                    return
                for qk in range(2):
                    pend = None
                    for c in range(12):
                        col0 = 416 + (qk * 12 + c) * 128
                        pa, pab = nps()
                        for kc in range(8):
                            mm(K, pa[:, :], w_in[:, kc, col0:col0 + 128], hT[:, kc, :], kc == 0, kc == 7,
                               [w_inb, hTb], [pab], inc=(kc == 7))
                        k_ = rope_p1(pa, pab, c128, c128b)
                        if pend is not None:
                            rope_p2(pend[0], 128, r128, r128b, s128, s128b, dblk[:, pend[1], :], [dblkb])
                        pend = (k_, c)
                    rope_p2(pend[0], 128, r128, r128b, s128, s128b, dblk[:, pend[1], :], [dblkb])
                    dst = C.dqT if qk == 0 else C.dkT
                    dstb = C.dqTb if qk == 0 else C.dkTb
                    tt0 = t if qk == 0 else tp
                    for h2 in range(2):
                        dv = dst.rearrange("g (hp h2) d n -> h2 d (g hp) n", h2=2)[h2, :, :, tt0:tt0 + 512]
                        K.dma('pool', dv, dblk[h2 * 64:(h2 + 1) * 64, :, :], dblkb, reads=[dblkb], dw=[dstb])
                for g in range(3):
                    vc0 = 416 + 3072 + g * 512
                    for j in range(4):
                        pv, pvb = nps()
                        for kc in range(8):
                            mm(K, pv[:, :], hT[:, kc, j * 128:(j + 1) * 128], w_in[:, kc, vc0:vc0 + 512], kc == 0,
                               kc == 7, [w_inb, hTb], [pvb], inc=(kc == 7))
                        cp(K, 'act' if j % 2 == 0 else 'dve', vblk[:, j, :, 0:64],
                           pv[:, :].rearrange("p (h d) -> p h d", d=64), [pvb], [vblkb])
                    K.dma('pool', C.dvA[tp:tp + 512, g].rearrange("(j p) h e -> p j h e", p=128), vblk[:], vblkb,
                          reads=[vblkb], dw=[C.dvAb])
        K.barrier()
        K.flush()


def phase2_mla(K, nc, C):
    scale = 96.0 ** -0.5
    with ExitStack() as es:
        sbf, psf = mk_alloc(nc, es)
        SMAX = max(S for (_, S, _) in C.seqs)
        kt, ktb = sbf("kt", [128, SMAX], BF16)
        K.op('pool', lambda e: e.memset(kt[64:128, :], 0.0), [], [ktb])
        va, vab = sbf("va", [128, SMAX // 128, 128], BF16)
        qs = [sbf("q%d" % i, [128, 512], BF16) for i in range(2)]
        for q_, qb__ in qs:
            K.op('pool', lambda e, q_=q_: e.memset(q_[64:128, :], 0.0), [], [qb__])
        pts = [sbf("pt%d" % i, [128, 1024], BF16) for i in range(3)]
        rc, rcb = sbf("rc", [128, 512], F32)
        ob = [sbf("ob%d" % i, [64, 512], BF16) for i in range(2)]
        PSs = [psf("pss%d" % i, (128, 1024)) for i in range(3)]
        PSo = [psf("pso%d" % i) for i in range(2)]
        qi = 0
        si = 0
        for (off, S, poff) in C.seqs:
            nkt = S // 128
            for h in range(8):
                K.dma('sp', kt[0:96, 0:S], C.kT[h, :, off:off + S], ktb, writes=[ktb], dr=[C.kTb])
                K.dma('sp', va[:, 0:nkt, :], C.vA[off:off + S, h, :].rearrange("(m p) e -> p m e", p=128), vab,
                      writes=[vab], dr=[C.vAb])
                for qb in range(S // 512):
                    t = off + qb * 512
                    q, qb_ = qs[qi % 2]
                    po, pob = PSo[qi % 2]
                    o_, ob_ = ob[qi % 2]
                    qi += 1
                    K.dma('sp', q[0:96, :], C.qT[h, :, t:t + 512], qb_, writes=[qb_], dr=[C.qTb])
                    prev = None
                    npair = nkt // 2
                    for mp in range(npair + 1):
                        cur = None
                        if mp < npair:
                            ps_, psb_ = PSs[si % 3]
                            pt, ptb = pts[si % 3]
                            si += 1
                            for hf in range(2):
                                m = 2 * mp + hf
                                mm(K, ps_[:, hf * 512:(hf + 1) * 512], kt[:, m * 128:(m + 1) * 128], q[:, :], True,
                                   True, [ktb, qb_], [psb_], inc=(hf == 1))
                            actf(K, pt[:, :], ps_[:, :], AF.Exp, [psb_], [ptb], scale=scale)
                            cur = (mp, pt, ptb)
                        if prev is not None:
                            pm, ppt, pptb = prev
                            for hf in range(2):
                                m = 2 * pm + hf
                                mm(K, po[:, :], va[:, m, :], ppt[:, hf * 512:(hf + 1) * 512], m == 0, m == nkt - 1,
                                   [vab, pptb], [pob], inc=(m == nkt - 1))
                        prev = cur
                    K.op('dve', lambda e, o=rc[64:128, :], i=po[64:128, :]: e.reciprocal(out=o, in_=i), [pob], [rcb])
                    tt(K, 'dve', o_[0:64, :], po[0:64, :], rc[64:128, :], ALU.mult, [pob, rcb], [ob_])
                    K.dma('pool', C.oT[h * 64:(h + 1) * 64, t:t + 512], o_[:], ob_, reads=[ob_], dw=[C.oTb])
        K.barrier()
        K.flush()


def phase2_dil(K, nc, C):
    scale = 0.125
    DIL = (1, 4, 16)
    with ExitStack() as es:
        sbf, psf = mk_alloc(nc, es)
        mk = {}
        for name in ('A128', 'B128', 'A128f', 'B128l', 'A1f', 'B1l', 'A32', 'A32u0', 'A32u32', 'A32l', 'B32', 'ALL'):
            mk[name] = sbf("m_" + name, [128, 512], BF16)
            K.dma('sp', mk[name][0][:], C.cst['m_' + name][:, :], mk[name][1], writes=[mk[name][1]])
        ident, identb = sbf("ident", [128, 128], BF16)
        K.dma('sp', ident[:], C.cst['ident'][:, :], identb, writes=[identb])
        zl, zlb = sbf("zl", [128, 128], BF16)
        K.op('pool', lambda e: e.memset(zl[:], 0.0), [], [zlb])
        zr, zrb = sbf("zr", [128, 512], BF16)
        K.op('pool', lambda e: e.memset(zr[:], 0.0), [], [zrb])
        spans = [512 + 128 * d for d in DIL]
        ntile = [5, 8, 32]
        nbuf = [2, 2, 1]
        ktl = [[sbf("kt%d_%d" % (g, i), [64, 4, spans[g]], BF16) for i in range(nbuf[g])] for g in range(3)]
        vtl = [[sbf("vt%d_%d" % (g, i), [128, ntile[g], 4, 128], BF16) for i in range(nbuf[g])] for g in range(3)]
        qtl = [sbf("dq%d" % i, [64, 4, 512], BF16) for i in range(6)]
        pts = [sbf("dpt%d" % i, [128, 512], BF16) for i in range(4)]
        sidx = [0]
        oidx = [0]
        acc = [sbf("acc%d" % i, [128, 512], F32) for i in range(8)]
        rc, rcb = sbf("drc", [128, 512], F32)
        obs = [sbf("dob%d" % i, [64, 512], BF16) for i in range(8)]
        PSs = [psf("dpss%d" % i) for i in range(4)]
        PSo = [psf("dpso%d" % i) for i in range(3)]
        ui = [0, 0, 0]
        qi = 0
        si = 0
        oi = 0
        ai = 0
        for (off, S, poff) in C.seqs:
            for qc in range(S // 512):
                t0 = qc * 512
                t = off + t0
                for hh in range(2):
                    accs = [acc[(ai % 2) * 4 + hl] + obs[(ai % 2) * 4 + hl] for hl in range(4)]
                    ai += 1
                    units = []
                    for g in range(3):
                        d = DIL[g]
                        L = S // d
                        k_, kb_ = ktl[g][ui[g] % nbuf[g]]
                        v_, vb_ = vtl[g][ui[g] % nbuf[g]]
                        ui[g] += 1
                        q_, qb_ = qtl[((ai - 1) % 2) * 3 + g]
                        ks = poff + t0 - 64 * d
                        K.dma('sp', k_[:], C.dkT[g, hh * 4:(hh + 1) * 4, :, ks:ks + spans[g]].rearrange(
                            "h d n -> d h n"), kb_, writes=[kb_], dr=[C.dkTb])
                        K.dma('sp', q_[:], C.dqT[g, hh * 4:(hh + 1) * 4, :, t:t + 512].rearrange("h d n -> d h n"),
                              qb_, writes=[qb_], dr=[C.dqTb])
                        if g == 0:
                            src = C.dvA[ks:ks + 640, g, hh * 4:(hh + 1) * 4, :].rearrange(
                                "(m p) h e -> p m h e", p=128)
                            K.dma('sp', v_[:], src, vb_, writes=[vb_], dr=[C.dvAb])
                        else:
                            for mi in range(2):
                                base = ks + mi * 128 * d
                                nk_ = 32 if (g == 2 and mi == 1) else 128
                                src = C.dvA[base:base + nk_ * d, g, hh * 4:(hh + 1) * 4, :].rearrange(
                                    "(p r) h e -> p r h e", r=d)
                                K.dma('sp', v_[0:nk_, mi * d:(mi + 1) * d, :, :], src, vb_, writes=[vb_],
                                      dr=[C.dvAb])
                        for hl in range(4):
                            units.append((g, d, L, hl, k_, kb_, v_, vb_, q_, qb_))
                    def stage1(u):
                        g, d, L, hl, k_, kb_, v_, vb_, q_, qb_ = u
                        nsub, nq = (4, 128) if g < 2 else (16, 32)
                        u0c = t0 // d
                        res = []
                        for ab in range(2):
                            ps_, psb_ = PSs[sidx[0] % 4]
                            pt, ptb = pts[sidx[0] % 4]
                            sidx[0] += 1
                            if g == 2:
                                if ab == 0:
                                    nm = 'A32u0' if u0c == 0 else ('A32u32' if u0c == 32 else (
                                        'A32l' if u0c == L - 32 else 'A32'))
                                else:
                                    nm = 'ALL' if u0c >= L - 64 else 'B32'
                            elif g == 1:
                                if ab == 0:
                                    nm = 'A128f' if u0c == 0 else 'A128'
                                else:
                                    nm = 'B128l' if u0c == L - 128 else 'B128'
                            else:
                                if ab == 0:
                                    nm = 'A1f' if t0 == 0 else 'A128'
                                else:
                                    nm = 'B1l' if t0 == S - 512 else 'B128'
                            mt, mtb = mk[nm]
                            nk_ = 32 if (g == 2 and ab == 1) else 128
                            mm(K, ps_[0:nk_, :], ident[0:nk_, 0:nk_], mt[0:nk_, :], True, False, [identb, mtb],
                               [psb_])
                            for s_ in range(nsub):
                                if g == 0:
                                    kk = k_[:, hl, (s_ + ab) * 128:(s_ + ab + 1) * 128]
                                    qq = q_[:, hl, s_ * 128:(s_ + 1) * 128]
                                else:
                                    kk = k_[:, hl, ab * 128 * d + s_:ab * 128 * d + s_ + (nk_ - 1) * d + 1:d]
                                    qq = q_[:, hl, s_:512:d]
                                mm(K, ps_[0:nk_, s_ * nq:(s_ + 1) * nq], kk, qq, False, s_ == nsub - 1,
                                   [kb_, qb_], [psb_], inc=(s_ == nsub - 1))
                            actf(K, pt[0:nk_, :], ps_[0:nk_, :], AF.Exp, [psb_], [ptb], scale=scale)
                            res.append((pt, ptb, nk_))
                        return res

                    def stage2(u, res):
                        g, d, L, hl, k_, kb_, v_, vb_, q_, qb_ = u
                        h = hh * 4 + hl
                        nsub, nq = (4, 128) if g < 2 else (16, 32)
                        po, pob = PSo[oidx[0] % 3]
                        oidx[0] += 1
                        mm(K, po[:, :], zl[:, :], zr[:, :], True, False, [zlb, zrb], [pob])
                        for ab in range(2):
                            pt, ptb, nk_ = res[ab]
                            for s_ in range(nsub):
                                if g == 0:
                                    vv = v_[:, s_ + ab, hl, :]
                                else:
                                    vv = v_[0:nk_, ab * d + s_, hl, :]
                                last = (ab == 1 and s_ == nsub - 1)
                                mm(K, po[:, s_ * nq:(s_ + 1) * nq], vv, pt[0:nk_, s_ * nq:(s_ + 1) * nq], False, last,
                                   [vb_, ptb], [pob], inc=last)
                        a_, ab_, o_, ob_ = accs[hl]
                        if g == 0:
                            cp(K, 'act', a_[:, :], po[:, :], [pob], [ab_])
                        else:
                            src = po[:, :].rearrange("p (r i) -> p i r", r=d)
                            tt(K, 'dve', a_[:, :].rearrange("p (i r) -> p i r", r=d),
                               a_[:, :].rearrange("p (i r) -> p i r", r=d), src, ALU.add, [pob, ab_], [ab_])
                        if g == 2:
                            K.op('dve', lambda e, o=rc[0:64, :], i=a_[64:128, :]: e.reciprocal(out=o, in_=i),
                                 [ab_], [rcb])
                            tt(K, 'dve', o_[0:64, :], a_[0:64, :], rc[0:64, :], ALU.mult, [ab_, rcb], [ob_])
                            K.dma('pool', C.oT[512 + h * 64:512 + (h + 1) * 64, t:t + 512], o_[:], ob_,
                                  reads=[ob_], dw=[C.oTb])

                    prev = None
                    for u in units:
                        r_ = stage1(u)
                        if prev is not None:
                            stage2(*prev)
                        prev = (u, r_)
                    stage2(*prev)
        K.barrier()
        K.flush()


def phase_proj(K, nc, C, inT, inTb, nk, wsrc, xin, xinb, xout, xoutb, tag, seqs=None):
    seqs = seqs or C.seqs
    with ExitStack() as es:
        sbf, psf = mk_alloc(nc, es)
        w, wb = sbf(tag + "w", [128, nk, D], BF16)
        with ExitStack() as es2:
            sbf2, _ = mk_alloc(nc, es2)
            load_w(K, sbf2, lambda kc, c0, c1: w[:, kc, c0:c1], wb, wsrc, nk, D, None, None, tag + "l")
            K.barrier()
            K.flush()
        its = [sbf(tag + "in%d" % i, [128, nk, 512], BF16) for i in range(2)]
        xs = [sbf(tag + "x%d" % i, [128, 4, D], F32) for i in range(2)]
        PS = [psf(tag + "ps%d" % i) for i in range(4)]
        bi = 0
        pi = 0
        for (off, S, poff) in seqs:
            for blk in range(S // 512):
                t = off + blk * 512
                it, itb = its[bi % 2]
                x_, xb_ = xs[bi % 2]
                bi += 1
                K.dma('sp', it[:], inT[:, t:t + 512].rearrange("(c p) n -> p c n", p=128), itb, writes=[itb],
                      dr=[inTb])
                K.dma('sp', x_[:], xin[t:t + 512, :].rearrange("(j p) d -> p j d", p=128), xb_, writes=[xb_],
                      dr=[xinb])
                for j in range(4):
                    for hf in range(2):
                        ps_, psb_ = PS[pi % 4]
                        pi += 1
                        for kc in range(nk):
                            mm(K, ps_[:, :], it[:, kc, j * 128:(j + 1) * 128], w[:, kc, hf * 512:(hf + 1) * 512],
                               kc == 0, kc == nk - 1, [itb, wb], [psb_], inc=(kc == nk - 1))
                        tt(K, 'dve', x_[:, j, hf * 512:(hf + 1) * 512], x_[:, j, hf * 512:(hf + 1) * 512], ps_[:, :],
                           ALU.add, [xb_, psb_], [xb_])
                K.dma('pool', xout[t:t + 512, :].rearrange("(j p) d -> p j d", p=128), x_[:], xb_, reads=[xb_],
                      dw=[xoutb])
        K.barrier()
        K.flush()


def phase_ffn(K, nc, C, layer, xin, xinb, xout, xoutb, final, tag, seqs=None):
    W = C.W
    seqs = seqs or C.seqs
    with ExitStack() as es:
        sbf, psf = mk_alloc(nc, es)
        ident, identb = sbf(tag + "ident", [128, 128], BF16)
        K.dma('sp', ident[:], C.cst['ident'][:, :], identb, writes=[identb])
        gf, gfb = sbf(tag + "gf", [128, 8], F32)
        K.dma('sp', gf[:], W['norm_ffn'][layer].rearrange("(c p) -> p c", p=128), gfb, writes=[gfb])
        wgu, wgub = sbf(tag + "wgu", [128, 8, 2 * D_FF], BF16)
        wdn, wdnb = sbf(tag + "wdn", [128, 22, D], BF16)
        with ExitStack() as es2:
            sbf2, _ = mk_alloc(nc, es2)
            load_w(K, sbf2, lambda kc, c0, c1: wgu[:, kc, c0:c1], wgub, W['w_gu'][layer], 8, 2 * D_FF, gf, gfb,
                   tag + "lg")
            load_w(K, sbf2, lambda kc, c0, c1: wdn[:, kc, c0:c1], wdnb, W['w_down'][layer], 22, D, None, None,
                   tag + "ld")
            K.barrier()
            K.flush()
        if final:
            gfin, gfinb = sbf(tag + "gfin", [128, D], F32)
            K.dma('sp', gfin[:], W['norm_final'].partition_broadcast(128), gfinb, writes=[gfinb])
        x_, xb_ = sbf(tag + "x", [128, 4, D], F32)
        xbj = [Buf(tag + "x%d" % j) for j in range(4)]
        h4, h4b = sbf(tag + "h4", [128, 4, D], BF16)
        hT, hTb = sbf(tag + "hT", [128, 8, 512], BF16)
        scr, scrb = sbf(tag + "scr", [128, D], BF16)
        tmp = {k: sbf(tag + "n_" + k, [128, 4], F32) for k in ('ss', 'ms', 'sd', 'rs')}
        aT, aTb = sbf(tag + "aT", [128, 22, 512], BF16)
        sg = [sbf(tag + "sg%d" % i, [128, 512], F32) for i in range(2)]
        psT = [psf(tag + "psT%d" % i, (128, 1024), BF16) for i in range(2)]
        PS = [psf(tag + "ps%d" % i) for i in range(6)]
        pi = 0
        gi = 0
        for (off, S, poff) in seqs:
            for blk in range(S // 512):
                t = off + blk * 512
                for j in range(4):
                    K.dma('sp', x_[:, j, :], xin[t + j * 128:t + (j + 1) * 128, :], xbj[j], writes=[xbj[j]],
                          dr=[xinb])
                rmsnorm_tok(K, x_, xbj, 4, D, h4, h4b, tmp, scr, scrb)
                transpose_blk(K, h4, h4b, 4, 8, hT, hTb, [p[0] for p in psT], [p[1] for p in psT], ident, identb)
                for c in range(22):
                    pg, pgb = PS[pi % 6]
                    pu, pub = PS[(pi + 1) % 6]
                    pi += 2
                    for kc in range(8):
                        mm(K, pg[:, :], wgu[:, kc, c * 128:(c + 1) * 128], hT[:, kc, :], kc == 0, kc == 7,
                           [wgub, hTb], [pgb], inc=(kc == 7))
                    for kc in range(8):
                        mm(K, pu[:, :], wgu[:, kc, D_FF + c * 128:D_FF + (c + 1) * 128], hT[:, kc, :], kc == 0,
                           kc == 7, [wgub, hTb], [pub], inc=(kc == 7))
                    s_, sb_ = sg[gi % 2]
                    gi += 1
                    actf(K, s_[:, :], pg[:, :], AF.Silu, [pgb], [sb_])
                    tt(K, 'dve', aT[:, c, :], s_[:, :], pu[:, :], ALU.mult, [sb_, pub], [aTb])
                for j in range(4):
                    for hf in range(2):
                        ps_, psb_ = PS[pi % 6]
                        pi += 1
                        for c in range(22):
                            mm(K, ps_[:, :], aT[:, c, j * 128:(j + 1) * 128], wdn[:, c, hf * 512:(hf + 1) * 512],
                               c == 0, c == 21, [aTb, wdnb], [psb_], inc=(c == 21))
                        tt(K, 'dve', x_[:, j, hf * 512:(hf + 1) * 512], x_[:, j, hf * 512:(hf + 1) * 512], ps_[:, :],
                           ALU.add, [xbj[j], psb_], [xbj[j]])
                    if not final:
                        K.dma('pool', xout[t + j * 128:t + (j + 1) * 128, :], x_[:, j, :], xbj[j], reads=[xbj[j]],
                              dw=[xoutb])
                if final:
                    ss, ssb = tmp['ss']
                    ms, msb = tmp['ms']
                    sd, sdb = tmp['sd']
                    rs, rsb = tmp['rs']
                    for j in range(4):
                        actf(K, scr[:, :], x_[:, j, :], AF.Square, [xbj[j]], [scrb, ssb], accum=ss[:, j:j + 1])
                    ts(K, 'dve', ms[:, 0:4], ss[:, 0:4], 1.0 / D, EPS, ALU.mult, ALU.add, [ssb], [msb])
                    actf(K, sd[:, 0:4], ms[:, 0:4], AF.Sqrt, [msb], [sdb])
                    K.op('dve', lambda e: e.reciprocal(out=rs[:, 0:4], in_=sd[:, 0:4]), [sdb], [rsb])
                    for j in range(4):
                        K.op('dve', lambda e, o=x_[:, j, :], r=rs[:, j:j + 1]: e.scalar_tensor_tensor(
                            out=o, in0=o, scalar=r, in1=gfin[:, :], op0=ALU.mult, op1=ALU.mult),
                            [xbj[j], rsb, gfinb], [xbj[j]])
                        K.dma('pool', xout[t + j * 128:t + (j + 1) * 128, :], x_[:, j, :], xbj[j], reads=[xbj[j]],
                              dw=[xoutb])
        K.barrier()
        K.flush()


def phase4(K, nc, C):
    W = C.W
    with ExitStack() as es:
        sbf, psf = mk_alloc(nc, es)
        ident, identb = sbf("p4ident", [128, 128], BF16)
        K.dma('sp', ident[:], C.cst['ident'][:, :], identb, writes=[identb])
        g1, g1b = sbf("p4g", [128, 8], F32)
        K.dma('sp', g1[:], W['norm_mix'][1].rearrange("(c p) -> p c", p=128), g1b, writes=[g1b])
        w, wb = sbf("p4w", [128, 8, 2 * D_RNN], BF16)
        with ExitStack() as es2:
            sbf2, _ = mk_alloc(nc, es2)
            load_w(K, sbf2, lambda kc, c0, c1: w[:, kc, c0:c1], wb, W['w_in_r'][0], 8, 2 * D_RNN, g1, g1b, "p4l")
            K.barrier()
            K.flush()
        zt, ztb = sbf("p4z", [128, 12, 2], F32)
        K.op('pool', lambda e: e.memset(zt[:], 0.0), [], [ztb])
        for si, (off, S, poff) in enumerate(C.seqs):
            b0 = off + 4 * si
            K.dma('pool', C.xr[:, b0:b0 + 2].rearrange("(c p) n -> p c n", p=128), zt[:], ztb, reads=[ztb],
                  dw=[C.xrb])
            K.dma('pool', C.xr[:, b0 + 2 + S:b0 + 4 + S].rearrange("(c p) n -> p c n", p=128), zt[:], ztb,
                  reads=[ztb], dw=[C.xrb])
        x_, xb_ = sbf("p4x", [128, 4, D], F32)
        h4, h4b = sbf("p4h4", [128, 4, D], BF16)
        hT, hTb = sbf("p4hT", [128, 8, 512], BF16)
        scr, scrb = sbf("p4scr", [128, D], BF16)
        tmp = {k: sbf("p4n_" + k, [128, 4], F32) for k in ('ss', 'ms', 'sd', 'rs')}
        yb, ybb = sbf("p4y", [128, 12, 512], BF16)
        xrb_, xrbb = sbf("p4xr", [128, 12, 512], F32)
        psT = [psf("p4psT%d" % i, (128, 1024), BF16) for i in range(2)]
        PS = [psf("p4ps%d" % i) for i in range(6)]
        pi = 0
        for si, (off, S, poff) in enumerate(C.seqs):
            for blk in range(S // 512):
                t = off + blk * 512
                tx = off + 4 * si + 2 + blk * 512
                K.dma('sp', x_[:], C.x1[t:t + 512, :].rearrange("(j p) d -> p j d", p=128), xb_, writes=[xb_],
                      dr=[C.x1b])
                rmsnorm_tok(K, x_, xb_, 4, D, h4, h4b, tmp, scr, scrb)
                transpose_blk(K, h4, h4b, 4, 8, hT, hTb, [p[0] for p in psT], [p[1] for p in psT], ident, identb)
                for c in range(24):
                    ps_, psb_ = PS[pi % 6]
                    pi += 1
                    for kc in range(8):
                        mm(K, ps_[:, :], w[:, kc, c * 128:(c + 1) * 128], hT[:, kc, :], kc == 0, kc == 7, [wb, hTb],
                           [psb_], inc=(kc == 7))
                    if c < 12:
                        actf(K, yb[:, c, :], ps_[:, :], AF.Gelu_apprx_tanh, [psb_], [ybb])
                    else:
                        cp(K, 'dve', xrb_[:, c - 12, :], ps_[:, :], [psb_], [xrbb])
                K.dma('pool', C.yg[:, t:t + 512].rearrange("(c p) n -> p c n", p=128), yb[:], ybb, reads=[ybb],
                      dw=[C.ygb])
                K.dma('pool', C.xr[:, tx:tx + 512].rearrange("(c p) n -> p c n", p=128), xrb_[:], xrbb,
                      reads=[xrbb], dw=[C.xrb])
        K.barrier()
        K.flush()


def phase5(K, nc, C):
    W = C.W
    TC = 2048
    with ExitStack() as es:
        sbf, psf = mk_alloc(nc, es)
        cw, cwb = sbf("cw", [128, 12, 4], F32)
        cb, cbb = sbf("cb", [128, 12], F32)
        gb, gbb = sbf("gb", [128, 12, 4], F32)
        lam, lamb = sbf("lam", [128, 12, 2], F32)
        cc, ccb = sbf("cc", [128, 12, 2], F32)
        gw, gwb = sbf("gw", [128, 12, 4, 128], BF16)
        for j in range(4):
            K.dma('sp', cw[:, :, j], W['conv_w'][0][j].rearrange("(n p) -> p n", p=128), cwb, writes=[cwb])
        K.dma('sp', cb[:], W['conv_b'][0].rearrange("(n p) -> p n", p=128), cbb, writes=[cbb])
        for a in range(2):
            for k in range(2):
                K.dma('sp', gb[:, :, a * 2 + k], W['lru_b_gate'][0][a, k].rearrange("(n p) -> p n", p=128), gbb,
                      writes=[gbb])
            K.dma('sp', lam[:, :, a], W['lru_lambda'][0][a].rearrange("(n p) -> p n", p=128), lamb, writes=[lamb])
        with ExitStack() as es2:
            sbf2, _ = mk_alloc(nc, es2)
            gst, gstb = sbf2("gst", [128, 12, 4, 128], F32)
            for a in range(2):
                for k in range(2):
                    K.dma('sp', gst[:, :, a * 2 + k, :], W['lru_w_gate'][0][a, k].rearrange("n c d -> c n d"), gstb,
                          writes=[gstb])
            cp(K, 'dve', gw[:], gst[:], [gstb], [gwb])
            ex, exb = sbf2("sp_x", [128, 12, 2], F32)
            lnv, lnvb = sbf2("sp_ln", [128, 12, 2], F32)
            ser, serb = sbf2("sp_ser", [128, 12, 2], F32)
            msk, mskb = sbf2("sp_m", [128, 12, 2], F32)
            actf(K, ex[:], lam[:], AF.Exp, [lamb], [exb], scale=-1.0)
            actf(K, lnv[:], ex[:], AF.Ln, [exb], [lnvb], bias=1.0)
            ts(K, 'dve', ser[:], ex[:], -0.25, 1.0 / 3.0, ALU.mult, ALU.add, [exb], [serb])
            tt(K, 'dve', ser[:], ser[:], ex[:], ALU.mult, [serb, exb], [serb])
            ts(K, 'dve', ser[:], ser[:], -1.0, 0.5, ALU.mult, ALU.add, [serb], [serb])
            tt(K, 'dve', ser[:], ser[:], ex[:], ALU.mult, [serb, exb], [serb])
            ts(K, 'dve', ser[:], ser[:], -1.0, 1.0, ALU.mult, ALU.add, [serb], [serb])
            tt(K, 'dve', ser[:], ser[:], ex[:], ALU.mult, [serb, exb], [serb])
            K.op('dve', lambda e: e.tensor_single_scalar(out=msk[:], in_=ex[:], scalar=0.05, op=ALU.is_lt),
                 [exb], [mskb])
            tt(K, 'dve', ser[:], ser[:], lnv[:], ALU.subtract, [serb, lnvb], [serb])
            tt(K, 'dve', ser[:], ser[:], msk[:], ALU.mult, [serb, mskb], [serb])
            tt(K, 'dve', cc[:], ser[:], lnv[:], ALU.add, [serb, lnvb], [ccb])
            ts(K, 'dve', cc[:], cc[:], -8.0, None, ALU.mult, None, [ccb], [ccb])
            K.barrier()
            K.flush()
        sets = []
        for i in range(2):
            st_ = {}
            for nm, shp, dt in (("xrt", [128, TC + 4], F32), ("xc", [128, TC], F32), ("xcbf", [128, TC], BF16),
                                ("rt", [128, TC], F32), ("it", [128, TC], F32), ("at", [128, TC], F32),
                                ("ml", [128, TC], F32), ("ut", [128, TC], F32), ("hs", [128, TC], F32),
                                ("hfl", [128, TC], F32), ("yl", [128, TC], BF16), ("ot", [128, TC], BF16)):
                st_[nm] = sbf("%s_%d" % (nm, i), shp, dt)
            sets.append(st_)
        uidx = 0
        car, carb = sbf("car", [128, 1], F32)
        PSG = [psf("p5pg%d" % i, (128, 2048)) for i in range(2)]
        pi = 0
        units = []
        for si, (off, S, poff) in enumerate(C.seqs):
            ntc = S // TC
            xb0 = off + 4 * si + 2
            for n in range(12):
                for a in range(2):
                    order = range(ntc) if a == 0 else range(ntc - 1, -1, -1)
                    for oi, tc in enumerate(order):
                        units.append((off, ntc, xb0, n, a, oi, tc))

        def tiles(k):
            st_ = sets[k % 2]
            return [st_[nm] for nm in ("xrt", "xc", "xcbf", "rt", "it", "at", "ml", "ut", "hs", "hfl", "yl", "ot")]

        def stage_a(u, k):
            off, ntc, xb0, n, a, oi, tc = u
            (xrt, xrtb), (xc, xcb), (xcbf, xcbfb), (rt, rtb), (it, itb), (at, atb), (ml, mlb), (ut, utb) = tiles(k)[0:8]
            tx = xb0 + tc * TC
            K.dma('sp', xrt[:, 0:TC + 3], C.xr[n * 128:(n + 1) * 128, tx - 2:tx + TC + 1], xrtb,
                  writes=[xrtb], dr=[C.xrb])
            ts(K, 'dve', xc[:, :], xrt[:, 0:TC], cw[:, n, 0:1], cb[:, n:n + 1], ALU.mult, ALU.add,
               [xrtb, cwb, cbb], [xcb])
            for j in range(1, 4):
                K.op('dve', lambda e, j=j, n=n, xc=xc, xrt=xrt: e.scalar_tensor_tensor(
                    out=xc[:, :], in0=xrt[:, j:j + TC], scalar=cw[:, n, j:j + 1], in1=xc[:, :],
                    op0=ALU.mult, op1=ALU.add), [xrtb, cwb, xcb], [xcb])
            cp(K, 'act', xcbf[:, :], xc[:, :], [xcb], [xcbfb])
            for kk_, dst, dstb in ((0, rt, rtb), (1, it, itb)):
                ps_, psb_ = PSG[kk_]
                for q in range(4):
                    mm(K, ps_[:, q * 512:(q + 1) * 512], gw[:, n, a * 2 + kk_, :],
                       xcbf[:, q * 512:(q + 1) * 512], True, True, [gwb, xcbfb], [psb_], inc=(q == 3))
                actf(K, dst[:, :], ps_[:, :], AF.Sigmoid, [psb_, gbb], [dstb],
                     bias=gb[:, n, a * 2 + kk_:a * 2 + kk_ + 1])
            actf(K, at[:, :], rt[:, :], AF.Exp, [rtb, ccb], [atb], scale=cc[:, n, a:a + 1])

        def stage_a2(u, k):
            (xrt, xrtb), (xc, xcb), (xcbf, xcbfb), (rt, rtb), (it, itb), (at, atb), (ml, mlb), (ut, utb) = tiles(k)[0:8]
            tt(K, 'dve', ml[:, :], at[:, :], at[:, :], ALU.mult, [atb], [mlb])
            actf(K, ml[:, :], ml[:, :], AF.Sqrt, [mlb], [mlb], scale=-1.0, bias=1.0)
            tt(K, 'dve', ut[:, :], it[:, :], xc[:, :], ALU.mult, [itb, xcb], [utb])
            tt(K, 'dve', ut[:, :], ut[:, :], ml[:, :], ALU.mult, [utb, mlb], [utb])

        def stage_b(u, k):
            off, ntc, xb0, n, a, oi, tc = u
            tl = tiles(k)
            (at, atb), (ml, mlb), (ut, utb), (hs, hsb), (hfl, hflb), (yl, ylb), (ot, otb) = tl[5:12]
            t = off + tc * TC
            if a == 1:
                K.dma('sp', hfl[:], C.hf[n * 128:(n + 1) * 128, t:t + TC], hflb, writes=[hflb], dr=[C.hfb])
                K.dma('sp', yl[:], C.yg[n * 128:(n + 1) * 128, t:t + TC], ylb, writes=[ylb], dr=[C.ygb])
            init = 0.0 if oi == 0 else car[:, 0:1]
            rdc = [atb, utb] + ([] if oi == 0 else [carb])
            if a == 0:
                K.op('dve', lambda e, init=init, hs=hs, at=at, ut=ut: e.tensor_tensor_scan(
                    hs[:, :], at[:, :], ut[:, :], init, ALU.mult, ALU.add), rdc, [hsb])
                if ntc > 1:
                    cp(K, 'act', car[:, 0:1], hs[:, TC - 1:TC], [hsb], [carb])
                K.dma('pool', C.hf[n * 128:(n + 1) * 128, t:t + TC], hs[:, :], hsb, reads=[hsb], dw=[C.hfb])
            else:
                K.op('dve', lambda e, init=init, hs=hs, at=at, ut=ut: e.tensor_tensor_scan(
                    hs[:, ::-1], at[:, ::-1], ut[:, ::-1], init, ALU.mult, ALU.add), rdc, [hsb])
                if ntc > 1:
                    cp(K, 'act', car[:, 0:1], hs[:, 0:1], [hsb], [carb])
                tt(K, 'dve', hs[:, :], hs[:, :], hfl[:, :], ALU.add, [hsb, hflb], [hsb])
                tt(K, 'dve', ot[:, :], hs[:, :], yl[:, :], ALU.mult, [hsb, ylb], [otb])
                K.dma('pool', C.hT[n * 128:(n + 1) * 128, t:t + TC], ot[:, :], otb, reads=[otb], dw=[C.hTb])

        for i, u in enumerate(units):
            stage_a(u, i)
            if i >= 1:
                stage_b(units[i - 1], i - 1)
            stage_a2(u, i)
        stage_b(units[-1], len(units) - 1)
        K.barrier()
        K.flush()


def phase_select(K, nc, C):
    (off0, S0, _) = C.seqs[0]
    nch = S0 // 2048
    with ExitStack() as es:
        sbf, psf = mk_alloc(nc, es)
        sel, selb = sbf("sel", [128, nch], F32)
        K.dma('sp', sel[:], C.sel[:, :], selb, writes=[selb])
        hts = [sbf("sl_h%d" % i, [128, 2048], BF16) for i in range(3)]
        hacc = [sbf("sl_ha%d" % i, [128, 2048], F32) for i in range(2)]
        hob = [sbf("sl_ho%d" % i, [128, 2048], BF16) for i in range(2)]
        li = 0
        for n in range(12):
            a_, ab_ = hacc[n % 2]
            o_, ob_ = hob[n % 2]
            for c in range(nch):
                t_, tb_ = hts[li % 3]
                li += 1
                K.dma('sp', t_[:], C.hT[n * 128:(n + 1) * 128, off0 + c * 2048:off0 + (c + 1) * 2048], tb_,
                      writes=[tb_], dr=[C.hTb])
                if c == 0:
                    ts(K, 'dve', a_[:, :], t_[:, :], sel[:, 0:1], None, ALU.mult, None, [tb_, selb], [ab_])
                else:
                    K.op('dve', lambda e, a_=a_, t_=t_, c=c: e.scalar_tensor_tensor(
                        out=a_[:, :], in0=t_[:, :], scalar=sel[:, c:c + 1], in1=a_[:, :], op0=ALU.mult,
                        op1=ALU.add), [tb_, selb, ab_], [ab_])
            cp(K, 'act', o_[:, :], a_[:, :], [ab_], [ob_])
            K.dma('pool', C.hT2[n * 128:(n + 1) * 128, 0:2048], o_[:, :], ob_, reads=[ob_], dw=[C.hT2b])
        xts = [sbf("sl_x%d" % i, [128, D], F32) for i in range(3)]
        xacc = [sbf("sl_xa%d" % i, [128, D], F32) for i in range(2)]
        for j in range(16):
            a_, ab_ = xacc[j % 2]
            for c in range(nch):
                t_, tb_ = xts[li % 3]
                li += 1
                r0 = off0 + c * 2048 + j * 128
                K.dma('sp', t_[:], C.x1[r0:r0 + 128, :], tb_, writes=[tb_], dr=[C.x1b])
                eng = 'dve' if (li % 2 == 0) else 'pool'
                if c == 0:
                    ts(K, 'dve', a_[:, :], t_[:, :], sel[:, 0:1], None, ALU.mult, None, [tb_, selb], [ab_])
                else:
                    K.op('dve', lambda e, a_=a_, t_=t_, c=c: e.scalar_tensor_tensor(
                        out=a_[:, :], in0=t_[:, :], scalar=sel[:, c:c + 1], in1=a_[:, :], op0=ALU.mult,
                        op1=ALU.add), [tb_, selb, ab_], [ab_])
            K.dma('pool', C.x1c[j * 128:(j + 1) * 128, :], a_[:, :], ab_, reads=[ab_], dw=[C.x1cb])
        cb_ = Buf("selcopy")
        o2 = 2048
        for (off, S, poff) in C.seqs[1:]:
            for n in range(12):
                K.dma('sp', C.hT2[n * 128:(n + 1) * 128, o2:o2 + S], C.hT[n * 128:(n + 1) * 128, off:off + S], cb_,
                      dr=[C.hTb], dw=[C.hT2b])
            for r0 in range(0, S, 512):
                K.dma('sp', C.x1c[o2 + r0:o2 + r0 + 512, :], C.x1[off + r0:off + r0 + 512, :], cb_, dr=[C.x1b],
                      dw=[C.x1cb])
            o2 += S
        K.barrier()
        K.flush()


def _bf(a):
    return np.asarray(a, np.float32).astype(ml_dtypes.bfloat16)


def host_consts(smax):
    c = {}
    c['ident'] = _bf(np.eye(128))

    def rot(n, blocks):
        r = np.zeros((n, n), np.float32)
        for (b0, half) in blocks:
            for m in range(half):
                r[b0 + m + half, b0 + m] = -1.0
                r[b0 + m, b0 + m + half] = 1.0
        return r
    c['r96'] = _bf(rot(128, [(64, 16)]))
    c['r128'] = _bf(rot(128, [(0, 8), (64, 8)]))
    pos = np.arange(smax, dtype=np.float32)

    def tables(half):
        inv = np.power(np.float32(500000.0), -(np.arange(half, dtype=np.float32) / np.float32(half))).astype(np.float32)
        ang = (pos[:, None] * inv[None, :]).astype(np.float32).astype(np.float64)
        return np.cos(ang).T.astype(np.float32), np.sin(ang).T.astype(np.float32)
    co, si = tables(16)
    c96 = np.ones((128, smax), np.float32)
    s96 = np.zeros((128, smax), np.float32)
    c96[64:80] = co
    c96[80:96] = co
    s96[64:80] = si
    s96[80:96] = si
    c['c96'], c['s96'] = c96, s96
    co, si = tables(8)
    c128 = np.ones((128, smax), np.float32)
    s128 = np.zeros((128, smax), np.float32)
    for b0 in (0, 64):
        c128[b0:b0 + 8] = co
        c128[b0 + 8:b0 + 16] = co
        s128[b0:b0 + 8] = si
        s128[b0 + 8:b0 + 16] = si
    c['c128'], c['s128'] = c128, s128
    j = np.arange(128)[:, None]
    i = np.arange(128)[None, :]

    def tab(allowed, nq):
        m = np.where(allowed[:, :nq], 0.0, NEG).astype(np.float32)
        return _bf(np.tile(m, (1, 512 // nq)))
    A = j >= i
    B = j <= i
    c['m_A128'] = tab(A, 128)
    c['m_B128'] = tab(B, 128)
    c['m_A128f'] = tab(A & (j >= 64), 128)
    c['m_B128l'] = tab(B & (j < 64), 128)
    a1f = np.array(c['m_A128'])
    a1f[:, 0:128] = np.array(c['m_A128f'])[:, 0:128]
    c['m_A1f'] = a1f
    b1l = np.array(c['m_B128'])
    b1l[:, 384:512] = np.array(c['m_B128l'])[:, 0:128]
    c['m_B1l'] = b1l
    c['m_A32'] = tab(A, 32)
    c['m_A32u0'] = tab(A & (j >= 64), 32)
    c['m_A32u32'] = tab(A & (j >= 32), 32)
    c['m_A32l'] = tab(A & (j < 96), 32)
    c['m_B32'] = tab(B, 32)
    c['m_ALL'] = _bf(np.full((128, 512), NEG, np.float32))
    return c


WSHAPES = {
    "norm_mix": (2, 1024), "w_in_a": (1, 1024, 5024), "q_norm": (1, 256), "w_uq": (1, 256, 768),
    "kv_norm": (1, 128), "w_ukv": (1, 128, 1024), "w_out_a": (1, 1024, 1024), "w_in_r": (1, 1024, 3072),
    "conv_w": (1, 4, 1536), "conv_b": (1, 1536), "lru_w_gate": (1, 2, 2, 12, 128, 128),
    "lru_b_gate": (1, 2, 2, 1536), "lru_lambda": (1, 2, 1536), "w_out_r": (1, 1536, 1024),
    "norm_ffn": (2, 1024), "w_gu": (2, 1024, 5632), "w_down": (2, 2816, 1024), "norm_final": (1024,),
}


def build(seq_lens, stop_after=99, debug=False, dbg=''):
    nc = bass.Bass("TRN2", target_bir_lowering=False)
    C = Ctx()
    C.dbg = dbg
    seqs = []
    off = 0
    poff = PAD
    for S in seq_lens:
        seqs.append((off, S, poff))
        off += S
        poff += S + 2 * PAD
    T = off
    TP = poff - PAD
    C.seqs = seqs
    smax = max(seq_lens)
    C.x = nc.dram_tensor("x", [T, D], F32, kind="ExternalInput").ap()
    C.W = {k: nc.dram_tensor(k, list(v), F32, kind="ExternalInput").ap() for k, v in WSHAPES.items()}
    hc = host_consts(smax)
    C.cst = {}
    for k, v in hc.items():
        dt = BF16 if v.dtype == ml_dtypes.bfloat16 else F32
        C.cst[k] = nc.dram_tensor("c_" + k, list(v.shape), dt, kind="ExternalInput").ap()
    compact = stop_after >= 5 and len(seq_lens) >= 1 and seq_lens[0] % 2048 == 0
    C.compact = compact
    nch = seq_lens[0] // 2048
    seqs2 = [(0, 2048, 0)]
    o2 = 2048
    for S in seq_lens[1:]:
        seqs2.append((o2, S, 0))
        o2 += S
    T2 = o2
    C.seqs2 = seqs2
    C.y = nc.dram_tensor("y", [T2 if compact else T, D], F32, kind="ExternalOutput").ap()
    C.yb = Buf("y")
    C.sel = nc.dram_tensor("sel", [128, nch], F32, kind="ExternalInput").ap()

    def scratch(name, shape, dt):
        setattr(C, name, nc.dram_tensor("s_" + name, shape, dt).ap())
        setattr(C, name + "b", Buf(name))
    C.xb = Buf("x")
    scratch("qT", [8, 96, T], BF16)
    scratch("kT", [8, 96, T], BF16)
    scratch("vA", [T, 8, 128], BF16)
    scratch("dqT", [3, 8, 64, T], BF16)
    scratch("dkT", [3, 8, 64, TP], BF16)
    scratch("dvA", [TP, 3, 8, 128], BF16)
    scratch("oT", [1024, T], BF16)
    scratch("xm", [T, D], F32)
    scratch("x1", [T, D], F32)
    scratch("yg", [D_RNN, T], BF16)
    scratch("xr", [D_RNN, T + 4 * len(seq_lens)], F32)
    scratch("hf", [D_RNN, T], F32)
    scratch("hT", [D_RNN, T], BF16)
    scratch("xm2", [T, D], F32)
    scratch("hT2", [D_RNN, T2], BF16)
    scratch("x1c", [T2, D], F32)
    with ExitStack() as es:
        K = KB(nc, es)
        phase1(K, nc, C)
        if stop_after >= 2:
            phase2_mla(K, nc, C)
            phase2_dil(K, nc, C)
        if stop_after >= 3:
            phase_proj(K, nc, C, C.oT, C.oTb, 8, C.W['w_out_a'][0], C.x, C.xb, C.xm, C.xmb, "pa")
            phase_ffn(K, nc, C, 0, C.xm, C.xmb, C.x1 if stop_after > 3 else C.y, C.x1b if stop_after > 3 else C.yb,
                      False, "f0")
        if stop_after >= 4:
            phase4(K, nc, C)
            phase5(K, nc, C)
        if stop_after >= 5:
            phase_select(K, nc, C)
            phase_proj(K, nc, C, C.hT2, C.hT2b, 12, C.W['w_out_r'][0], C.x1c, C.x1cb, C.xm2, C.xm2b, "pb",
                       seqs=C.seqs2)
            phase_ffn(K, nc, C, 1, C.xm2, C.xm2b, C.y, C.yb, True, "f1", seqs=C.seqs2)
        K.barrier()
        K.real_flush()
        print("instructions recorded:", K.ninst, "dma sems:", len(K.all_dsem))
        if K.TRACE:
            for t in K.trace:
                print("TR", t)
    return nc, hc


SEQ_LENS = [16384, 2048, 2048]
_CACHE = {}


def kernel(**inputs):
    xp = np.asarray(inputs["x_prompt"], np.float32).reshape(16384, D)
    xs = np.asarray(inputs["x_sample"], np.float32).reshape(16, 2048, D)
    if "nc" not in _CACHE:
        _CACHE["nc"] = build(SEQ_LENS)
    nc, hc = _CACHE["nc"]
    in_maps = []
    for c in range(8):
        m = {"x": np.ascontiguousarray(np.concatenate([xp, xs[2 * c], xs[2 * c + 1]], axis=0))}
        for k in WSHAPES:
            m[k] = np.ascontiguousarray(np.asarray(inputs[k], np.float32))
        for k, v in hc.items():
            m["c_" + k] = v
        sel = np.zeros((128, 8), np.float32)
        sel[:, c] = 1.0
        m["sel"] = sel
        in_maps.append(m)
    res = run_bass_kernel_spmd(nc, in_maps, core_ids=list(range(8)))
    y_prompt = np.concatenate([np.asarray(res.results[c]["y"][0:2048], np.float32) for c in range(8)],
                              axis=0).reshape(1, 16384, D)
    y_sample = np.stack([np.asarray(res.results[c // 2]["y"][2048 * (1 + c % 2):2048 * (2 + c % 2)], np.float32)
                         for c in range(16)], axis=0)
    return (y_prompt, y_sample)
```

```python
import numpy as np
import ml_dtypes
from contextlib import ExitStack
import concourse.bass as bass
import concourse.mybir as mybir
from concourse.bass_utils import run_bass_kernel_spmd

F32 = mybir.dt.float32
BF16 = mybir.dt.bfloat16
AF = mybir.ActivationFunctionType
ALU = mybir.AluOpType
AX = mybir.AxisListType

D = 1024
PAD = 1024
MIX_IN = 5024
D_RNN = 1536
D_FF = 2816
EPS = 1e-6
NEG = -30000.0


class Sem:
    def __init__(s, h):
        s.h = h
        s.cnt = 0


class Buf:
    def __init__(s, name):
        s.name = name
        s.w = {}
        s.r = {}
        s.sem = None
        s.excl = False


class KB:
    ENG = ('pe', 'act', 'dve', 'pool', 'sp')

    def __init__(s, nc, es):
        s.nc = nc
        s.es = es
        s.esem = {e: Sem(es.enter_context(nc.semaphore('s_' + e))) for e in ('pe', 'act', 'dve', 'pool')}
        s.n = {e: 0 for e in s.esem}
        s.waited = {e: {} for e in s.ENG}
        s.prog = {e: [] for e in s.ENG}
        s.free_dsem = []
        s.all_dsem = []
        s.ninst = 0
        s.qhist = {}
        s.QDEPTH = 6
        s.pending = []
        s.pend_reads = set()
        import os
        s.MAXOPS = int(os.environ.get("MAXOPS") or "100000000")
        s.TRACE = bool(os.environ.get("KTRACE", ""))
        s.trace = []

    def dsem(s):
        if s.free_dsem:
            return s.free_dsem.pop()
        sm = Sem(s.es.enter_context(s.nc.semaphore('d%d' % len(s.all_dsem))))
        s.all_dsem.append(sm)
        return sm

    def _wait(s, eng, need):
        for sm, v in need.items():
            if s.waited[eng].get(sm, 0) >= v:
                continue
            s.waited[eng][sm] = v
            s.prog[eng].append(lambda e, h=sm.h, v=v: e.wait_ge(h, v))

    def op(s, eng, fn, reads=(), writes=(), inc=True):
        if s.ninst >= s.MAXOPS:
            return
        s._autoflush(writes)
        need = {}
        own = s.esem[eng]

        def add(d, war):
            for sm, v in d.items():
                if sm is own and eng == 'pe':
                    continue
                if need.get(sm, 0) < v:
                    need[sm] = v
        for b in reads:
            add(b.w, False)
            if b.excl:
                for sm, v in b.r.items():
                    if sm is not own and need.get(sm, 0) < v:
                        need[sm] = v
        for b in writes:
            add(b.w, False)
            add(b.r, True)
        s._wait(eng, need)
        s.ninst += 1
        if s.TRACE:
            import traceback
            fr = traceback.extract_stack(limit=5)
            s.trace.append((s.ninst, eng, [(f.lineno) for f in fr[:-1]]))
        if inc:
            s.n[eng] += 1
            v = s.n[eng]
            s.prog[eng].append(lambda e, fn=fn, h=own.h: fn(e).then_inc(h, 1))
        else:
            v = s.n[eng] + 1
            s.prog[eng].append(lambda e, fn=fn: fn(e))
        for b in reads:
            b.r[own] = max(b.r.get(own, 0), v)
        for b in writes:
            b.w = {own: v}
            b.r = {}

    def dma(s, q, out, in_, sb, reads=(), writes=(), dr=(), dw=()):
        if q == 'pool':
            s.pending.append((out, in_, sb, tuple(reads), tuple(writes), tuple(dr), tuple(dw)))
            for b in reads:
                s.pend_reads.add(id(b))
            for b in dw:
                s.pend_reads.add(id(b))
            return
        s._autoflush(tuple(writes) + tuple(dr))
        s._dma(q, out, in_, sb, reads, writes, dr, dw)

    def _autoflush(s, writes):
        if s.pending:
            for b in writes:
                if id(b) in s.pend_reads:
                    s.flush_stores()
                    return

    def flush_stores(s):
        pend = s.pending
        s.pending = []
        s.pend_reads = set()
        for (out, in_, sb, reads, writes, dr, dw) in pend:
            s._dma('sp', out, in_, sb, reads, writes, dr, dw)

    def _dma(s, q, out, in_, sb, reads=(), writes=(), dr=(), dw=()):
        if s.ninst >= s.MAXOPS:
            return
        if sb.sem is None:
            sb.sem = s.dsem()
        sm = sb.sem
        need = {}

        def add(d, skip_same=False):
            for x, v in d.items():
                if skip_same and x is sm:
                    continue
                if need.get(x, 0) < v:
                    need[x] = v
        for b in reads:
            add(b.w)
        for b in writes:
            add(b.w, True)
            add(b.r)
        for b in dr:
            add(b.w)
        for b in dw:
            add(b.r)
        s._wait(q, need)
        hist = s.qhist.setdefault(q, [])
        if len(hist) >= s.QDEPTH:
            osm, ov = hist[-s.QDEPTH]
            s._wait(q, {osm: 16 * osm.cnt})
        sm.cnt += 1
        v = 16 * sm.cnt
        hist.append((sm, v))
        if len(hist) > 64:
            del hist[:32]
        s.ninst += 1
        s.prog[q].append(lambda e, o=out, i=in_, h=sm.h: e.dma_start(out=o, in_=i).then_inc(h, 16))
        for b in reads:
            b.r[sm] = v
        for b in writes:
            b.w = {sm: v}
            b.r = {}
        for b in dr:
            b.r[sm] = v
        for b in dw:
            b.w[sm] = v

    def barrier(s):
        s.flush_stores()
        toks = {}
        for e, sm in s.esem.items():
            if s.n[e] > 0:
                toks[sm] = s.n[e]
        for sm in s.all_dsem:
            if sm.cnt > 0:
                toks[sm] = 16 * sm.cnt
        for e in s.ENG:
            s._wait(e, dict(toks))
        s.free_dsem = list(s.all_dsem)

    def flush(s):
        return

    def real_flush(s):
        nc = s.nc
        prog = s.prog
        with nc.allow_non_contiguous_dma(reason="small parameter gathers"), nc.Block() as blk:
            @blk.tensor
            def _(e):
                for f in prog['pe']:
                    f(e)

            @blk.scalar
            def _(e):
                for f in prog['act']:
                    f(e)

            @blk.vector
            def _(e):
                for f in prog['dve']:
                    f(e)

            @blk.gpsimd
            def _(e):
                for f in prog['pool']:
                    f(e)

            @blk.sync
            def _(e):
                for f in prog['sp']:
                    f(e)
        s.prog = {e: [] for e in s.ENG}


class Ctx:
    pass


_UID = [0]
NAMES = {}


def mk_alloc(nc, es):
    def sb(name, shape, dt):
        _UID[0] += 1
        NAMES[name] = "t%d_%s" % (_UID[0], name)
        t = es.enter_context(nc.sbuf_tensor("t%d_%s" % (_UID[0], name), shape, dt))
        return t, Buf(name)

    def ps(name, shape=(128, 512), dt=F32):
        _UID[0] += 1
        t = es.enter_context(nc.psum_tensor("p%d_%s" % (_UID[0], name), list(shape), dt))
        b = Buf(name)
        b.excl = True
        return t, b
    return sb, ps


def mm(K, out, lhsT, rhs, start, stop, reads, writes, inc=False):
    K.op('pe', lambda e: e.matmul(out, lhsT=lhsT, rhs=rhs, start=start, stop=stop, skip_group_check=True),
         reads, writes, inc)


def actf(K, out, in_, func, reads, writes, scale=1.0, bias=None, accum=None):
    kw = {}
    if bias is not None:
        kw['bias'] = bias
    if accum is not None:
        kw['accum_out'] = accum
    K.op('act', lambda e: e.activation(out=out, in_=in_, func=func, scale=scale, **kw), reads, writes)


def tt(K, eng, out, in0, in1, op, reads, writes):
    K.op(eng, lambda e: e.tensor_tensor(out=out, in0=in0, in1=in1, op=op), reads, writes)


def ts(K, eng, out, in0, s1, s2, op0, op1, reads, writes):
    if s2 is None:
        K.op(eng, lambda e: e.tensor_scalar(out=out, in0=in0, scalar1=s1, scalar2=None, op0=op0), reads, writes)
    else:
        K.op(eng, lambda e: e.tensor_scalar(out=out, in0=in0, scalar1=s1, scalar2=s2, op0=op0, op1=op1),
             reads, writes)


def cp(K, eng, out, in_, reads, writes):
    if eng == 'act':
        K.op('act', lambda e: e.copy(out=out, in_=in_), reads, writes)
    else:
        K.op(eng, lambda e: e.tensor_copy(out=out, in_=in_), reads, writes)


def load_w(K, sbf, dst, dstb, src, kch, ncols, gain, gainb, tag, colchunk=1024):
    st0, stb0 = sbf(tag + "_st0", [128, colchunk], F32)
    st1, stb1 = sbf(tag + "_st1", [128, colchunk], F32)
    sts = [(st0, stb0), (st1, stb1)]
    i = 0
    engs = ['dve', 'act', 'dve']
    for kc in range(kch):
        for c0 in range(0, ncols, colchunk):
            c1 = min(ncols, c0 + colchunk)
            st, stb = sts[i % 2]
            K.dma('sp', st[:, 0:c1 - c0], src[kc * 128:(kc + 1) * 128, c0:c1], stb, writes=[stb])
            eng = engs[i % 3]
            o = dst(kc, c0, c1)
            if gain is None:
                cp(K, eng, o, st[:, 0:c1 - c0], [stb], [dstb])
            elif eng == 'act':
                K.op('act', lambda e, o=o, a=st[:, 0:c1 - c0], g=gain[:, kc:kc + 1]: e.activation(
                    out=o, in_=a, func=AF.Copy, scale=g), [stb, gainb], [dstb])
            else:
                ts(K, eng, o, st[:, 0:c1 - c0], gain[:, kc:kc + 1], None, ALU.mult, None, [stb, gainb], [dstb])
            i += 1


def rmsnorm_tok(K, x4, xb, ntile, width, h4, hb, tmp, scr, scrb, eng_mul='pool'):
    ss, ssb = tmp['ss']
    ms, msb = tmp['ms']
    sd, sdb = tmp['sd']
    rs, rsb = tmp['rs']
    xbl = xb if isinstance(xb, (list, tuple)) else [xb] * ntile
    for j in range(ntile):
        actf(K, scr[:, 0:width], x4[:, j, :], AF.Square, [xbl[j]], [scrb, ssb], accum=ss[:, j:j + 1])
    ts(K, 'dve', ms[:, 0:ntile], ss[:, 0:ntile], 1.0 / width, EPS, ALU.mult, ALU.add, [ssb], [msb])
    actf(K, sd[:, 0:ntile], ms[:, 0:ntile], AF.Sqrt, [msb], [sdb])
    K.op('dve', lambda e: e.reciprocal(out=rs[:, 0:ntile], in_=sd[:, 0:ntile]), [sdb], [rsb])
    for j in range(ntile):
        if j % 2 == 0:
            K.op('act', lambda e, o=h4[:, j, :], a=x4[:, j, :], g=rs[:, j:j + 1]: e.activation(
                out=o, in_=a, func=AF.Copy, scale=g), [xbl[j], rsb], [hb])
        else:
            ts(K, 'dve', h4[:, j, :], x4[:, j, :], rs[:, j:j + 1], None, ALU.mult, None, [xbl[j], rsb], [hb])


def transpose_blk(K, h4, hb, ntile, nchunk, hT, hTb, psT, psTb, ident, identb):
    for j in range(ntile):
        pt, ptb = psT[j % len(psT)], psTb[j % len(psT)]
        for kc in range(nchunk):
            K.op('pe', lambda e, o=pt[:, kc * 128:(kc + 1) * 128], i=h4[:, j, kc * 128:(kc + 1) * 128]:
                 e.transpose(out=o, in_=i, identity=ident[:]), [hb, identb], [ptb], inc=(kc == nchunk - 1))
        cp(K, 'act' if j % 2 == 0 else 'dve', hT[:, :, j * 128:(j + 1) * 128],
           pt[:, 0:nchunk * 128].rearrange("p (c n) -> p c n", c=nchunk), [ptb], [hTb])


def colnorm(K, psl, pslb, nch, width, sq, sqb, ones, onesb, pss, pssb, rq, rqb, t1, t1b, outT, outb):
    for c in range(nch):
        actf(K, sq[:, c, :], psl[c][:, :], AF.Square, [pslb[c]], [sqb])
    for c in range(nch):
        mm(K, pss[:, :], ones[:, :], sq[:, c, :], c == 0, c == nch - 1, [onesb, sqb], [pssb], inc=(c == nch - 1))
    ts(K, 'dve', t1[:, :], pss[:, :], 1.0 / width, EPS, ALU.mult, ALU.add, [pssb], [t1b])
    actf(K, t1[:, :], t1[:, :], AF.Sqrt, [t1b], [t1b])
    K.op('dve', lambda e: e.reciprocal(out=rq[:, :], in_=t1[:, :]), [t1b], [rqb])
    for c in range(nch):
        tt(K, 'dve', outT[:, c, :], psl[c][:, :], rq[:, :], ALU.mult, [pslb[c], rqb], [outb])


def phase1(K, nc, C):
    SEQS = C.seqs
    with ExitStack() as es:
        sbf, psf = mk_alloc(nc, es)
        W = C.W
        ident, identb = sbf("ident", [128, 128], BF16)
        r96, r96b = sbf("r96", [128, 128], BF16)
        r128, r128b = sbf("r128", [128, 128], BF16)
        ones, onesb = sbf("ones", [128, 128], BF16)
        K.dma('sp', ident[:], C.cst['ident'][:, :], identb, writes=[identb])
        K.dma('sp', r96[:], C.cst['r96'][:, :], r96b, writes=[r96b])
        K.dma('sp', r128[:], C.cst['r128'][:, :], r128b, writes=[r128b])
        K.op('dve', lambda e: e.memset(ones[:], 1.0), [], [onesb])
        zt, ztb = sbf("zt", [128, 3072], BF16)
        K.op('pool', lambda e: e.memset(zt[:], 0.0), [], [ztb])
        for (off, S, poff) in SEQS:
            for p0 in (poff - PAD, poff + S):
                for g in range(3):
                    for h in range(8):
                        K.dma('pool', C.dkT[g, h, :, p0:p0 + PAD], zt[0:64, 0:PAD], ztb, reads=[ztb], dw=[C.dkTb])
                for r0 in range(0, PAD, 128):
                    K.dma('pool', C.dvA[p0 + r0:p0 + r0 + 128].rearrange("p g h e -> p (g h e)"), zt[:, :], ztb,
                          reads=[ztb], dw=[C.dvAb])
        g0, g0b = sbf("g0", [128, 8], F32)
        K.dma('sp', g0[:], W['norm_mix'][0].rearrange("(c p) -> p c", p=128), g0b, writes=[g0b])
        gq, gqb = sbf("gq", [128, 2], F32)
        K.dma('sp', gq[:], W['q_norm'][0].rearrange("(c p) -> p c", p=128), gqb, writes=[gqb])
        gk, gkb = sbf("gk", [128, 1], F32)
        K.dma('sp', gk[:], W['kv_norm'][0].rearrange("(c p) -> p c", p=128), gkb, writes=[gkb])
        w_in, w_inb = sbf("w_in", [128, 8, MIX_IN], BF16)
        wkr, wkrb = sbf("wkr", [128, 8, 128], BF16)
        w_uq, w_uqb = sbf("w_uq", [128, 2, 800], BF16)
        K.op('pool', lambda e: e.memset(w_uq[:], 0.0), [], [w_uqb])
        w_ukv, w_ukvb = sbf("w_ukv", [128, 1024], BF16)
        wvc, wvcb = sbf("wvc", [128, 512], BF16)
        with ExitStack() as es2:
            sbf2, _ = mk_alloc(nc, es2)
            load_w(K, sbf2, lambda kc, c0, c1: w_in[:, kc, c0:c1], w_inb, W['w_in_a'][0], 8, MIX_IN, g0, g0b, "wi")
            load_w(K, sbf2, lambda kc, c0, c1: w_uq[:, kc, c0:c1], w_uqb, W['w_uq'][0], 2, 768, gq, gqb, "wq")
            load_w(K, sbf2, lambda kc, c0, c1: w_ukv[:, c0:c1], w_ukvb, W['w_ukv'][0], 1, 1024, gk, gkb, "wk")
            for h in range(8):
                cp(K, 'dve', wvc[:, h * 64:(h + 1) * 64], w_ukv[:, h * 128 + 64:(h + 1) * 128], [w_ukvb], [wvcb])
            K.op('pool', lambda e: e.memset(wkr[:], 0.0), [], [wkrb])
            for kc in range(8):
                cp(K, 'dve', wkr[:, kc, 64:96], w_in[:, kc, 384:416], [w_inb], [wkrb])
            K.barrier()
            K.flush()
        if getattr(C, 'dbg', '') == 'init':
            return
        xblk, xblkb = sbf("xblk", [128, 4, D], F32)
        h4, h4b = sbf("h4", [128, 4, D], BF16)
        hT, hTb = sbf("hT", [128, 8, 512], BF16)
        scr, scrb = sbf("scr", [128, D], BF16)
        tmp = {k: sbf("n_" + k, [128, 4], F32) for k in ('ss', 'ms', 'sd', 'rs')}
        sq, sqb = sbf("sq", [128, 2, 512], BF16)
        cqn, cqnb = sbf("cqn", [128, 2, 512], BF16)
        ckvn, ckvnb = sbf("ckvn", [128, 1, 512], BF16)
        t1s = [sbf("t1_%d" % i, [128, 512], F32) for i in range(2)]
        t2, t2b = sbf("t2", [128, 512], F32)
        t3, t3b = sbf("t3", [128, 512], F32)
        rq, rqb = sbf("rq", [128, 512], F32)
        qas = [sbf("qa_%d" % i, [128, 512], BF16) for i in range(2)]
        rk = [0]
        qTb_, qTbb = sbf("qTblk", [96, 8, 512], BF16)
        kTb_, kTbb = sbf("kTblk", [96, 8, 512], BF16)
        krf, krfb = sbf("krf", [128, 512], BF16)
        vblk, vblkb = sbf("vblk", [128, 4, 8, 128], BF16)
        dblk, dblkb = sbf("dblk", [128, 12, 512], BF16)
        c96, c96b = sbf("c96", [128, 512], F32)
        s96, s96b = sbf("s96", [128, 512], F32)
        c128, c128b = sbf("c128", [128, 512], F32)
        s128, s128b = sbf("s128", [128, 512], F32)
        K.op('pool', lambda e: e.memset(vblk[:], 1.0), [], [vblkb])
        psT = []
        psTb = []
        for i in range(2):
            a, b = psf("psT%d" % i, (128, 1024), BF16)
            psT.append(a)
            psTb.append(b)
        PS = []
        PSb = []
        for i in range(6):
            a, b = psf("ps%d" % i)
            PS.append(a)
            PSb.append(b)
        rr = [0]

        def nps():
            rr[0] = (rr[0] + 1) % 4
            return PS[2 + rr[0]], PSb[2 + rr[0]]

        def rope_p1(psa, psab, ct, ctb):
            k = rk[0] % 2
            rk[0] += 1
            qa, qab = qas[k]
            t1, t1b = t1s[k]
            cp(K, 'act', qa[:, :], psa[:, :], [psab], [qab])
            tt(K, 'dve', t1[:, :], psa[:, :], ct[:, :], ALU.mult, [psab, ctb], [t1b])
            return k

        def rope_p2(k, nout, rmat, rmatb, st_, stb_, dst, dstw):
            qa, qab = qas[k]
            t1, t1b = t1s[k]
            pr, prb = nps()
            mm(K, pr[:, :], rmat[:, :], qa[:, :], True, True, [rmatb, qab], [prb], inc=True)
            tt(K, 'dve', t2[:, :], pr[:, :], st_[:, :], ALU.mult, [prb, stb_], [t2b])
            tt(K, 'dve', dst, t1[0:nout, :], t2[0:nout, :], ALU.add, [t1b, t2b], dstw)

        def rope_fm(psa, psab, nout, rmat, rmatb, ct, ctb, st_, stb_, dst, dstb, dstw):
            k = rope_p1(psa, psab, ct, ctb)
            rope_p2(k, nout, rmat, rmatb, st_, stb_, dst, dstw)

        for (off, S, poff) in SEQS:
            for blk in range(S // 512):
                t = off + blk * 512
                p = blk * 512
                tp = poff + blk * 512
                K.dma('sp', xblk[:], C.x[t:t + 512, :].rearrange("(j p) d -> p j d", p=128), xblkb, writes=[xblkb])
                K.dma('sp', c96[:], C.cst['c96'][:, p:p + 512], c96b, writes=[c96b])
                K.dma('sp', s96[:], C.cst['s96'][:, p:p + 512], s96b, writes=[s96b])
                K.dma('sp', c128[:], C.cst['c128'][:, p:p + 512], c128b, writes=[c128b])
                K.dma('sp', s128[:], C.cst['s128'][:, p:p + 512], s128b, writes=[s128b])
                rmsnorm_tok(K, xblk, xblkb, 4, D, h4, h4b, tmp, scr, scrb)
                transpose_blk(K, h4, h4b, 4, 8, hT, hTb, psT, psTb, ident, identb)
                if getattr(C, 'dbg', '') == 'norm':
                    K.barrier()
                    K.flush()
                    return
                for c in range(2):
                    for kc in range(8):
                        mm(K, PS[c][:, :], w_in[:, kc, c * 128:(c + 1) * 128], hT[:, kc, :], kc == 0, kc == 7,
                           [w_inb, hTb], [PSb[c]], inc=(kc == 7))
                pss, pssb = nps()
                colnorm(K, [PS[0], PS[1]], [PSb[0], PSb[1]], 2, 256, sq, sqb, ones, onesb, pss, pssb, rq, rqb,
                        t3, t3b, cqn, cqnb)
                pend = None
                for h in range(8):
                    pq, pqb = nps()
                    for c in range(2):
                        mm(K, pq[:, :], w_uq[:, c, h * 96:h * 96 + 128], cqn[:, c, :], c == 0, c == 1,
                           [w_uqb, cqnb], [pqb], inc=(c == 1))
                    k_ = rope_p1(pq, pqb, c96, c96b)
                    if pend is not None:
                        rope_p2(pend[0], 96, r96, r96b, s96, s96b, qTb_[0:96, pend[1], :], [qTbb])
                    pend = (k_, h)
                rope_p2(pend[0], 96, r96, r96b, s96, s96b, qTb_[0:96, pend[1], :], [qTbb])
                import os
                if os.environ.get("DBG3", "") != "nostore":
                    K.dma('pool', C.qT[:, :, t:t + 512].rearrange("h r n -> r h n"), qTb_[:], qTbb, reads=[qTbb],
                          dw=[C.qTb])
                if getattr(C, 'dbg', '') == 'q':
                    K.barrier()
                    K.flush()
                    return
                for kc in range(8):
                    mm(K, PS[0][:, :], w_in[:, kc, 256:384], hT[:, kc, :], kc == 0, kc == 7, [w_inb, hTb], [PSb[0]],
                       inc=(kc == 7))
                pss, pssb = nps()
                colnorm(K, [PS[0]], [PSb[0]], 1, 128, sq, sqb, ones, onesb, pss, pssb, rq, rqb, t3, t3b, ckvn, ckvnb)
                if getattr(C, 'dbg', '') == 'kv1':
                    K.barrier()
                    return
                for kc in range(8):
                    mm(K, PS[1][:, :], wkr[:, kc, :], hT[:, kc, :], kc == 0, kc == 7, [wkrb, hTb], [PSb[1]],
                       inc=(kc == 7))
                rope_fm(PS[1], PSb[1], 96, r96, r96b, c96, c96b, s96, s96b, krf[0:96, :], krfb, [krfb])
                if getattr(C, 'dbg', '') == 'kv2':
                    K.barrier()
                    return
                for h in range(8):
                    cp(K, 'act' if h % 2 == 1 else 'dve', kTb_[64:96, h, :], krf[64:96, :], [krfb], [kTbb])
                    pk, pkb = nps()
                    mm(K, pk[:, :], w_ukv[:, h * 128:(h + 1) * 128], ckvn[:, 0, :], True, True, [w_ukvb, ckvnb],
                       [pkb], inc=True)
                    cp(K, 'act' if h % 2 == 0 else 'dve', kTb_[0:64, h, :], pk[0:64, :], [pkb], [kTbb])
                if getattr(C, 'dbg', '') == 'kv3':
                    K.barrier()
                    return
                K.dma('pool', C.kT[:, :, t:t + 512].rearrange("h r n -> r h n"), kTb_[:], kTbb, reads=[kTbb],
                      dw=[C.kTb])
                if getattr(C, 'dbg', '') == 'kv4':
                    K.barrier()
                    return
                for j in range(4):
                    pv, pvb = nps()
                    mm(K, pv[:, :], ckvn[:, 0, j * 128:(j + 1) * 128], wvc[:, :],
                       True, True, [wvcb, ckvnb], [pvb], inc=True)
                    cp(K, 'act' if j % 2 == 0 else 'dve', vblk[:, j, :, 0:64],
                       pv[:, :].rearrange("p (h d) -> p h d", d=64), [pvb], [vblkb])
                K.dma('pool', C.vA[t:t + 512].rearrange("(j p) h e -> p j h e", p=128), vblk[:], vblkb,
                      reads=[vblkb], dw=[C.vAb])
                if getattr(C, 'dbg', '') == 'kv':
                    K.barrier()
                    K.flush()
                    return
                for qk in range(2):
                    pend = None
                    for c in range(12):
                        col0 = 416 + (qk * 12 + c) * 128
                        pa, pab = nps()
                        for kc in range(8):
                            mm(K, pa[:, :], w_in[:, kc, col0:col0 + 128], hT[:, kc, :], kc == 0, kc == 7,
                               [w_inb, hTb], [pab], inc=(kc == 7))
                        k_ = rope_p1(pa, pab, c128, c128b)
                        if pend is not None:
                            rope_p2(pend[0], 128, r128, r128b, s128, s128b, dblk[:, pend[1], :], [dblkb])
                        pend = (k_, c)
                    rope_p2(pend[0], 128, r128, r128b, s128, s128b, dblk[:, pend[1], :], [dblkb])
                    dst = C.dqT if qk == 0 else C.dkT
                    dstb = C.dqTb if qk == 0 else C.dkTb
                    tt0 = t if qk == 0 else tp
                    for h2 in range(2):
                        dv = dst.rearrange("g (hp h2) d n -> h2 d (g hp) n", h2=2)[h2, :, :, tt0:tt0 + 512]
                        K.dma('pool', dv, dblk[h2 * 64:(h2 + 1) * 64, :, :], dblkb, reads=[dblkb], dw=[dstb])
                for g in range(3):
                    vc0 = 416 + 3072 + g * 512
                    for j in range(4):
                        pv, pvb = nps()
                        for kc in range(8):
                            mm(K, pv[:, :], hT[:, kc, j * 128:(j + 1) * 128], w_in[:, kc, vc0:vc0 + 512], kc == 0,
                               kc == 7, [w_inb, hTb], [pvb], inc=(kc == 7))
                        cp(K, 'act' if j % 2 == 0 else 'dve', vblk[:, j, :, 0:64],
                           pv[:, :].rearrange("p (h d) -> p h d", d=64), [pvb], [vblkb])
                    K.dma('pool', C.dvA[tp:tp + 512, g].rearrange("(j p) h e -> p j h e", p=128), vblk[:], vblkb,
                          reads=[vblkb], dw=[C.dvAb])
        K.barrier()
        K.flush()


def phase2_mla(K, nc, C):
    scale = 96.0 ** -0.5
    with ExitStack() as es:
        sbf, psf = mk_alloc(nc, es)
        SMAX = max(S for (_, S, _) in C.seqs)
        kt, ktb = sbf("kt", [128, SMAX], BF16)
        K.op('pool', lambda e: e.memset(kt[64:128, :], 0.0), [], [ktb])
        va, vab = sbf("va", [128, SMAX // 128, 128], BF16)
        qs = [sbf("q%d" % i, [128, 512], BF16) for i in range(2)]
        for q_, qb__ in qs:
            K.op('pool', lambda e, q_=q_: e.memset(q_[64:128, :], 0.0), [], [qb__])
        pts = [sbf("pt%d" % i, [128, 1024], BF16) for i in range(3)]
        rc, rcb = sbf("rc", [128, 512], F32)
        ob = [sbf("ob%d" % i, [64, 512], BF16) for i in range(2)]
        PSs = [psf("pss%d" % i, (128, 1024)) for i in range(3)]
        PSo = [psf("pso%d" % i) for i in range(2)]
        qi = 0
        si = 0
        for (off, S, poff) in C.seqs:
            nkt = S // 128
            for h in range(8):
                K.dma('sp', kt[0:96, 0:S], C.kT[h, :, off:off + S], ktb, writes=[ktb], dr=[C.kTb])
                K.dma('sp', va[:, 0:nkt, :], C.vA[off:off + S, h, :].rearrange("(m p) e -> p m e", p=128), vab,
                      writes=[vab], dr=[C.vAb])
                for qb in range(S // 512):
                    t = off + qb * 512
                    q, qb_ = qs[qi % 2]
                    po, pob = PSo[qi % 2]
                    o_, ob_ = ob[qi % 2]
                    qi += 1
                    K.dma('sp', q[0:96, :], C.qT[h, :, t:t + 512], qb_, writes=[qb_], dr=[C.qTb])
                    prev = None
                    npair = nkt // 2
                    for mp in range(npair + 1):
                        cur = None
                        if mp < npair:
                            ps_, psb_ = PSs[si % 3]
                            pt, ptb = pts[si % 3]
                            si += 1
                            for hf in range(2):
                                m = 2 * mp + hf
                                mm(K, ps_[:, hf * 512:(hf + 1) * 512], kt[:, m * 128:(m + 1) * 128], q[:, :], True,
                                   True, [ktb, qb_], [psb_], inc=(hf == 1))
                            actf(K, pt[:, :], ps_[:, :], AF.Exp, [psb_], [ptb], scale=scale)
                            cur = (mp, pt, ptb)
                        if prev is not None:
                            pm, ppt, pptb = prev
                            for hf in range(2):
                                m = 2 * pm + hf
                                mm(K, po[:, :], va[:, m, :], ppt[:, hf * 512:(hf + 1) * 512], m == 0, m == nkt - 1,
                                   [vab, pptb], [pob], inc=(m == nkt - 1))
                        prev = cur
                    K.op('dve', lambda e, o=rc[64:128, :], i=po[64:128, :]: e.reciprocal(out=o, in_=i), [pob], [rcb])
                    tt(K, 'dve', o_[0:64, :], po[0:64, :], rc[64:128, :], ALU.mult, [pob, rcb], [ob_])
                    K.dma('pool', C.oT[h * 64:(h + 1) * 64, t:t + 512], o_[:], ob_, reads=[ob_], dw=[C.oTb])
        K.barrier()
        K.flush()


def phase2_dil(K, nc, C):
    scale = 0.125
    DIL = (1, 4, 16)
    with ExitStack() as es:
        sbf, psf = mk_alloc(nc, es)
        mk = {}
        for name in ('A128', 'B128', 'A128f', 'B128l', 'A1f', 'B1l', 'A32', 'A32u0', 'A32u32', 'A32l', 'B32', 'ALL'):
            mk[name] = sbf("m_" + name, [128, 512], BF16)
            K.dma('sp', mk[name][0][:], C.cst['m_' + name][:, :], mk[name][1], writes=[mk[name][1]])
        ident, identb = sbf("ident", [128, 128], BF16)
        K.dma('sp', ident[:], C.cst['ident'][:, :], identb, writes=[identb])
        zl, zlb = sbf("zl", [128, 128], BF16)
        K.op('pool', lambda e: e.memset(zl[:], 0.0), [], [zlb])
        zr, zrb = sbf("zr", [128, 512], BF16)
        K.op('pool', lambda e: e.memset(zr[:], 0.0), [], [zrb])
        spans = [512 + 128 * d for d in DIL]
        ntile = [5, 8, 32]
        nbuf = [2, 2, 1]
        ktl = [[sbf("kt%d_%d" % (g, i), [64, 4, spans[g]], BF16) for i in range(nbuf[g])] for g in range(3)]
        vtl = [[sbf("vt%d_%d" % (g, i), [128, ntile[g], 4, 128], BF16) for i in range(nbuf[g])] for g in range(3)]
        qtl = [sbf("dq%d" % i, [64, 4, 512], BF16) for i in range(6)]
        pts = [sbf("dpt%d" % i, [128, 512], BF16) for i in range(4)]
        sidx = [0]
        oidx = [0]
        acc = [sbf("acc%d" % i, [128, 512], F32) for i in range(8)]
        rc, rcb = sbf("drc", [128, 512], F32)
        obs = [sbf("dob%d" % i, [64, 512], BF16) for i in range(8)]
        PSs = [psf("dpss%d" % i) for i in range(4)]
        PSo = [psf("dpso%d" % i) for i in range(3)]
        ui = [0, 0, 0]
        qi = 0
        si = 0
        oi = 0
        ai = 0
        for (off, S, poff) in C.seqs:
            for qc in range(S // 512):
                t0 = qc * 512
                t = off + t0
                for hh in range(2):
                    accs = [acc[(ai % 2) * 4 + hl] + obs[(ai % 2) * 4 + hl] for hl in range(4)]
                    ai += 1
                    units = []
                    for g in range(3):
                        d = DIL[g]
                        L = S // d
                        k_, kb_ = ktl[g][ui[g] % nbuf[g]]
                        v_, vb_ = vtl[g][ui[g] % nbuf[g]]
                        ui[g] += 1
                        q_, qb_ = qtl[((ai - 1) % 2) * 3 + g]
                        ks = poff + t0 - 64 * d
                        K.dma('sp', k_[:], C.dkT[g, hh * 4:(hh + 1) * 4, :, ks:ks + spans[g]].rearrange(
                            "h d n -> d h n"), kb_, writes=[kb_], dr=[C.dkTb])
                        K.dma('sp', q_[:], C.dqT[g, hh * 4:(hh + 1) * 4, :, t:t + 512].rearrange("h d n -> d h n"),
                              qb_, writes=[qb_], dr=[C.dqTb])
                        if g == 0:
                            src = C.dvA[ks:ks + 640, g, hh * 4:(hh + 1) * 4, :].rearrange(
                                "(m p) h e -> p m h e", p=128)
                            K.dma('sp', v_[:], src, vb_, writes=[vb_], dr=[C.dvAb])
                        else:
                            for mi in range(2):
                                base = ks + mi * 128 * d
                                nk_ = 32 if (g == 2 and mi == 1) else 128
                                src = C.dvA[base:base + nk_ * d, g, hh * 4:(hh + 1) * 4, :].rearrange(
                                    "(p r) h e -> p r h e", r=d)
                                K.dma('sp', v_[0:nk_, mi * d:(mi + 1) * d, :, :], src, vb_, writes=[vb_],
                                      dr=[C.dvAb])
                        for hl in range(4):
                            units.append((g, d, L, hl, k_, kb_, v_, vb_, q_, qb_))
                    def stage1(u):
                        g, d, L, hl, k_, kb_, v_, vb_, q_, qb_ = u
                        nsub, nq = (4, 128) if g < 2 else (16, 32)
                        u0c = t0 // d
                        res = []
                        for ab in range(2):
                            ps_, psb_ = PSs[sidx[0] % 4]
                            pt, ptb = pts[sidx[0] % 4]
                            sidx[0] += 1
                            if g == 2:
                                if ab == 0:
                                    nm = 'A32u0' if u0c == 0 else ('A32u32' if u0c == 32 else (
                                        'A32l' if u0c == L - 32 else 'A32'))
                                else:
                                    nm = 'ALL' if u0c >= L - 64 else 'B32'
                            elif g == 1:
                                if ab == 0:
                                    nm = 'A128f' if u0c == 0 else 'A128'
                                else:
                                    nm = 'B128l' if u0c == L - 128 else 'B128'
                            else:
                                if ab == 0:
                                    nm = 'A1f' if t0 == 0 else 'A128'
                                else:
                                    nm = 'B1l' if t0 == S - 512 else 'B128'
                            mt, mtb = mk[nm]
                            nk_ = 32 if (g == 2 and ab == 1) else 128
                            mm(K, ps_[0:nk_, :], ident[0:nk_, 0:nk_], mt[0:nk_, :], True, False, [identb, mtb],
                               [psb_])
                            for s_ in range(nsub):
                                if g == 0:
                                    kk = k_[:, hl, (s_ + ab) * 128:(s_ + ab + 1) * 128]
                                    qq = q_[:, hl, s_ * 128:(s_ + 1) * 128]
                                else:
                                    kk = k_[:, hl, ab * 128 * d + s_:ab * 128 * d + s_ + (nk_ - 1) * d + 1:d]
                                    qq = q_[:, hl, s_:512:d]
                                mm(K, ps_[0:nk_, s_ * nq:(s_ + 1) * nq], kk, qq, False, s_ == nsub - 1,
                                   [kb_, qb_], [psb_], inc=(s_ == nsub - 1))
                            actf(K, pt[0:nk_, :], ps_[0:nk_, :], AF.Exp, [psb_], [ptb], scale=scale)
                            res.append((pt, ptb, nk_))
                        return res

                    def stage2(u, res):
                        g, d, L, hl, k_, kb_, v_, vb_, q_, qb_ = u
                        h = hh * 4 + hl
                        nsub, nq = (4, 128) if g < 2 else (16, 32)
                        po, pob = PSo[oidx[0] % 3]
                        oidx[0] += 1
                        mm(K, po[:, :], zl[:, :], zr[:, :], True, False, [zlb, zrb], [pob])
                        for ab in range(2):
                            pt, ptb, nk_ = res[ab]
                            for s_ in range(nsub):
                                if g == 0:
                                    vv = v_[:, s_ + ab, hl, :]
                                else:
                                    vv = v_[0:nk_, ab * d + s_, hl, :]
                                last = (ab == 1 and s_ == nsub - 1)
                                mm(K, po[:, s_ * nq:(s_ + 1) * nq], vv, pt[0:nk_, s_ * nq:(s_ + 1) * nq], False, last,
                                   [vb_, ptb], [pob], inc=last)
                        a_, ab_, o_, ob_ = accs[hl]
                        if g == 0:
                            cp(K, 'act', a_[:, :], po[:, :], [pob], [ab_])
                        else:
                            src = po[:, :].rearrange("p (r i) -> p i r", r=d)
                            tt(K, 'dve', a_[:, :].rearrange("p (i r) -> p i r", r=d),
                               a_[:, :].rearrange("p (i r) -> p i r", r=d), src, ALU.add, [pob, ab_], [ab_])
                        if g == 2:
                            K.op('dve', lambda e, o=rc[0:64, :], i=a_[64:128, :]: e.reciprocal(out=o, in_=i),
                                 [ab_], [rcb])
                            tt(K, 'dve', o_[0:64, :], a_[0:64, :], rc[0:64, :], ALU.mult, [ab_, rcb], [ob_])
                            K.dma('pool', C.oT[512 + h * 64:512 + (h + 1) * 64, t:t + 512], o_[:], ob_,
                                  reads=[ob_], dw=[C.oTb])

                    prev = None
                    for u in units:
                        r_ = stage1(u)
                        if prev is not None:
                            stage2(*prev)
                        prev = (u, r_)
                    stage2(*prev)
        K.barrier()
        K.flush()


def phase_proj(K, nc, C, inT, inTb, nk, wsrc, xin, xinb, xout, xoutb, tag, seqs=None):
    seqs = seqs or C.seqs
    with ExitStack() as es:
        sbf, psf = mk_alloc(nc, es)
        w, wb = sbf(tag + "w", [128, nk, D], BF16)
        with ExitStack() as es2:
            sbf2, _ = mk_alloc(nc, es2)
            load_w(K, sbf2, lambda kc, c0, c1: w[:, kc, c0:c1], wb, wsrc, nk, D, None, None, tag + "l")
            K.barrier()
            K.flush()
        its = [sbf(tag + "in%d" % i, [128, nk, 512], BF16) for i in range(2)]
        xs = [sbf(tag + "x%d" % i, [128, 4, D], F32) for i in range(2)]
        PS = [psf(tag + "ps%d" % i) for i in range(4)]
        bi = 0
        pi = 0
        for (off, S, poff) in seqs:
            for blk in range(S // 512):
                t = off + blk * 512
                it, itb = its[bi % 2]
                x_, xb_ = xs[bi % 2]
                bi += 1
                K.dma('sp', it[:], inT[:, t:t + 512].rearrange("(c p) n -> p c n", p=128), itb, writes=[itb],
                      dr=[inTb])
                K.dma('sp', x_[:], xin[t:t + 512, :].rearrange("(j p) d -> p j d", p=128), xb_, writes=[xb_],
                      dr=[xinb])
                for j in range(4):
                    for hf in range(2):
                        ps_, psb_ = PS[pi % 4]
                        pi += 1
                        for kc in range(nk):
                            mm(K, ps_[:, :], it[:, kc, j * 128:(j + 1) * 128], w[:, kc, hf * 512:(hf + 1) * 512],
                               kc == 0, kc == nk - 1, [itb, wb], [psb_], inc=(kc == nk - 1))
                        tt(K, 'dve', x_[:, j, hf * 512:(hf + 1) * 512], x_[:, j, hf * 512:(hf + 1) * 512], ps_[:, :],
                           ALU.add, [xb_, psb_], [xb_])
                K.dma('pool', xout[t:t + 512, :].rearrange("(j p) d -> p j d", p=128), x_[:], xb_, reads=[xb_],
                      dw=[xoutb])
        K.barrier()
        K.flush()


def phase_ffn(K, nc, C, layer, xin, xinb, xout, xoutb, final, tag, seqs=None):
    W = C.W
    seqs = seqs or C.seqs
    with ExitStack() as es:
        sbf, psf = mk_alloc(nc, es)
        ident, identb = sbf(tag + "ident", [128, 128], BF16)
        K.dma('sp', ident[:], C.cst['ident'][:, :], identb, writes=[identb])
        gf, gfb = sbf(tag + "gf", [128, 8], F32)
        K.dma('sp', gf[:], W['norm_ffn'][layer].rearrange("(c p) -> p c", p=128), gfb, writes=[gfb])
        wgu, wgub = sbf(tag + "wgu", [128, 8, 2 * D_FF], BF16)
        wdn, wdnb = sbf(tag + "wdn", [128, 22, D], BF16)
        with ExitStack() as es2:
            sbf2, _ = mk_alloc(nc, es2)
            load_w(K, sbf2, lambda kc, c0, c1: wgu[:, kc, c0:c1], wgub, W['w_gu'][layer], 8, 2 * D_FF, gf, gfb,
                   tag + "lg")
            load_w(K, sbf2, lambda kc, c0, c1: wdn[:, kc, c0:c1], wdnb, W['w_down'][layer], 22, D, None, None,
                   tag + "ld")
            K.barrier()
            K.flush()
        if final:
            gfin, gfinb = sbf(tag + "gfin", [128, D], F32)
            K.dma('sp', gfin[:], W['norm_final'].partition_broadcast(128), gfinb, writes=[gfinb])
        x_, xb_ = sbf(tag + "x", [128, 4, D], F32)
        xbj = [Buf(tag + "x%d" % j) for j in range(4)]
        h4, h4b = sbf(tag + "h4", [128, 4, D], BF16)
        hT, hTb = sbf(tag + "hT", [128, 8, 512], BF16)
        scr, scrb = sbf(tag + "scr", [128, D], BF16)
        tmp = {k: sbf(tag + "n_" + k, [128, 4], F32) for k in ('ss', 'ms', 'sd', 'rs')}
        aT, aTb = sbf(tag + "aT", [128, 22, 512], BF16)
        sg = [sbf(tag + "sg%d" % i, [128, 512], F32) for i in range(2)]
        psT = [psf(tag + "psT%d" % i, (128, 1024), BF16) for i in range(2)]
        PS = [psf(tag + "ps%d" % i) for i in range(6)]
        pi = 0
        gi = 0
        for (off, S, poff) in seqs:
            for blk in range(S // 512):
                t = off + blk * 512
                for j in range(4):
                    K.dma('sp', x_[:, j, :], xin[t + j * 128:t + (j + 1) * 128, :], xbj[j], writes=[xbj[j]],
                          dr=[xinb])
                rmsnorm_tok(K, x_, xbj, 4, D, h4, h4b, tmp, scr, scrb)
                transpose_blk(K, h4, h4b, 4, 8, hT, hTb, [p[0] for p in psT], [p[1] for p in psT], ident, identb)
                for c in range(22):
                    pg, pgb = PS[pi % 6]
                    pu, pub = PS[(pi + 1) % 6]
                    pi += 2
                    for kc in range(8):
                        mm(K, pg[:, :], wgu[:, kc, c * 128:(c + 1) * 128], hT[:, kc, :], kc == 0, kc == 7,
                           [wgub, hTb], [pgb], inc=(kc == 7))
                    for kc in range(8):
                        mm(K, pu[:, :], wgu[:, kc, D_FF + c * 128:D_FF + (c + 1) * 128], hT[:, kc, :], kc == 0,
                           kc == 7, [wgub, hTb], [pub], inc=(kc == 7))
                    s_, sb_ = sg[gi % 2]
                    gi += 1
                    actf(K, s_[:, :], pg[:, :], AF.Silu, [pgb], [sb_])
                    tt(K, 'dve', aT[:, c, :], s_[:, :], pu[:, :], ALU.mult, [sb_, pub], [aTb])
                for j in range(4):
                    for hf in range(2):
                        ps_, psb_ = PS[pi % 6]
                        pi += 1
                        for c in range(22):
                            mm(K, ps_[:, :], aT[:, c, j * 128:(j + 1) * 128], wdn[:, c, hf * 512:(hf + 1) * 512],
                               c == 0, c == 21, [aTb, wdnb], [psb_], inc=(c == 21))
                        tt(K, 'dve', x_[:, j, hf * 512:(hf + 1) * 512], x_[:, j, hf * 512:(hf + 1) * 512], ps_[:, :],
                           ALU.add, [xbj[j], psb_], [xbj[j]])
                    if not final:
                        K.dma('pool', xout[t + j * 128:t + (j + 1) * 128, :], x_[:, j, :], xbj[j], reads=[xbj[j]],
                              dw=[xoutb])
                if final:
                    ss, ssb = tmp['ss']
                    ms, msb = tmp['ms']
                    sd, sdb = tmp['sd']
                    rs, rsb = tmp['rs']
                    for j in range(4):
                        actf(K, scr[:, :], x_[:, j, :], AF.Square, [xbj[j]], [scrb, ssb], accum=ss[:, j:j + 1])
                    ts(K, 'dve', ms[:, 0:4], ss[:, 0:4], 1.0 / D, EPS, ALU.mult, ALU.add, [ssb], [msb])
                    actf(K, sd[:, 0:4], ms[:, 0:4], AF.Sqrt, [msb], [sdb])
                    K.op('dve', lambda e: e.reciprocal(out=rs[:, 0:4], in_=sd[:, 0:4]), [sdb], [rsb])
                    for j in range(4):
                        K.op('dve', lambda e, o=x_[:, j, :], r=rs[:, j:j + 1]: e.scalar_tensor_tensor(
                            out=o, in0=o, scalar=r, in1=gfin[:, :], op0=ALU.mult, op1=ALU.mult),
                            [xbj[j], rsb, gfinb], [xbj[j]])
                        K.dma('pool', xout[t + j * 128:t + (j + 1) * 128, :], x_[:, j, :], xbj[j], reads=[xbj[j]],
                              dw=[xoutb])
        K.barrier()
        K.flush()


def phase4(K, nc, C):
    W = C.W
    with ExitStack() as es:
        sbf, psf = mk_alloc(nc, es)
        ident, identb = sbf("p4ident", [128, 128], BF16)
        K.dma('sp', ident[:], C.cst['ident'][:, :], identb, writes=[identb])
        g1, g1b = sbf("p4g", [128, 8], F32)
        K.dma('sp', g1[:], W['norm_mix'][1].rearrange("(c p) -> p c", p=128), g1b, writes=[g1b])
        w, wb = sbf("p4w", [128, 8, 2 * D_RNN], BF16)
        with ExitStack() as es2:
            sbf2, _ = mk_alloc(nc, es2)
            load_w(K, sbf2, lambda kc, c0, c1: w[:, kc, c0:c1], wb, W['w_in_r'][0], 8, 2 * D_RNN, g1, g1b, "p4l")
            K.barrier()
            K.flush()
        zt, ztb = sbf("p4z", [128, 12, 2], F32)
        K.op('pool', lambda e: e.memset(zt[:], 0.0), [], [ztb])
        for si, (off, S, poff) in enumerate(C.seqs):
            b0 = off + 4 * si
            K.dma('pool', C.xr[:, b0:b0 + 2].rearrange("(c p) n -> p c n", p=128), zt[:], ztb, reads=[ztb],
                  dw=[C.xrb])
            K.dma('pool', C.xr[:, b0 + 2 + S:b0 + 4 + S].rearrange("(c p) n -> p c n", p=128), zt[:], ztb,
                  reads=[ztb], dw=[C.xrb])
        x_, xb_ = sbf("p4x", [128, 4, D], F32)
        h4, h4b = sbf("p4h4", [128, 4, D], BF16)
        hT, hTb = sbf("p4hT", [128, 8, 512], BF16)
        scr, scrb = sbf("p4scr", [128, D], BF16)
        tmp = {k: sbf("p4n_" + k, [128, 4], F32) for k in ('ss', 'ms', 'sd', 'rs')}
        yb, ybb = sbf("p4y", [128, 12, 512], BF16)
        xrb_, xrbb = sbf("p4xr", [128, 12, 512], F32)
        psT = [psf("p4psT%d" % i, (128, 1024), BF16) for i in range(2)]
        PS = [psf("p4ps%d" % i) for i in range(6)]
        pi = 0
        for si, (off, S, poff) in enumerate(C.seqs):
            for blk in range(S // 512):
                t = off + blk * 512
                tx = off + 4 * si + 2 + blk * 512
                K.dma('sp', x_[:], C.x1[t:t + 512, :].rearrange("(j p) d -> p j d", p=128), xb_, writes=[xb_],
                      dr=[C.x1b])
                rmsnorm_tok(K, x_, xb_, 4, D, h4, h4b, tmp, scr, scrb)
                transpose_blk(K, h4, h4b, 4, 8, hT, hTb, [p[0] for p in psT], [p[1] for p in psT], ident, identb)
                for c in range(24):
                    ps_, psb_ = PS[pi % 6]
                    pi += 1
                    for kc in range(8):
                        mm(K, ps_[:, :], w[:, kc, c * 128:(c + 1) * 128], hT[:, kc, :], kc == 0, kc == 7, [wb, hTb],
                           [psb_], inc=(kc == 7))
                    if c < 12:
                        actf(K, yb[:, c, :], ps_[:, :], AF.Gelu_apprx_tanh, [psb_], [ybb])
                    else:
                        cp(K, 'dve', xrb_[:, c - 12, :], ps_[:, :], [psb_], [xrbb])
                K.dma('pool', C.yg[:, t:t + 512].rearrange("(c p) n -> p c n", p=128), yb[:], ybb, reads=[ybb],
                      dw=[C.ygb])
                K.dma('pool', C.xr[:, tx:tx + 512].rearrange("(c p) n -> p c n", p=128), xrb_[:], xrbb,
                      reads=[xrbb], dw=[C.xrb])
        K.barrier()
        K.flush()


def phase5(K, nc, C):
    W = C.W
    TC = 2048
    with ExitStack() as es:
        sbf, psf = mk_alloc(nc, es)
        cw, cwb = sbf("cw", [128, 12, 4], F32)
        cb, cbb = sbf("cb", [128, 12], F32)
        gb, gbb = sbf("gb", [128, 12, 4], F32)
        lam, lamb = sbf("lam", [128, 12, 2], F32)
        cc, ccb = sbf("cc", [128, 12, 2], F32)
        cc2, cc2b = sbf("cc2", [128, 12, 2], F32)
        gw, gwb = sbf("gw", [128, 12, 4, 128], BF16)
        for j in range(4):
            K.dma('sp', cw[:, :, j], W['conv_w'][0][j].rearrange("(n p) -> p n", p=128), cwb, writes=[cwb])
        K.dma('sp', cb[:], W['conv_b'][0].rearrange("(n p) -> p n", p=128), cbb, writes=[cbb])
        for a in range(2):
            for k in range(2):
                K.dma('sp', gb[:, :, a * 2 + k], W['lru_b_gate'][0][a, k].rearrange("(n p) -> p n", p=128), gbb,
                      writes=[gbb])
            K.dma('sp', lam[:, :, a], W['lru_lambda'][0][a].rearrange("(n p) -> p n", p=128), lamb, writes=[lamb])
        with ExitStack() as es2:
            sbf2, _ = mk_alloc(nc, es2)
            gst, gstb = sbf2("gst", [128, 12, 4, 128], F32)
            for a in range(2):
                for k in range(2):
                    K.dma('sp', gst[:, :, a * 2 + k, :], W['lru_w_gate'][0][a, k].rearrange("n c d -> c n d"), gstb,
                          writes=[gstb])
            cp(K, 'dve', gw[:], gst[:], [gstb], [gwb])
            ex, exb = sbf2("sp_x", [128, 12, 2], F32)
            lnv, lnvb = sbf2("sp_ln", [128, 12, 2], F32)
            ser, serb = sbf2("sp_ser", [128, 12, 2], F32)
            msk, mskb = sbf2("sp_m", [128, 12, 2], F32)
            actf(K, ex[:], lam[:], AF.Exp, [lamb], [exb], scale=-1.0)
            actf(K, lnv[:], ex[:], AF.Ln, [exb], [lnvb], bias=1.0)
            ts(K, 'dve', ser[:], ex[:], -0.25, 1.0 / 3.0, ALU.mult, ALU.add, [exb], [serb])
            tt(K, 'dve', ser[:], ser[:], ex[:], ALU.mult, [serb, exb], [serb])
            ts(K, 'dve', ser[:], ser[:], -1.0, 0.5, ALU.mult, ALU.add, [serb], [serb])
            tt(K, 'dve', ser[:], ser[:], ex[:], ALU.mult, [serb, exb], [serb])
            ts(K, 'dve', ser[:], ser[:], -1.0, 1.0, ALU.mult, ALU.add, [serb], [serb])
            tt(K, 'dve', ser[:], ser[:], ex[:], ALU.mult, [serb, exb], [serb])
            K.op('dve', lambda e: e.tensor_single_scalar(out=msk[:], in_=ex[:], scalar=0.05, op=ALU.is_lt),
                 [exb], [mskb])
            tt(K, 'dve', ser[:], ser[:], lnv[:], ALU.subtract, [serb, lnvb], [serb])
            tt(K, 'dve', ser[:], ser[:], msk[:], ALU.mult, [serb, mskb], [serb])
            tt(K, 'dve', cc[:], ser[:], lnv[:], ALU.add, [serb, lnvb], [ccb])
            ts(K, 'dve', cc[:], cc[:], -8.0, None, ALU.mult, None, [ccb], [ccb])
            ts(K, 'dve', cc2[:], cc[:], 2.0, None, ALU.mult, None, [ccb], [cc2b])
            K.barrier()
            K.flush()
        sets = []
        for i in range(2):
            st_ = {}
            for nm, shp, dt in (("xrt", [128, TC + 4], F32), ("xc", [128, TC], F32), ("xcbf", [128, TC], BF16),
                                ("rt", [128, TC], F32), ("it", [128, TC], F32), ("at", [128, TC], F32),
                                ("ml", [128, TC], F32), ("ut", [128, TC], F32), ("hs", [128, TC], F32),
                                ("hfl", [128, TC], F32), ("yl", [128, TC], BF16), ("ot", [128, TC], BF16)):
                st_[nm] = sbf("%s_%d" % (nm, i), shp, dt)
            sets.append(st_)
        uidx = 0
        car, carb = sbf("car", [128, 1], F32)
        PSG = [psf("p5pg%d" % i, (128, 2048)) for i in range(2)]
        pi = 0
        units = []
        for si, (off, S, poff) in enumerate(C.seqs):
            ntc = S // TC
            xb0 = off + 4 * si + 2
            for n in range(12):
                for a in range(2):
                    order = range(ntc) if a == 0 else range(ntc - 1, -1, -1)
                    for oi, tc in enumerate(order):
                        units.append((off, ntc, xb0, n, a, oi, tc))

        def tiles(k):
            st_ = sets[k % 2]
            return [st_[nm] for nm in ("xrt", "xc", "xcbf", "rt", "it", "at", "ml", "ut", "hs", "hfl", "yl", "ot")]

        def stage_a(u, k):
            off, ntc, xb0, n, a, oi, tc = u
            (xrt, xrtb), (xc, xcb), (xcbf, xcbfb), (rt, rtb), (it, itb), (at, atb), (ml, mlb), (ut, utb) = tiles(k)[0:8]
            tx = xb0 + tc * TC
            K.dma('sp', xrt[:, 0:TC + 3], C.xr[n * 128:(n + 1) * 128, tx - 2:tx + TC + 1], xrtb,
                  writes=[xrtb], dr=[C.xrb])
            ts(K, 'dve', xc[:, :], xrt[:, 0:TC], cw[:, n, 0:1], cb[:, n:n + 1], ALU.mult, ALU.add,
               [xrtb, cwb, cbb], [xcb])
            for j in range(1, 4):
                K.op('dve', lambda e, j=j, n=n, xc=xc, xrt=xrt: e.scalar_tensor_tensor(
                    out=xc[:, :], in0=xrt[:, j:j + TC], scalar=cw[:, n, j:j + 1], in1=xc[:, :],
                    op0=ALU.mult, op1=ALU.add), [xrtb, cwb, xcb], [xcb])
            cp(K, 'act', xcbf[:, :], xc[:, :], [xcb], [xcbfb])
            for kk_, dst, dstb in ((0, rt, rtb), (1, it, itb)):
                ps_, psb_ = PSG[kk_]
                for q in range(4):
                    mm(K, ps_[:, q * 512:(q + 1) * 512], gw[:, n, a * 2 + kk_, :],
                       xcbf[:, q * 512:(q + 1) * 512], True, True, [gwb, xcbfb], [psb_], inc=(q == 3))
                actf(K, dst[:, :], ps_[:, :], AF.Sigmoid, [psb_, gbb], [dstb],
                     bias=gb[:, n, a * 2 + kk_:a * 2 + kk_ + 1])
            actf(K, at[:, :], rt[:, :], AF.Exp, [rtb, ccb], [atb], scale=cc[:, n, a:a + 1])
            actf(K, ml[:, :], rt[:, :], AF.Exp, [rtb, cc2b], [mlb], scale=cc2[:, n, a:a + 1])

        def stage_a2(u, k):
            (xrt, xrtb), (xc, xcb), (xcbf, xcbfb), (rt, rtb), (it, itb), (at, atb), (ml, mlb), (ut, utb) = tiles(k)[0:8]
            actf(K, ml[:, :], ml[:, :], AF.Sqrt, [mlb], [mlb], scale=-1.0, bias=1.0)
            tt(K, 'dve', ut[:, :], it[:, :], xc[:, :], ALU.mult, [itb, xcb], [utb])
            tt(K, 'dve', ut[:, :], ut[:, :], ml[:, :], ALU.mult, [utb, mlb], [utb])

        def stage_b(u, k):
            off, ntc, xb0, n, a, oi, tc = u
            tl = tiles(k)
            (at, atb), (ml, mlb), (ut, utb), (hs, hsb), (hfl, hflb), (yl, ylb), (ot, otb) = tl[5:12]
            t = off + tc * TC
            if a == 1:
                K.dma('sp', hfl[:], C.hf[n * 128:(n + 1) * 128, t:t + TC], hflb, writes=[hflb], dr=[C.hfb])
                K.dma('sp', yl[:], C.yg[n * 128:(n + 1) * 128, t:t + TC], ylb, writes=[ylb], dr=[C.ygb])
            init = 0.0 if oi == 0 else car[:, 0:1]
            rdc = [atb, utb] + ([] if oi == 0 else [carb])
            if a == 0:
                K.op('dve', lambda e, init=init, hs=hs, at=at, ut=ut: e.tensor_tensor_scan(
                    hs[:, :], at[:, :], ut[:, :], init, ALU.mult, ALU.add), rdc, [hsb])
                if ntc > 1:
                    cp(K, 'act', car[:, 0:1], hs[:, TC - 1:TC], [hsb], [carb])
                K.dma('pool', C.hf[n * 128:(n + 1) * 128, t:t + TC], hs[:, :], hsb, reads=[hsb], dw=[C.hfb])
            else:
                K.op('dve', lambda e, init=init, hs=hs, at=at, ut=ut: e.tensor_tensor_scan(
                    hs[:, ::-1], at[:, ::-1], ut[:, ::-1], init, ALU.mult, ALU.add), rdc, [hsb])
                if ntc > 1:
                    cp(K, 'act', car[:, 0:1], hs[:, 0:1], [hsb], [carb])
                tt(K, 'dve', hs[:, :], hs[:, :], hfl[:, :], ALU.add, [hsb, hflb], [hsb])
                tt(K, 'dve', ot[:, :], hs[:, :], yl[:, :], ALU.mult, [hsb, ylb], [otb])
                K.dma('pool', C.hT[n * 128:(n + 1) * 128, t:t + TC], ot[:, :], otb, reads=[otb], dw=[C.hTb])

        for i, u in enumerate(units):
            stage_a(u, i)
            if i >= 1:
                stage_b(units[i - 1], i - 1)
            stage_a2(u, i)
        stage_b(units[-1], len(units) - 1)
        K.barrier()
        K.flush()


def phase_select(K, nc, C):
    (off0, S0, _) = C.seqs[0]
    nch = S0 // 2048
    with ExitStack() as es:
        sbf, psf = mk_alloc(nc, es)
        sel, selb = sbf("sel", [128, nch], F32)
        K.dma('sp', sel[:], C.sel[:, :], selb, writes=[selb])
        hts = [sbf("sl_h%d" % i, [128, 2048], BF16) for i in range(3)]
        hacc = [sbf("sl_ha%d" % i, [128, 2048], F32) for i in range(2)]
        hob = [sbf("sl_ho%d" % i, [128, 2048], BF16) for i in range(2)]
        li = 0
        for n in range(12):
            a_, ab_ = hacc[n % 2]
            o_, ob_ = hob[n % 2]
            for c in range(nch):
                t_, tb_ = hts[li % 3]
                li += 1
                K.dma('sp', t_[:], C.hT[n * 128:(n + 1) * 128, off0 + c * 2048:off0 + (c + 1) * 2048], tb_,
                      writes=[tb_], dr=[C.hTb])
                if c == 0:
                    ts(K, 'dve', a_[:, :], t_[:, :], sel[:, 0:1], None, ALU.mult, None, [tb_, selb], [ab_])
                else:
                    K.op('dve', lambda e, a_=a_, t_=t_, c=c: e.scalar_tensor_tensor(
                        out=a_[:, :], in0=t_[:, :], scalar=sel[:, c:c + 1], in1=a_[:, :], op0=ALU.mult,
                        op1=ALU.add), [tb_, selb, ab_], [ab_])
            cp(K, 'act', o_[:, :], a_[:, :], [ab_], [ob_])
            K.dma('pool', C.hT2[n * 128:(n + 1) * 128, 0:2048], o_[:, :], ob_, reads=[ob_], dw=[C.hT2b])
        xts = [sbf("sl_x%d" % i, [128, D], F32) for i in range(3)]
        xacc = [sbf("sl_xa%d" % i, [128, D], F32) for i in range(2)]
        for j in range(16):
            a_, ab_ = xacc[j % 2]
            for c in range(nch):
                t_, tb_ = xts[li % 3]
                li += 1
                r0 = off0 + c * 2048 + j * 128
                K.dma('sp', t_[:], C.x1[r0:r0 + 128, :], tb_, writes=[tb_], dr=[C.x1b])
                eng = 'dve' if (li % 2 == 0) else 'pool'
                if c == 0:
                    ts(K, 'dve', a_[:, :], t_[:, :], sel[:, 0:1], None, ALU.mult, None, [tb_, selb], [ab_])
                else:
                    K.op('dve', lambda e, a_=a_, t_=t_, c=c: e.scalar_tensor_tensor(
                        out=a_[:, :], in0=t_[:, :], scalar=sel[:, c:c + 1], in1=a_[:, :], op0=ALU.mult,
                        op1=ALU.add), [tb_, selb, ab_], [ab_])
            K.dma('pool', C.x1c[j * 128:(j + 1) * 128, :], a_[:, :], ab_, reads=[ab_], dw=[C.x1cb])
        cb_ = Buf("selcopy")
        o2 = 2048
        for (off, S, poff) in C.seqs[1:]:
            for n in range(12):
                K.dma('sp', C.hT2[n * 128:(n + 1) * 128, o2:o2 + S], C.hT[n * 128:(n + 1) * 128, off:off + S], cb_,
                      dr=[C.hTb], dw=[C.hT2b])
            for r0 in range(0, S, 512):
                K.dma('sp', C.x1c[o2 + r0:o2 + r0 + 512, :], C.x1[off + r0:off + r0 + 512, :], cb_, dr=[C.x1b],
                      dw=[C.x1cb])
            o2 += S
        K.barrier()
        K.flush()


def _bf(a):
    return np.asarray(a, np.float32).astype(ml_dtypes.bfloat16)


def host_consts(smax):
    c = {}
    c['ident'] = _bf(np.eye(128))

    def rot(n, blocks):
        r = np.zeros((n, n), np.float32)
        for (b0, half) in blocks:
            for m in range(half):
                r[b0 + m + half, b0 + m] = -1.0
                r[b0 + m, b0 + m + half] = 1.0
        return r
    c['r96'] = _bf(rot(128, [(64, 16)]))
    c['r128'] = _bf(rot(128, [(0, 8), (64, 8)]))
    pos = np.arange(smax, dtype=np.float32)

    def tables(half):
        inv = np.power(np.float32(500000.0), -(np.arange(half, dtype=np.float32) / np.float32(half))).astype(np.float32)
        ang = (pos[:, None] * inv[None, :]).astype(np.float32).astype(np.float64)
        return np.cos(ang).T.astype(np.float32), np.sin(ang).T.astype(np.float32)
    co, si = tables(16)
    c96 = np.ones((128, smax), np.float32)
    s96 = np.zeros((128, smax), np.float32)
    c96[64:80] = co
    c96[80:96] = co
    s96[64:80] = si
    s96[80:96] = si
    c['c96'], c['s96'] = c96, s96
    co, si = tables(8)
    c128 = np.ones((128, smax), np.float32)
    s128 = np.zeros((128, smax), np.float32)
    for b0 in (0, 64):
        c128[b0:b0 + 8] = co
        c128[b0 + 8:b0 + 16] = co
        s128[b0:b0 + 8] = si
        s128[b0 + 8:b0 + 16] = si
    c['c128'], c['s128'] = c128, s128
    j = np.arange(128)[:, None]
    i = np.arange(128)[None, :]

    def tab(allowed, nq):
        m = np.where(allowed[:, :nq], 0.0, NEG).astype(np.float32)
        return _bf(np.tile(m, (1, 512 // nq)))
    A = j >= i
    B = j <= i
    c['m_A128'] = tab(A, 128)
    c['m_B128'] = tab(B, 128)
    c['m_A128f'] = tab(A & (j >= 64), 128)
    c['m_B128l'] = tab(B & (j < 64), 128)
    a1f = np.array(c['m_A128'])
    a1f[:, 0:128] = np.array(c['m_A128f'])[:, 0:128]
    c['m_A1f'] = a1f
    b1l = np.array(c['m_B128'])
    b1l[:, 384:512] = np.array(c['m_B128l'])[:, 0:128]
    c['m_B1l'] = b1l
    c['m_A32'] = tab(A, 32)
    c['m_A32u0'] = tab(A & (j >= 64), 32)
    c['m_A32u32'] = tab(A & (j >= 32), 32)
    c['m_A32l'] = tab(A & (j < 96), 32)
    c['m_B32'] = tab(B, 32)
    c['m_ALL'] = _bf(np.full((128, 512), NEG, np.float32))
    return c


WSHAPES = {
    "norm_mix": (2, 1024), "w_in_a": (1, 1024, 5024), "q_norm": (1, 256), "w_uq": (1, 256, 768),
    "kv_norm": (1, 128), "w_ukv": (1, 128, 1024), "w_out_a": (1, 1024, 1024), "w_in_r": (1, 1024, 3072),
    "conv_w": (1, 4, 1536), "conv_b": (1, 1536), "lru_w_gate": (1, 2, 2, 12, 128, 128),
    "lru_b_gate": (1, 2, 2, 1536), "lru_lambda": (1, 2, 1536), "w_out_r": (1, 1536, 1024),
    "norm_ffn": (2, 1024), "w_gu": (2, 1024, 5632), "w_down": (2, 2816, 1024), "norm_final": (1024,),
}


def build(seq_lens, stop_after=99, debug=False, dbg=''):
    nc = bass.Bass("TRN2", target_bir_lowering=False)
    C = Ctx()
    C.dbg = dbg
    seqs = []
    off = 0
    poff = PAD
    for S in seq_lens:
        seqs.append((off, S, poff))
        off += S
        poff += S + 2 * PAD
    T = off
    TP = poff - PAD
    C.seqs = seqs
    smax = max(seq_lens)
    C.x = nc.dram_tensor("x", [T, D], F32, kind="ExternalInput").ap()
    C.W = {k: nc.dram_tensor(k, list(v), F32, kind="ExternalInput").ap() for k, v in WSHAPES.items()}
    hc = host_consts(smax)
    C.cst = {}
    for k, v in hc.items():
        dt = BF16 if v.dtype == ml_dtypes.bfloat16 else F32
        C.cst[k] = nc.dram_tensor("c_" + k, list(v.shape), dt, kind="ExternalInput").ap()
    compact = stop_after >= 5 and len(seq_lens) >= 1 and seq_lens[0] % 2048 == 0
    C.compact = compact
    nch = seq_lens[0] // 2048
    seqs2 = [(0, 2048, 0)]
    o2 = 2048
    for S in seq_lens[1:]:
        seqs2.append((o2, S, 0))
        o2 += S
    T2 = o2
    C.seqs2 = seqs2
    C.y = nc.dram_tensor("y", [T2 if compact else T, D], F32, kind="ExternalOutput").ap()
    C.yb = Buf("y")
    C.sel = nc.dram_tensor("sel", [128, nch], F32, kind="ExternalInput").ap()

    def scratch(name, shape, dt):
        setattr(C, name, nc.dram_tensor("s_" + name, shape, dt).ap())
        setattr(C, name + "b", Buf(name))
    C.xb = Buf("x")
    scratch("qT", [8, 96, T], BF16)
    scratch("kT", [8, 96, T], BF16)
    scratch("vA", [T, 8, 128], BF16)
    scratch("dqT", [3, 8, 64, T], BF16)
    scratch("dkT", [3, 8, 64, TP], BF16)
    scratch("dvA", [TP, 3, 8, 128], BF16)
    scratch("oT", [1024, T], BF16)
    scratch("xm", [T, D], F32)
    scratch("x1", [T, D], F32)
    scratch("yg", [D_RNN, T], BF16)
    scratch("xr", [D_RNN, T + 4 * len(seq_lens)], F32)
    scratch("hf", [D_RNN, T], F32)
    scratch("hT", [D_RNN, T], BF16)
    scratch("xm2", [T, D], F32)
    scratch("hT2", [D_RNN, T2], BF16)
    scratch("x1c", [T2, D], F32)
    with ExitStack() as es:
        K = KB(nc, es)
        phase1(K, nc, C)
        if stop_after >= 2:
            phase2_mla(K, nc, C)
            phase2_dil(K, nc, C)
        if stop_after >= 3:
            phase_proj(K, nc, C, C.oT, C.oTb, 8, C.W['w_out_a'][0], C.x, C.xb, C.xm, C.xmb, "pa")
            phase_ffn(K, nc, C, 0, C.xm, C.xmb, C.x1 if stop_after > 3 else C.y, C.x1b if stop_after > 3 else C.yb,
                      False, "f0")
        if stop_after >= 4:
            phase4(K, nc, C)
            phase5(K, nc, C)
        if stop_after >= 5:
            phase_select(K, nc, C)
            phase_proj(K, nc, C, C.hT2, C.hT2b, 12, C.W['w_out_r'][0], C.x1c, C.x1cb, C.xm2, C.xm2b, "pb",
                       seqs=C.seqs2)
            phase_ffn(K, nc, C, 1, C.xm2, C.xm2b, C.y, C.yb, True, "f1", seqs=C.seqs2)
        K.barrier()
        K.real_flush()
        print("instructions recorded:", K.ninst, "dma sems:", len(K.all_dsem))
        if K.TRACE:
            for t in K.trace:
                print("TR", t)
    return nc, hc


SEQ_LENS = [16384, 2048, 2048]
_CACHE = {}


def kernel(**inputs):
    xp = np.asarray(inputs["x_prompt"], np.float32).reshape(16384, D)
    xs = np.asarray(inputs["x_sample"], np.float32).reshape(16, 2048, D)
    if "nc" not in _CACHE:
        _CACHE["nc"] = build(SEQ_LENS)
    nc, hc = _CACHE["nc"]
    in_maps = []
    for c in range(8):
        m = {"x": np.ascontiguousarray(np.concatenate([xp, xs[2 * c], xs[2 * c + 1]], axis=0))}
        for k in WSHAPES:
            m[k] = np.ascontiguousarray(np.asarray(inputs[k], np.float32))
        for k, v in hc.items():
            m["c_" + k] = v
        sel = np.zeros((128, 8), np.float32)
        sel[:, c] = 1.0
        m["sel"] = sel
        in_maps.append(m)
    res = run_bass_kernel_spmd(nc, in_maps, core_ids=list(range(8)))
    y_prompt = np.concatenate([np.asarray(res.results[c]["y"][0:2048], np.float32) for c in range(8)],
                              axis=0).reshape(1, 16384, D)
    y_sample = np.stack([np.asarray(res.results[c // 2]["y"][2048 * (1 + c % 2):2048 * (2 + c % 2)], np.float32)
                         for c in range(16)], axis=0)
    return (y_prompt, y_sample)
```
